# Optimizing a Trainium2 kernel written in Bass

```python
import math
import jax, jax.numpy as jnp
from jax import lax
import numpy as np

D_MODEL = 1024
BATCH = 2
SEQ = 16384
DEPTH = 4

HEAD_DIM = 64
BLK = 128
GRID_W = 64
WIN_HEADS = 4
WIN_KV_HEADS = 2
WINDOW = 128
MLA_HEADS = 4
MLA_Q_RANK = 256
MLA_KV_RANK = 128
MLA_NOPE = 64
MLA_ROPE = 32
MLA_V = 64
AX_HEADS = 4
AX_KV_HEADS = 2
DIFF_HEADS = 4
DIFF_QK = 32
DIFF_V = 64
A_COLS = (WIN_HEADS + 2 * WIN_KV_HEADS) * HEAD_DIM
B_COLS = MLA_Q_RANK + MLA_KV_RANK + MLA_ROPE
C_COLS = (AX_HEADS + 2 * AX_KV_HEADS) * HEAD_DIM
D_COLS = DIFF_HEADS * (4 * DIFF_QK + DIFF_V)
N_IN = A_COLS + B_COLS + C_COLS + D_COLS
D_MIX = WIN_HEADS * HEAD_DIM + MLA_HEADS * MLA_V + AX_HEADS * HEAD_DIM + DIFF_HEADS * DIFF_V
D_FF = 2816
ROPE_BASE = 10000.0
NORM_EPS = 1e-5
NEG_INF = -1e30

kernel_name = "hybrid_parallel_heads_deepnorm_encoder"


def _layer_norm(x, g, b):
    xf = x.astype(jnp.float32)
    mu = jnp.mean(xf, -1, keepdims=True)
    xc = xf - mu
    var = jnp.mean(xc * xc, -1, keepdims=True)
    y = xc * lax.rsqrt(var + NORM_EPS) * g.astype(jnp.float32) + b.astype(jnp.float32)
    return y.astype(x.dtype)


def _rms_norm(x, g):
    xf = x.astype(jnp.float32)
    y = xf * lax.rsqrt(jnp.mean(xf * xf, -1, keepdims=True) + NORM_EPS)
    return (y * g.astype(jnp.float32)).astype(x.dtype)


def _swiglu(h, w_gu, w_down):
    gate, up = jnp.split(h @ w_gu, 2, axis=-1)
    return (jax.nn.silu(gate) * up) @ w_down


def _alibi_slopes(n):
    return 2.0 ** (-8.0 * (jnp.arange(n, dtype=jnp.float32) + 1.0) / n)


def _rope_angles(pos, dim):
    inv = ROPE_BASE ** (-jnp.arange(0, dim, 2, dtype=jnp.float32) / dim)
    ang = pos.astype(jnp.float32)[:, None] * inv[None, :]
    return jnp.cos(ang), jnp.sin(ang)


def _apply_rope(x, cos, sin):
    c = cos[:, None, :]
    s = sin[:, None, :]
    x1, x2 = jnp.split(x.astype(jnp.float32), 2, axis=-1)
    return jnp.concatenate([x1 * c - x2 * s, x1 * s + x2 * c], -1).astype(x.dtype)


def _dense_attention(q, k, v, scale):
    B, S, Hq, Dk = q.shape
    Hkv = k.shape[2]
    G = Hq // Hkv
    nb = S // BLK
    qb = q.reshape(B, nb, BLK, Hkv, G, Dk).transpose(1, 0, 2, 3, 4, 5)

    def block(qi):
        s = jnp.einsum('bqhgd,bkhd->bhgqk', qi, k).astype(jnp.float32) * scale
        p = jax.nn.softmax(s, axis=-1)
        return jnp.einsum('bhgqk,bkhd->bqhgd', p.astype(v.dtype), v)

    o = lax.map(block, qb)
    return o.transpose(1, 0, 2, 3, 4, 5).reshape(B, S, Hq * v.shape[-1])


def _mixer_window(a, sink, slopes):
    B, S, _ = a.shape
    nb = S // BLK
    G = WIN_HEADS // WIN_KV_HEADS
    q, k, v = jnp.split(a, [WIN_HEADS * HEAD_DIM, (WIN_HEADS + WIN_KV_HEADS) * HEAD_DIM], axis=-1)
    qb = q.reshape(B, nb, BLK, WIN_KV_HEADS, G, HEAD_DIM)

    def band(t):
        tp = jnp.pad(t.reshape(B, S, WIN_KV_HEADS, HEAD_DIM), ((0, 0), (BLK, BLK), (0, 0), (0, 0)))
        tp = tp.reshape(B, nb + 2, BLK, WIN_KV_HEADS, HEAD_DIM)
        return jnp.concatenate([tp[:, :-2], tp[:, 1:-1], tp[:, 2:]], axis=2)

    kw = band(k)
    vw = band(v)
    s = jnp.einsum('bnqhgd,bnkhd->bnhgqk', qb, kw).astype(jnp.float32) * (HEAD_DIM ** -0.5)
    dist = jnp.abs(jnp.arange(BLK)[:, None] - jnp.arange(3 * BLK)[None, :] + BLK)
    kpos = jnp.arange(nb)[:, None] * BLK - BLK + jnp.arange(3 * BLK)[None, :]
    allowed = (dist <= WINDOW)[None] & ((kpos >= 0) & (kpos < S))[:, None, :]
    bias = -slopes.reshape(WIN_KV_HEADS, G, 1, 1) * dist.astype(jnp.float32)
    s = jnp.where(allowed[None, :, None, None], s + bias, NEG_INF)
    sink_logit = jnp.broadcast_to(sink.astype(jnp.float32).reshape(1, 1, WIN_KV_HEADS, G, 1, 1),
                                  s.shape[:-1] + (1,))
    p = jax.nn.softmax(jnp.concatenate([s, sink_logit], axis=-1), axis=-1)[..., :-1]
    o = jnp.einsum('bnhgqk,bnkhd->bnqhgd', p.astype(vw.dtype), vw)
    return o.reshape(B, S, WIN_HEADS * HEAD_DIM)


def _mixer_mla(b, q_norm_g, w_uq, kv_norm_g, w_ukv, pos):
    B, S, _ = b.shape
    c_q, c_kv, k_rope = jnp.split(b, [MLA_Q_RANK, MLA_Q_RANK + MLA_KV_RANK], axis=-1)
    q = (_rms_norm(c_q, q_norm_g) @ w_uq).reshape(B, S, MLA_HEADS, MLA_NOPE + MLA_ROPE)
    kv = (_rms_norm(c_kv, kv_norm_g) @ w_ukv).reshape(B, S, MLA_HEADS, MLA_NOPE + MLA_V)
    cos, sin = _rope_angles(pos, MLA_ROPE)
    q = jnp.concatenate([q[..., :MLA_NOPE], _apply_rope(q[..., MLA_NOPE:], cos, sin)], axis=-1)
    k_r = _apply_rope(k_rope[:, :, None, :], cos, sin)
    k = jnp.concatenate([kv[..., :MLA_NOPE], jnp.broadcast_to(k_r, (B, S, MLA_HEADS, MLA_ROPE))], axis=-1)
    v = kv[..., MLA_NOPE:]
    return _dense_attention(q, k, v, (MLA_NOPE + MLA_ROPE) ** -0.5)


def _mixer_axial(c, q_g, k_g):
    B, S, _ = c.shape
    q, k, v = jnp.split(c, [AX_HEADS * HEAD_DIM, (AX_HEADS + AX_KV_HEADS) * HEAD_DIM], axis=-1)
    q = _rms_norm(q.reshape(B, S, AX_HEADS, HEAD_DIM), q_g)
    k = _rms_norm(k.reshape(B, S, AX_KV_HEADS, HEAD_DIM), k_g)
    v = v.reshape(B, S, AX_KV_HEADS, HEAD_DIM)
    rows = S // GRID_W
    row = jnp.repeat(jnp.arange(rows, dtype=jnp.int32), GRID_W)
    col = jnp.tile(jnp.arange(GRID_W, dtype=jnp.int32), rows)
    half = HEAD_DIM // 2
    cr, sr = _rope_angles(row, half)
    cc, sc = _rope_angles(col, half)

    def axial(t):
        return jnp.concatenate([_apply_rope(t[..., :half], cr, sr), _apply_rope(t[..., half:], cc, sc)], axis=-1)

    return _dense_attention(axial(q), axial(k), v, HEAD_DIM ** -0.5)


def _mixer_diff(d, lam_params, subln_g, slopes, layer_idx):
    B, S, _ = d.shape
    H = DIFF_HEADS
    q, k, v = jnp.split(d, [H * 2 * DIFF_QK, H * 4 * DIFF_QK], axis=-1)
    q = q.reshape(B, S, H, 2, DIFF_QK)
    k = k.reshape(B, S, H, 2, DIFF_QK)
    v = v.reshape(B, S, H, DIFF_V)
    lam_init = 0.8 - 0.6 * math.exp(-0.3 * layer_idx)
    lp = lam_params.astype(jnp.float32)
    lam = jnp.exp(jnp.sum(lp[0] * lp[1])) - jnp.exp(jnp.sum(lp[2] * lp[3])) + lam_init
    nb = S // BLK
    qb = q.reshape(B, nb, BLK, H, 2, DIFF_QK).transpose(1, 0, 2, 3, 4, 5)
    kpos = jnp.arange(S)
    scale = DIFF_QK ** -0.5

    def block(args):
        qi, i = args
        tq = i * BLK + jnp.arange(BLK)
        dist = jnp.abs(tq[:, None] - kpos[None, :]).astype(jnp.float32)
        bias = -slopes[:, None, None] * dist
        s = jnp.einsum('bqhmd,bkhmd->bhmqk', qi, k).astype(jnp.float32) * scale + bias[None, :, None]
        p = jax.nn.softmax(s, axis=-1)
        w = p[:, :, 0] - lam * p[:, :, 1]
        return jnp.einsum('bhqk,bkhd->bqhd', w.astype(v.dtype), v)

    o = lax.map(block, (qb, jnp.arange(nb)))
    o = o.transpose(1, 0, 2, 3, 4).reshape(B, S, H, DIFF_V)
    o = _rms_norm(o, subln_g) * (1.0 - lam_init)
    return o.reshape(B, S, H * DIFF_V)


def _token_mixers(h, w_in, sink, mla_q_norm, mla_w_uq, mla_kv_norm, mla_w_ukv,
                  ax_q_norm, ax_k_norm, diff_lambda, diff_subln, w_out, layer_idx):
    S = h.shape[1]
    proj = h @ w_in
    a, b, c, d = jnp.split(proj, [A_COLS, A_COLS + B_COLS, A_COLS + B_COLS + C_COLS], axis=-1)
    pos = jnp.arange(S, dtype=jnp.int32)
    slopes = _alibi_slopes(WIN_HEADS + DIFF_HEADS)
    o_a = _mixer_window(a, sink, slopes[:WIN_HEADS])
    o_b = _mixer_mla(b, mla_q_norm, mla_w_uq, mla_kv_norm, mla_w_ukv, pos)
    o_c = _mixer_axial(c, ax_q_norm, ax_k_norm)
    o_d = _mixer_diff(d, diff_lambda, diff_subln, slopes[WIN_HEADS:], layer_idx)
    return jnp.concatenate([o_a, o_b, o_c, o_d], axis=-1) @ w_out


def setup_inputs(seed: int = 0) -> dict:
    key = jax.random.key(seed)
    ks = jax.random.split(key, 20)
    f32 = jnp.float32
    beta = (8 * DEPTH) ** -0.25

    def nrm(k, shape, scale):
        return jax.random.normal(k, shape, f32) * scale

    return {
        "x": nrm(ks[0], (BATCH, SEQ, D_MODEL), 1.0),
        "w_in": nrm(ks[1], (DEPTH, D_MODEL, N_IN), D_MODEL ** -0.5),
        "win_sink": nrm(ks[2], (DEPTH, WIN_HEADS), 0.5),
        "mla_q_norm": 1.0 + nrm(ks[3], (DEPTH, MLA_Q_RANK), 0.02),
        "mla_w_uq": nrm(ks[4], (DEPTH, MLA_Q_RANK, MLA_HEADS * (MLA_NOPE + MLA_ROPE)), MLA_Q_RANK ** -0.5),
        "mla_kv_norm": 1.0 + nrm(ks[5], (DEPTH, MLA_KV_RANK), 0.02),
        "mla_w_ukv": nrm(ks[6], (DEPTH, MLA_KV_RANK, MLA_HEADS * (MLA_NOPE + MLA_V)), MLA_KV_RANK ** -0.5),
        "ax_q_norm": 1.0 + nrm(ks[7], (DEPTH, HEAD_DIM), 0.02),
        "ax_k_norm": 1.0 + nrm(ks[8], (DEPTH, HEAD_DIM), 0.02),
        "diff_lambda": nrm(ks[9], (DEPTH, 4, DIFF_QK), 0.1),
        "diff_subln": 1.0 + nrm(ks[10], (DEPTH, DIFF_V), 0.02),
        "w_out": nrm(ks[11], (DEPTH, D_MIX, D_MODEL), beta * D_MIX ** -0.5),
        "ffn_w_gu": nrm(ks[12], (DEPTH, 2, D_MODEL, 2 * D_FF), D_MODEL ** -0.5),
        "ffn_w_down": nrm(ks[13], (DEPTH, 2, D_FF, D_MODEL), beta * D_FF ** -0.5),
        "ln_g": 1.0 + nrm(ks[14], (DEPTH, 3, D_MODEL), 0.02),
        "ln_b": nrm(ks[15], (DEPTH, 3, D_MODEL), 0.02),
    }


def reference(x, w_in, win_sink, mla_q_norm, mla_w_uq, mla_kv_norm, mla_w_ukv,
              ax_q_norm, ax_k_norm, diff_lambda, diff_subln, w_out,
              ffn_w_gu, ffn_w_down, ln_g, ln_b):
    alpha = (2 * DEPTH) ** 0.25
    for l in range(DEPTH):
        x = _layer_norm(alpha * x + 0.5 * _swiglu(x, ffn_w_gu[l, 0], ffn_w_down[l, 0]), ln_g[l, 0], ln_b[l, 0])
        mix = _token_mixers(x, w_in[l], win_sink[l], mla_q_norm[l], mla_w_uq[l], mla_kv_norm[l], mla_w_ukv[l],
                            ax_q_norm[l], ax_k_norm[l], diff_lambda[l], diff_subln[l], w_out[l], l)
        x = _layer_norm(alpha * x + mix, ln_g[l, 1], ln_b[l, 1])
        x = _layer_norm(alpha * x + 0.5 * _swiglu(x, ffn_w_gu[l, 1], ffn_w_down[l, 1]), ln_g[l, 2], ln_b[l, 2])
    return x
```

```python
import contextlib
import math
import numpy as np
import ml_dtypes
import concourse.bass as bass
import concourse.mybir as mybir
from concourse.bass_utils import run_bass_kernel_spmd

F32 = mybir.dt.float32
BF16 = mybir.dt.bfloat16
AF = mybir.ActivationFunctionType
ALU = mybir.AluOpType

D_MODEL = 1024
SEQ = 16384
DEPTH = 4
D_FF = 2816
NFC = D_FF // 128
TOK = 4096
NTT = TOK // 128
NG = TOK // 512
ALPHA = (2 * DEPTH) ** 0.25
EPS = 1e-5
N_IN = 2208

ENGS = ("pe", "act", "dve", "pool", "sp")
_CACHE = {}


class Sched:
    def __init__(self, nc):
        self.nc = nc
        self.ops = {e: [] for e in ENGS}
        self.lastw = {}
        self.readers = {}
        self.waited = {e: {} for e in ENGS}
        self.dma_cnt = {}
        self.dma_sems = []

    def _add_dep(self, eng, deps, tok):
        if tok is None:
            return
        kind, src, idx = tok
        if kind == 'eng' and src == eng and eng in ('pe', 'sp'):
            return
        w = self.waited[eng]
        k = (kind, src)
        if w.get(k, -1) >= idx:
            return
        w[k] = idx
        deps.append(tok)

    def op(self, eng, fn, reads=(), writes=(), dma=None, ndma=1):
        deps = []
        for r in reads:
            self._add_dep(eng, deps, self.lastw.get(r))
        for w in writes:
            self._add_dep(eng, deps, self.lastw.get(w))
            for t in self.readers.get(w, ()):
                self._add_dep(eng, deps, t)
        idx = len(self.ops[eng])
        self.ops[eng].append(dict(fn=fn, deps=deps, sig=False, dma=dma, ndma=ndma))
        if dma is not None:
            if dma not in self.dma_cnt:
                self.dma_cnt[dma] = 0
                self.dma_sems.append(dma)
            self.dma_cnt[dma] += ndma
            tok = ('dma', dma, self.dma_cnt[dma])
        else:
            tok = ('eng', eng, idx)
        for r in reads:
            self.readers.setdefault(r, []).append(tok)
        for w in writes:
            self.lastw[w] = tok
            self.readers[w] = []
        return tok

    def prepare(self):
        for e in ENGS:
            for rec in self.ops[e]:
                for kind, src, idx in rec['deps']:
                    if kind == 'eng':
                        self.ops[src][idx]['sig'] = True
            for rec in reversed(self.ops[e]):
                if rec['dma'] is None:
                    rec['sig'] = True
                    break
        self.sigcount = {}
        for e in ENGS:
            c = 0
            arr = []
            for rec in self.ops[e]:
                if rec['sig']:
                    c += 1
                arr.append(c)
            self.sigcount[e] = arr

    def run(self, e, h, esem, dpool, ebase, dbase):
        dsem = {k: dpool[i] for i, k in enumerate(self.dma_sems)}
        db = {k: dbase[i] for i, k in enumerate(self.dma_sems)}
        for rec in self.ops[e]:
            for kind, src, idx in rec['deps']:
                if kind == 'eng':
                    h.wait_ge(esem[src], ebase[src] + self.sigcount[src][idx])
                else:
                    h.wait_ge(dsem[src], db[src] + 16 * idx)
            r = rec['fn'](h)
            if rec['dma'] is not None:
                rl = r if isinstance(r, (list, tuple)) else [r]
                assert len(rl) == rec['ndma'], (len(rl), rec['ndma'])
                for ins in rl:
                    ins.then_inc(dsem[rec['dma']], 16)
            elif rec['sig']:
                r.then_inc(esem[e], 1)
        for e2 in ENGS:
            if self.sigcount[e2] and self.sigcount[e2][-1] > 0:
                h.wait_ge(esem[e2], ebase[e2] + self.sigcount[e2][-1])
        for k in self.dma_sems:
            h.wait_ge(dsem[k], db[k] + 16 * self.dma_cnt[k])


NUM_DEV = None
ARENA_WORDS = 48 * 1024 - 512


class Ctx:
    def __init__(self):
        self.nc = bass.Bass("TRN2", target_bir_lowering=False, num_devices=NUM_DEV)
        self.st = contextlib.ExitStack()
        self.arena = self.st.enter_context(self.nc.sbuf_tensor("arena", [128, ARENA_WORDS], F32))
        self.psum = self.st.enter_context(self.nc.psum_tensor("psum_all", [128, 8 * 512], F32))
        self.phases = []
        self.new_phase()

    def new_phase(self):
        self.S = Sched(self.nc)
        self.phases.append(self.S)
        self.off = 0
        self.bank = 0

    @staticmethod
    def _view(ap, shape, dt):
        n = int(np.prod(shape[1:]))
        if dt != F32:
            ap = ap.bitcast(dt)
        ap = ap[:, 0:n]
        if len(shape) == 3:
            ap = ap.rearrange("p (a b) -> p a b", b=shape[2])
        elif len(shape) == 4:
            ap = ap.rearrange("p (a b c) -> p a b c", b=shape[2], c=shape[3])
        return ap

    def sb(self, name, shape, dt):
        assert shape[0] == 128
        n = int(np.prod(shape[1:]))
        words = (n * (2 if dt == BF16 else 4) + 3) // 4
        assert self.off + words <= ARENA_WORDS, (name, self.off, words)
        ap = self.arena[:, self.off:self.off + words]
        self.off += words
        return self._view(ap, shape, dt)

    def ps(self, name, shape, dt):
        n = int(np.prod(shape[1:]))
        nb = (n * (2 if dt == BF16 else 4) + 2047) // 2048
        assert self.bank + nb <= 8, name
        ap = self.psum[:, self.bank * 512:(self.bank + nb) * 512]
        self.bank += nb
        return self._view(ap, shape, dt)

    def dram(self, name, shape, dt, kind="Internal"):
        return self.nc.dram_tensor(name, list(shape), dt, kind=kind).ap()

    def finish(self):
        nc = self.nc
        for S in self.phases:
            S.prepare()
        ndp = max(len(S.dma_sems) for S in self.phases)
        nph = len(self.phases)
        with contextlib.ExitStack() as st:
            esem = {e: st.enter_context(nc.semaphore("s_" + e)) for e in ENGS}
            dpool = [st.enter_context(nc.semaphore(f"d_{i}")) for i in range(ndp)]
            block = st.enter_context(nc.Block())
            ebases, dbases = [], []
            eb = {e: 0 for e in ENGS}
            dbv = [0] * ndp
            for S in self.phases:
                ebases.append(dict(eb))
                dbases.append(list(dbv))
                for e in ENGS:
                    eb[e] += S.sigcount[e][-1] if S.sigcount[e] else 0
                for i, k in enumerate(S.dma_sems):
                    dbv[i] += 16 * S.dma_cnt[k]

            def run(e, h):
                for pi, S in enumerate(self.phases):
                    S.run(e, h, esem, dpool, ebases[pi], dbases[pi])

            @block.tensor
            def _(h):
                run('pe', h)

            @block.scalar
            def _(h):
                run('act', h)

            @block.vector
            def _(h):
                run('dve', h)

            @block.gpsimd
            def _(h):
                run('pool', h)

            @block.sync
            def _(h):
                run('sp', h)
        self.st.close()
        return nc


def make_ident(C, name="ident"):
    S = C.S
    idf = C.sb(name + "_f", [128, 128], F32)
    idb = C.sb(name, [128, 128], BF16)
    S.op('pool', lambda h: h.memset(idf[:], 1.0), writes=[name + '_f'])
    S.op('pool', lambda h: h.affine_select(out=idf[:], in_=idf[:], pattern=[[-1, 128]],
                                            compare_op=ALU.is_equal, fill=0.0, base=0,
                                            channel_multiplier=1),
         reads=[name + '_f'], writes=[name + '_f'])
    S.op('pool', lambda h: h.tensor_copy(out=idb[:], in_=idf[:]), reads=[name + '_f'], writes=[name])
    return idb, idf


def load_xT_group(C, P, g, xin, scale_resid=True):
    S = C.S
    XR = P['XR'][g % 2]
    rk = ('XR', g % 2)
    for t in range(4):
        tt = g * 4 + t
        S.op('sp', lambda h, t=t, tt=tt: h.dma_start(out=XR[:, t, :], in_=xin[tt * 128:(tt + 1) * 128, :]),
             writes=[rk + (t,)], dma=f"xr{g % 2}{t}")
        xb = P['xb'][t % 2]
        S.op('dve', lambda h, t=t, xb=xb: h.tensor_copy(out=xb[:], in_=XR[:, t, :]),
             reads=[rk + (t,)], writes=[('xb', t % 2)])
        if scale_resid:
            S.op('act', lambda h, t=t: h.mul(out=XR[:, t, :], in_=XR[:, t, :], mul=ALPHA),
                 reads=[rk + (t,)], writes=[rk + (t,)])
        pT = P['pT']
        for k in range(8):
            S.op('pe', lambda h, k=k, xb=xb: h.transpose(out=pT[:, k, :], in_=xb[:, k * 128:(k + 1) * 128],
                                                          identity=P['ident'][:]),
                 reads=[('xb', t % 2), 'ident'], writes=['pT'])
        S.op('act', lambda h, t=t: h.copy(out=P['XT'][:, :, t * 128:(t + 1) * 128], in_=pT[:, :, :]),
             reads=['pT'], writes=['XT'])


def ln_tiles(C, P, g, xout, lnidx):
    S = C.S
    XR = P['XR'][g % 2]
    rk = ('XR', g % 2)
    st, mv, rstd = P['st'], P['mv'], P['rstd']
    for t in range(4):
        for hh in range(2):
            S.op('dve', lambda h, t=t, hh=hh: h.bn_stats(out=st[:, t, hh * 6:(hh + 1) * 6],
                                                          in_=XR[:, t, hh * 512:(hh + 1) * 512]),
                 reads=[rk + (t,)], writes=[('st', t)])
        S.op('dve', lambda h, t=t: h.bn_aggr(out=mv[:, t, :], in_=st[:, t, :]),
             reads=[('st', t)], writes=['mv'])
    S.op('dve', lambda h: h.tensor_scalar_add(out=rstd[:, :], in0=mv[:, :, 1], scalar1=EPS),
         reads=['mv'], writes=['rstd'])
    S.op('act', lambda h: h.sqrt(out=rstd[:, :], in_=rstd[:, :]), reads=['rstd'], writes=['rstd'])
    S.op('dve', lambda h: h.reciprocal(out=rstd[:, :], in_=rstd[:, :]), reads=['rstd'], writes=['rstd'])
    for t in range(4):
        tt = g * 4 + t
        ot = P['ot'][t % 2]
        ok = ('ot', t % 2)
        S.op('dve', lambda h, t=t, ot=ot: h.tensor_scalar(out=ot[:], in0=XR[:, t, :], scalar1=mv[:, t, 0:1],
                                                           scalar2=rstd[:, t:t + 1], op0=ALU.subtract,
                                                           op1=ALU.mult),
             reads=[rk + (t,), 'mv', 'rstd'], writes=[ok])
        S.op('pool', lambda h, ot=ot: h.tensor_tensor(out=ot[:], in0=ot[:], in1=P['G'][:, lnidx, :], op=ALU.mult),
             reads=[ok, 'GB'], writes=[ok])
        S.op('pool', lambda h, ot=ot: h.tensor_tensor(out=ot[:], in0=ot[:], in1=P['B'][:, lnidx, :], op=ALU.add),
             reads=[ok, 'GB'], writes=[ok])
        S.op('pool', lambda h, tt=tt, ot=ot: h.dma_start(out=xout[tt * 128:(tt + 1) * 128, :], in_=ot[:]),
             reads=[ok], dma=f"xo{t % 2}")


def phase_ffn(C, xin, xout, wgu, wd, lng, lnb, tag="f", dbg=None):
    nc, S = C.nc, C.S
    P = {}
    P['ident'], _ = make_ident(C, "ident" + tag)
    S.lastw['ident'] = S.lastw["ident" + tag]
    WD = C.sb("WD" + tag, [128, NFC, 1024], BF16)
    P['XT'] = C.sb("XT" + tag, [128, 8, 512], BF16)
    AT = C.sb("AT" + tag, [128, NFC, 512], BF16)
    P['XR'] = [C.sb(f"XR{i}" + tag, [128, 4, 1024], F32) for i in range(2)]
    WG = [C.sb(f"WG{i}" + tag, [128, 8, 256], BF16) for i in range(3)]
    STG = [C.sb(f"STG{i}" + tag, [128, 8, 256], F32) for i in range(2)]
    WDS = [C.sb(f"WDS{i}" + tag, [128, 1024], F32) for i in range(2)]
    P['G'] = C.sb("G" + tag, [128, 1, 1024], F32)
    P['B'] = C.sb("B" + tag, [128, 1, 1024], F32)
    P['xb'] = [C.sb(f"xb{i}" + tag, [128, 1024], BF16) for i in range(2)]
    SG = [C.sb(f"sg{i}" + tag, [128, 512], F32) for i in range(2)]
    P['ot'] = [C.sb(f"ot{i}" + tag, [128, 1024], F32) for i in range(2)]
    P['st'] = C.sb("st" + tag, [128, 4, 12], F32)
    P['mv'] = C.sb("mv" + tag, [128, 4, 2], F32)
    P['rstd'] = C.sb("rstd" + tag, [128, 4], F32)
    P['pT'] = C.ps("pT" + tag, [128, 8, 128], BF16)
    PG = [C.ps(f"pg{i}" + tag, [128, 512], F32) for i in range(2)]
    PU = [C.ps(f"pu{i}" + tag, [128, 512], F32) for i in range(2)]
    PY = [C.ps(f"py{i}" + tag, [128, 512], F32) for i in range(2)]
    WGS = C.dram("wgs" + tag, [NFC, 128, 2048], BF16)

    S.op('sp', lambda h: [h.dma_start(out=P['G'][:, 0, :], in_=lng.partition_broadcast(128)),
                          h.dma_start(out=P['B'][:, 0, :], in_=lnb.partition_broadcast(128))],
         writes=['GB'], dma="gb", ndma=2)
    wgu_r = wgu.rearrange("(k p) n -> p k n", p=128)
    for c in range(NFC):
        stg = STG[c % 2]
        S.op('sp', lambda h, c=c, stg=stg: [
            h.dma_start(out=stg[:, :, 0:128], in_=wgu_r[:, :, c * 128:(c + 1) * 128]),
            h.dma_start(out=stg[:, :, 128:256], in_=wgu_r[:, :, D_FF + c * 128:D_FF + (c + 1) * 128])],
            writes=[('stg', c % 2)], dma=f"stg{c % 2}", ndma=2)
        wg = WG[c % 3]
        S.op('pool', lambda h, stg=stg, wg=wg: h.tensor_copy(out=wg[:], in_=stg[:]),
             reads=[('stg', c % 2)], writes=[('wg', c % 3)])
        S.op('pool', lambda h, c=c, wg=wg: h.dma_start(out=WGS[c].rearrange("p (k n) -> p k n", k=8), in_=wg[:]),
             reads=[('wg', c % 3)], writes=[('wgs', c)], dma=f"wgsw{c % 3}")
    for c in range(NFC):
        wds = WDS[c % 2]
        S.op('sp', lambda h, c=c, wds=wds: h.dma_start(out=wds[:], in_=wd[c * 128:(c + 1) * 128, :]),
             writes=[('wds', c % 2)], dma=f"wds{c % 2}")
        S.op('dve', lambda h, c=c, wds=wds: h.tensor_copy(out=WD[:, c, :], in_=wds[:]),
             reads=[('wds', c % 2)], writes=['WD'])

    for g in range(NG):
        load_xT_group(C, P, g, xin)
        XR = P['XR'][g % 2]
        rk = ('XR', g % 2)
        for c in range(NFC):
            wg = WG[c % 3]
            S.op('sp', lambda h, c=c, wg=wg: h.dma_start(out=wg[:], in_=WGS[c].rearrange("p (k n) -> p k n", k=8)),
                 reads=[('wgs', c)], writes=[('wg', c % 3)], dma=f"wg{c % 3}")
            pg, pu = PG[c % 2], PU[c % 2]
            for k in range(8):
                S.op('pe', lambda h, k=k, wg=wg, pg=pg: h.matmul(pg[:], lhsT=wg[:, k, 0:128], rhs=P['XT'][:, k, :],
                                                                 start=(k == 0), stop=(k == 7)),
                     reads=[('wg', c % 3), 'XT'], writes=[('pg', c % 2)])
            for k in range(8):
                S.op('pe', lambda h, k=k, wg=wg, pu=pu: h.matmul(pu[:], lhsT=wg[:, k, 128:256], rhs=P['XT'][:, k, :],
                                                                 start=(k == 0), stop=(k == 7)),
                     reads=[('wg', c % 3), 'XT'], writes=[('pu', c % 2)])
            sg = SG[c % 2]
            S.op('act', lambda h, pg=pg, sg=sg: h.activation(out=sg[:], in_=pg[:], func=AF.Silu),
                 reads=[('pg', c % 2)], writes=[('sg', c % 2)])
            S.op('dve', lambda h, c=c, sg=sg, pu=pu: h.tensor_tensor(out=AT[:, c, :], in0=sg[:], in1=pu[:], op=ALU.mult),
                 reads=[('sg', c % 2), ('pu', c % 2)], writes=['AT'])
        if dbg is not None and g == 0:
            S.op('sp', lambda h: h.dma_start(out=dbg['xt'].rearrange("p (k n) -> p k n", k=8), in_=P['XT'][:]), reads=['XT'], dma='dbg1')
            S.op('sp', lambda h: h.dma_start(out=dbg['at'].rearrange("p (k n) -> p k n", k=NFC), in_=AT[:]), reads=['AT'], dma='dbg2')
            S.op('sp', lambda h: h.dma_start(out=dbg['wd'].rearrange("p (k n) -> p k n", k=NFC), in_=WD[:]), reads=['WD'], dma='dbg3')
            S.op('sp', lambda h: h.dma_start(out=dbg['wg'].rearrange("p (k n) -> p k n", k=8), in_=WG[(NFC - 1) % 3][:]), reads=[('wg', (NFC - 1) % 3)], dma='dbg4')
        for t in range(4):
            for hh in range(2):
                py = PY[hh]
                for c in range(NFC):
                    S.op('pe', lambda h, c=c, t=t, hh=hh, py=py: h.matmul(
                        py[:], lhsT=AT[:, c, t * 128:(t + 1) * 128], rhs=WD[:, c, hh * 512:(hh + 1) * 512],
                        start=(c == 0), stop=(c == NFC - 1)),
                        reads=['AT', 'WD'], writes=[('py', hh)])
                S.op('dve', lambda h, t=t, hh=hh, py=py, XR=XR: h.scalar_tensor_tensor(
                    out=XR[:, t, hh * 512:(hh + 1) * 512], in0=py[:], scalar=0.5,
                    in1=XR[:, t, hh * 512:(hh + 1) * 512], op0=ALU.mult, op1=ALU.add),
                    reads=[('py', hh), rk + (t,)], writes=[rk + (t,)])
        ln_tiles(C, P, g, xout, 0)


QROW_A, QROW_B, QROW_C, QROW_D, RQ = 0, 256, 640, 896, 1152
KROW_A, KROW_BN, KROW_BR, KROW_C, KROW_D, RK = 0, 128, 384, 416, 544, 800
VCOL_A, VCOL_C, VCOL_D, VCOL_B, RV = 0, 128, 256, 512, 768
SC_A = 64 ** -0.5
SC_B = 96 ** -0.5
SC_C = 64 ** -0.5
SC_D = 32 ** -0.5
PR_SKIP = set()
PAR_GQ0, PAR_GQ1, PAR_GKV, PAR_AXQ, PAR_AXQP, PAR_AXK, PAR_AXKP = 0, 1, 2, 3, 4, 5, 6
NPAR = 8


def phase_pr(C, xin, w_in, w_uq, w_ukv, par, tcc, tcs, tbc, tbs, QT, KT, V, tag="p"):
    nc, S = C.nc, C.S
    P = {}
    P['ident'], _ = make_ident(C, "ident" + tag)
    S.lastw['ident'] = S.lastw["ident" + tag]
    P['XT'] = C.sb("XT" + tag, [128, 8, 512], BF16)
    P['XR'] = [C.sb(f"XR{i}" + tag, [128, 4, 1024], F32) for i in range(2)]
    P['xb'] = [C.sb(f"xb{i}" + tag, [128, 1024], BF16) for i in range(2)]
    P['pT'] = C.ps("pT" + tag, [128, 8, 128], BF16)
    WIN = C.sb("WIN" + tag, [128, 8, N_IN], BF16)
    WINP = C.sb("WINP" + tag, [128, 8, 480], BF16)
    WINV = C.sb("WINV" + tag, [128, 8, 512], BF16)
    WST = [C.sb(f"WST{i}" + tag, [128, N_IN], F32) for i in range(2)]
    WUQ = C.sb("WUQ" + tag, [128, 2, 384], BF16)
    WUQP = C.sb("WUQP" + tag, [128, 2, 384], BF16)
    WUKVK = C.sb("WUKVK" + tag, [128, 256], BF16)
    WUKVV = C.sb("WUKVV" + tag, [128, 256], BF16)
    PAR = C.sb("PAR" + tag, [128, NPAR], F32)
    ONESF = C.sb("ONESF" + tag, [128, 128], BF16)
    BDF = C.sb("BDF" + tag, [128, 128], BF16)
    SQH = [C.sb(f"SQH{i}" + tag, [128, 512], BF16) for i in range(2)]
    SQL = [C.sb(f"SQL{i}" + tag, [128, 512], BF16) for i in range(2)]
    TCC = [C.sb(f"TCC{i}" + tag, [128, 512], F32) for i in range(2)]
    TCS = [C.sb(f"TCS{i}" + tag, [128, 512], F32) for i in range(2)]
    TBC = [C.sb(f"TBC{i}" + tag, [128, 512], F32) for i in range(2)]
    TBS = [C.sb(f"TBS{i}" + tag, [128, 512], F32) for i in range(2)]
    SQ = [C.sb(f"SQ{i}" + tag, [128, 512], F32) for i in range(2)]
    R = C.sb("R" + tag, [128, 512], F32)
    T1 = C.sb("T1" + tag, [128, 512], F32)
    T2 = C.sb("T2" + tag, [128, 512], F32)
    CQN = C.sb("CQN" + tag, [128, 2, 512], BF16)
    CKVN = C.sb("CKVN" + tag, [128, 512], BF16)
    OB = [C.sb(f"OB{i}" + tag, [128, 512], BF16) for i in range(4)]
    VO = [C.sb(f"VO{i}" + tag, [128, RV], BF16) for i in range(2)]
    PA = [C.ps(f"pa{i}" + tag, [128, 512], F32) for i in range(4)]
    PB = [C.ps(f"pb{i}" + tag, [128, 512], F32) for i in range(2)]
    st_ = dict(pa=0, ob=0, pb=0)

    S.op('sp', lambda h: h.dma_start(out=PAR[:], in_=par[:, :]), writes=['PAR'], dma="par")
    S.op('pool', lambda h: h.memset(ONESF[:], 1.0), writes=['ONESF'])
    S.op('pool', lambda h: h.memset(BDF[:], 0.0), writes=['BDF'])
    S.op('pool', lambda h: h.memset(BDF[0:64, 0:64], 1.0), reads=['BDF'], writes=['BDF'])
    S.op('pool', lambda h: h.memset(BDF[64:128, 64:128], 1.0), reads=['BDF'], writes=['BDF'])
    for k in range(8):
        wst = WST[k % 2]
        S.op('sp', lambda h, k=k, wst=wst: h.dma_start(out=wst[:], in_=w_in[k * 128:(k + 1) * 128, :]),
             writes=[('wst', k % 2)], dma=f"wst{k % 2}")
        S.op('dve', lambda h, k=k, wst=wst: h.tensor_copy(out=WIN[:, k, :], in_=wst[:]),
             reads=[('wst', k % 2)], writes=[('WIN', k)])
        src = WIN[:, k, 928:1312].rearrange("p (a b c) -> p a b c", b=2, c=16)
        dst = WINP[:, k, 0:384].rearrange("p (a b c) -> p a b c", b=2, c=16)
        S.op('pool', lambda h, src=src, dst=dst: h.tensor_copy(out=dst[:, :, 0, :], in_=src[:, :, 1, :]),
             reads=[('WIN', k)], writes=[('WINP', k)])
        S.op('pool', lambda h, src=src, dst=dst: h.tensor_copy(out=dst[:, :, 1, :], in_=src[:, :, 0, :]),
             reads=[('WIN', k)], writes=[('WINP', k)])
        S.op('pool', lambda h, k=k: h.tensor_copy(out=WINP[:, k, 384:448], in_=WIN[:, k, 832:896]),
             reads=[('WIN', k)], writes=[('WINP', k)])
        S.op('pool', lambda h, k=k: h.tensor_copy(out=WINP[:, k, 448:464], in_=WIN[:, k, 912:928]),
             reads=[('WIN', k)], writes=[('WINP', k)])
        S.op('pool', lambda h, k=k: h.tensor_copy(out=WINP[:, k, 464:480], in_=WIN[:, k, 896:912]),
             reads=[('WIN', k)], writes=[('WINP', k)])
        S.op('pool', lambda h, k=k: h.tensor_copy(out=WINV[:, k, 0:128], in_=WIN[:, k, 384:512]),
             reads=[('WIN', k)], writes=[('WINV', k)])
        S.op('pool', lambda h, k=k: h.tensor_copy(out=WINV[:, k, 128:256], in_=WIN[:, k, 1312:1440]),
             reads=[('WIN', k)], writes=[('WINV', k)])
        S.op('pool', lambda h, k=k: h.tensor_copy(out=WINV[:, k, 256:512], in_=WIN[:, k, 1952:2208]),
             reads=[('WIN', k)], writes=[('WINV', k)])
    for j in range(2):
        wst = WST[j % 2]
        S.op('sp', lambda h, j=j, wst=wst: h.dma_start(out=wst[:, 0:384], in_=w_uq[j * 128:(j + 1) * 128, :]),
             writes=[('wst', j % 2)], dma=f"wst{j % 2}")
        S.op('dve', lambda h, j=j, wst=wst: h.tensor_copy(out=WUQ[:, j, :], in_=wst[:, 0:384]),
             reads=[('wst', j % 2)], writes=['WUQ'])
        sv = WUQ[:, j, :].rearrange("p (a b) -> p a b", b=96)
        dv = WUQP[:, j, :].rearrange("p (a b) -> p a b", b=96)
        S.op('pool', lambda h, sv=sv, dv=dv: h.tensor_copy(out=dv[:, :, 0:64], in_=sv[:, :, 0:64]),
             reads=['WUQ'], writes=['WUQP'])
        S.op('pool', lambda h, sv=sv, dv=dv: h.tensor_copy(out=dv[:, :, 64:80], in_=sv[:, :, 80:96]),
             reads=['WUQ'], writes=['WUQP'])
        S.op('pool', lambda h, sv=sv, dv=dv: h.tensor_copy(out=dv[:, :, 80:96], in_=sv[:, :, 64:80]),
             reads=['WUQ'], writes=['WUQP'])
    wst = WST[0]
    S.op('sp', lambda h: h.dma_start(out=WST[0][:, 0:512], in_=w_ukv[:, :]), writes=[('wst', 0)], dma="wst0")
    sv = WST[0][:, 0:512].rearrange("p (a b) -> p a b", b=128)
    S.op('dve', lambda h, sv=sv: h.tensor_copy(out=WUKVK[:, :].rearrange("p (a b) -> p a b", b=64), in_=sv[:, :, 0:64]),
         reads=[('wst', 0)], writes=['WUKV'])
    S.op('dve', lambda h, sv=sv: h.tensor_copy(out=WUKVV[:, :].rearrange("p (a b) -> p a b", b=64), in_=sv[:, :, 64:128]),
         reads=[('wst', 0)], writes=['WUKV'])
    WKEYS = [('WIN', k) for k in range(8)] + [('WINP', k) for k in range(8)] + [('WINV', k) for k in range(8)]

    def pa_next():
        i = st_['pa'] % 4
        st_['pa'] += 1
        return PA[i], ('pa', i)

    def pb_next():
        i = st_['pb'] % 2
        st_['pb'] += 1
        return PB[i], ('pb', i)

    def ob_next():
        i = st_['ob'] % 4
        st_['ob'] += 1
        return OB[i], ('ob', i)

    def fm(W, c0, m):
        pa, pk = pa_next()
        for k in range(8):
            S.op('pe', lambda h, k=k, pa=pa: h.matmul(pa[0:m, :], lhsT=W[:, k, c0:c0 + m], rhs=P['XT'][:, k, :],
                                                       start=(k == 0), stop=(k == 7)),
                 reads=WKEYS + ['XT'], writes=[pk])
        return pa, pk

    def store(ob, ok, lo, hi, dst, row0):
        g = st_['g']
        S.op('sp', lambda h: h.dma_start(out=dst[row0:row0 + (hi - lo), g * 512:(g + 1) * 512], in_=ob[lo:hi, :]),
             reads=[ok], dma="st" + ok[0] + str(ok[1]))

    def rms_scale(ssps, sskey, inv_n, lnscale):
        S.op('act', lambda h: h.activation(out=R[:], in_=ssps[:], func=AF.Ln, bias=EPS, scale=inv_n),
             reads=[sskey], writes=['R'])
        S.op('act', lambda h: h.activation(out=R[:], in_=R[:], func=AF.Exp, bias=lnscale, scale=-0.5),
             reads=['R'], writes=['R'])

    def sumsq(srcs, lhsT, lkey):
        pb, pbk = pb_next()
        n = len(srcs)
        for j, (ps_, psk) in enumerate(srcs):
            S.op('act', lambda h, ps_=ps_, j=j: h.activation(out=SQ[j][:], in_=ps_[:], func=AF.Square),
                 reads=[psk], writes=[('SQ', j)])
            S.op('dve', lambda h, j=j: h.tensor_copy(out=SQH[j][:], in_=SQ[j][:]), reads=[('SQ', j)], writes=[('SQH', j)])
            S.op('dve', lambda h, j=j: h.tensor_tensor(out=SQL[j][:], in0=SQ[j][:], in1=SQH[j][:], op=ALU.subtract),
                 reads=[('SQ', j), ('SQH', j)], writes=[('SQL', j)])
        for j in range(n):
            S.op('pe', lambda h, pb=pb, j=j: h.matmul(pb[:], lhsT=lhsT[:], rhs=SQH[j][:], start=(j == 0), stop=False),
                 reads=[lkey, ('SQH', j)], writes=[pbk])
            S.op('pe', lambda h, pb=pb, j=j: h.matmul(pb[:], lhsT=lhsT[:], rhs=SQL[j][:], start=False, stop=(j == n - 1)),
                 reads=[lkey, ('SQL', j)], writes=[pbk])
        return pb, pbk

    for g in range(NG):
        st_['g'] = g
        load_xT_group(C, P, g, xin, scale_resid=False)
        tcc_, tcs_, tbc_, tbs_ = TCC[g % 2], TCS[g % 2], TBC[g % 2], TBS[g % 2]
        tk = ('tab', g % 2)
        S.op('sp', lambda h, g=g, a=tcc_, b=tcs_, c=tbc_, d=tbs_: [
            h.dma_start(out=a[:], in_=tcc[:, g * 512:(g + 1) * 512]),
            h.dma_start(out=b[:], in_=tcs[:, g * 512:(g + 1) * 512]),
            h.dma_start(out=c[64:96, :], in_=tbc[:, g * 512:(g + 1) * 512]),
            h.dma_start(out=d[64:96, :], in_=tbs[:, g * 512:(g + 1) * 512])],
            writes=[tk], dma=f"tab{g % 2}", ndma=4)

        for (c0, scale, dst, row0) in [] if 'AD' in PR_SKIP else [(0, SC_A, QT, QROW_A), (128, SC_A, QT, QROW_A + 128),
                                       (256, None, KT, KROW_A),
                                       (1440, SC_D, QT, QROW_D), (1568, SC_D, QT, QROW_D + 128),
                                       (1696, None, KT, KROW_D), (1824, None, KT, KROW_D + 128)]:
            pa, pk = fm(WIN, c0, 128)
            ob, ok = ob_next()
            if scale is None:
                S.op('act', lambda h, pa=pa, ob=ob: h.copy(out=ob[:], in_=pa[:]), reads=[pk], writes=[ok])
            else:
                S.op('act', lambda h, pa=pa, ob=ob, scale=scale: h.mul(out=ob[:], in_=pa[:], mul=scale),
                     reads=[pk], writes=[ok])
            store(ob, ok, 0, 128, dst, row0)

        for (c0, cp0, gcol, gpcol, lnsc, dst, row0) in [] if 'C' in PR_SKIP else [
                (928, 0, PAR_AXQ, PAR_AXQP, math.log(SC_C), QT, QROW_C),
                (1056, 128, PAR_AXQ, PAR_AXQP, math.log(SC_C), QT, QROW_C + 128),
                (1184, 256, PAR_AXK, PAR_AXKP, 0.0, KT, KROW_C)]:
            pm, pmk = fm(WIN, c0, 128)
            pp, ppk = fm(WINP, cp0, 128)
            pb, pbk = sumsq([(pm, pmk)], BDF, 'BDF')
            rms_scale(pb, pbk, 1.0 / 64, lnsc)
            S.op('dve', lambda h, pm=pm, gcol=gcol, a=tcc_: h.scalar_tensor_tensor(
                out=T1[:], in0=pm[:], scalar=PAR[:, gcol:gcol + 1], in1=a[:], op0=ALU.mult, op1=ALU.mult),
                reads=[pmk, 'PAR', tk], writes=['T1'])
            S.op('dve', lambda h, pp=pp, gpcol=gpcol, b=tcs_: h.scalar_tensor_tensor(
                out=T2[:], in0=pp[:], scalar=PAR[:, gpcol:gpcol + 1], in1=b[:], op0=ALU.mult, op1=ALU.mult),
                reads=[ppk, 'PAR', tk], writes=['T2'])
            S.op('dve', lambda h: h.tensor_tensor(out=T1[:], in0=T1[:], in1=T2[:], op=ALU.add),
                 reads=['T1', 'T2'], writes=['T1'])
            ob, ok = ob_next()
            S.op('dve', lambda h, ob=ob: h.tensor_tensor(out=ob[:], in0=T1[:], in1=R[:], op=ALU.mult),
                 reads=['T1', 'R'], writes=[ok])
            store(ob, ok, 0, 128, dst, row0)

        if 'B' in PR_SKIP:
            continue
        pcq = []
        for j in range(2):
            pa, pk = fm(WIN, 512 + j * 128, 128)
            pcq.append((pa, pk))
        pb, pbk = sumsq(pcq, ONESF, 'ONESF')
        rms_scale(pb, pbk, 1.0 / 256, 0.0)
        for j in range(2):
            pa, pk = pcq[j]
            S.op('dve', lambda h, pa=pa, j=j: h.scalar_tensor_tensor(
                out=CQN[:, j, :], in0=pa[:], scalar=PAR[:, PAR_GQ0 + j:PAR_GQ0 + j + 1], in1=R[:],
                op0=ALU.mult, op1=ALU.mult), reads=[pk, 'PAR', 'R'], writes=['CQN'])
        for hh in range(4):
            pm, pmk = pa_next()
            pp, ppk = pa_next()
            for (ps_, psk, W) in ((pm, pmk, WUQ), (pp, ppk, WUQP)):
                for j in range(2):
                    S.op('pe', lambda h, ps_=ps_, W=W, j=j, hh=hh: h.matmul(
                        ps_[0:96, :], lhsT=W[:, j, hh * 96:(hh + 1) * 96], rhs=CQN[:, j, :],
                        start=(j == 0), stop=(j == 1)), reads=['WUQ', 'WUQP', 'CQN'], writes=[psk])
            ob, ok = ob_next()
            S.op('act', lambda h, pm=pm, ob=ob: h.mul(out=ob[0:64, :], in_=pm[0:64, :], mul=SC_B),
                 reads=[pmk], writes=[ok])
            S.op('dve', lambda h, pm=pm, c=tbc_: h.scalar_tensor_tensor(
                out=T1[64:96, :], in0=pm[64:96, :], scalar=SC_B, in1=c[64:96, :], op0=ALU.mult, op1=ALU.mult),
                reads=[pmk, tk], writes=['T1'])
            S.op('dve', lambda h, pp=pp, d=tbs_: h.scalar_tensor_tensor(
                out=T2[64:96, :], in0=pp[64:96, :], scalar=SC_B, in1=d[64:96, :], op0=ALU.mult, op1=ALU.mult),
                reads=[ppk, tk], writes=['T2'])
            S.op('dve', lambda h, ob=ob: h.tensor_tensor(out=ob[64:96, :], in0=T1[64:96, :], in1=T2[64:96, :], op=ALU.add),
                 reads=['T1', 'T2', ok], writes=[ok])
            store(ob, ok, 0, 96, QT, QROW_B + hh * 96)
        pkv, pkvk = fm(WIN, 768, 128)
        pb, pbk = sumsq([(pkv, pkvk)], ONESF, 'ONESF')
        rms_scale(pb, pbk, 1.0 / 128, 0.0)
        S.op('dve', lambda h, pkv=pkv: h.scalar_tensor_tensor(
            out=CKVN[:], in0=pkv[:], scalar=PAR[:, PAR_GKV:PAR_GKV + 1], in1=R[:], op0=ALU.mult, op1=ALU.mult),
            reads=[pkvk, 'PAR', 'R'], writes=['CKVN'])
        for jj in range(2):
            pa, pk = pa_next()
            S.op('pe', lambda h, pa=pa, jj=jj: h.matmul(pa[:], lhsT=WUKVK[:, jj * 128:(jj + 1) * 128], rhs=CKVN[:],
                                                         start=True, stop=True), reads=['WUKV', 'CKVN'], writes=[pk])
            ob, ok = ob_next()
            S.op('act', lambda h, pa=pa, ob=ob: h.copy(out=ob[:], in_=pa[:]), reads=[pk], writes=[ok])
            store(ob, ok, 0, 128, KT, KROW_BN + jj * 128)
        pm, pmk = fm(WIN, 832, 96)
        pp, ppk = fm(WINP, 384, 96)
        S.op('dve', lambda h, pm=pm, c=tbc_: h.tensor_tensor(out=T1[64:96, :], in0=pm[64:96, :], in1=c[64:96, :], op=ALU.mult),
             reads=[pmk, tk], writes=['T1'])
        S.op('dve', lambda h, pp=pp, d=tbs_: h.tensor_tensor(out=T2[64:96, :], in0=pp[64:96, :], in1=d[64:96, :], op=ALU.mult),
             reads=[ppk, tk], writes=['T2'])
        ob, ok = ob_next()
        S.op('dve', lambda h, ob=ob: h.tensor_tensor(out=ob[64:96, :], in0=T1[64:96, :], in1=T2[64:96, :], op=ALU.add),
             reads=['T1', 'T2'], writes=[ok])
        store(ob, ok, 64, 96, KT, KROW_BR)
        for tl in range(4):
            pa, pk = pa_next()
            for k in range(8):
                S.op('pe', lambda h, pa=pa, k=k, tl=tl: h.matmul(pa[:], lhsT=P['XT'][:, k, tl * 128:(tl + 1) * 128],
                                                                 rhs=WINV[:, k, :], start=(k == 0), stop=(k == 7)),
                     reads=WKEYS + ['XT'], writes=[pk])
            pb, pbk = pb_next()
            S.op('pe', lambda h, pb=pb, tl=tl: h.matmul(pb[:, 0:256], lhsT=CKVN[:, tl * 128:(tl + 1) * 128], rhs=WUKVV[:],
                                                        start=True, stop=True), reads=['CKVN', 'WUKV'], writes=[pbk])
            vo = VO[tl % 2]
            vk = ('vo', tl % 2)
            S.op('act', lambda h, pa=pa, vo=vo: h.copy(out=vo[:, 0:512], in_=pa[:]), reads=[pk], writes=[vk])
            S.op('dve', lambda h, pb=pb, vo=vo: h.tensor_copy(out=vo[:, 512:768], in_=pb[:, 0:256]), reads=[pbk, vk], writes=[vk])
            tt = g * 4 + tl
            S.op('sp', lambda h, vo=vo, tt=tt: h.dma_start(out=V[tt * 128:(tt + 1) * 128, :], in_=vo[:]),
                 reads=[vk], dma=f"vo{tl % 2}")


NLP = 136
LP_SINK, LP_LAM, LP_LINIT, LP_1MLINIT, LP_SUBG = 0, 4, 132, 133, 134
KCOLS = 5 * TOK
AT_UNITS = None
AT_GROUPS = None


def phase_at(C, QT, KTall, Vall, kaug, qaug, ddiag, nsi, wdist, wmask, lpar, mixT, tag="a"):
    nc, S = C.nc, C.S
    IDB, _ = make_ident(C, "identa")
    KS = [C.sb(f"KS{i}", [128, KCOLS], BF16) for i in range(2)]
    VS = [C.sb(f"VS{i}", [128, 128, 96], BF16) for i in range(2)]
    QS = [C.sb(f"QS{i}", [128, TOK], BF16) for i in range(2)]
    DD = C.sb("DD", [128, 4, 2, 512], BF16)
    NSI = C.sb("NSI", [128, 8, 128], BF16)
    WDIST = C.sb("WDIST", [128, 3, 128], BF16)
    WMASK = C.sb("WMASK", [128, 5, 128], BF16)
    LP = C.sb("LP", [128, NLP], F32)
    SM = C.sb("SM", [128, 16], F32)
    TMP = C.sb("TMPa", [128, 32], F32)
    ONESB = C.sb("ONESBa", [128, 64], BF16)
    PT = [C.sb(f"PT{i}", [128, 1024], BF16) for i in range(3)]
    RC = [C.sb(f"RC{i}", [128, 512], F32) for i in range(2)]
    A1 = C.sb("A1", [128, 512], F32)
    DN = A1
    A2 = C.sb("A2", [128, 512], F32)
    O1 = A2
    SQd = C.sb("SQd", [128, 512], F32)
    SQHd = C.sb("SQHd", [128, 512], BF16)
    SQLd = C.sb("SQLd", [128, 512], BF16)
    Rr = C.sb("Rr", [128, 512], F32)
    OUTB = [C.sb(f"OUTB{i}", [128, 512], BF16) for i in range(2)]
    PSS = [C.ps(f"pss{i}", [128, 1024], F32) for i in range(2)]
    PO = [C.ps(f"po{i}", [128, 512], F32) for i in range(3)]
    PBC = C.ps("pbc", [128, 512], F32)
    cnt = dict(po=0, rc=0, outb=0)

    S.op('sp', lambda h: [h.dma_start(out=DD[:].rearrange("p a b c -> p (a b c)"), in_=ddiag[:, :]),
                          h.dma_start(out=NSI[:].rearrange("p a b -> p (a b)"), in_=nsi[:, :]),
                          h.dma_start(out=WDIST[:].rearrange("p a b -> p (a b)"), in_=wdist[:, :]),
                          h.dma_start(out=WMASK[:].rearrange("p a b -> p (a b)"), in_=wmask[:, :]),
                          h.dma_start(out=LP[:], in_=lpar[:, :])],
         writes=['CONST'], dma="const", ndma=5)
    S.op('pool', lambda h: h.memset(ONESB[:], 1.0), writes=['ONESB'])
    for i in range(2):
        S.op('pool', lambda h, i=i: h.memset(VS[i][:, :, 64:96], 1.0), writes=[('Vones', i)])
    S.op('dve', lambda h: h.tensor_tensor(out=TMP[:], in0=LP[:, 4:36], in1=LP[:, 36:68], op=ALU.mult), reads=['CONST'], writes=['TMP'])
    S.op('dve', lambda h: h.reduce_sum(out=SM[:, 0:1], in_=TMP[:], axis=mybir.AxisListType.X), reads=['TMP'], writes=['SM0'])
    S.op('dve', lambda h: h.tensor_tensor(out=TMP[:], in0=LP[:, 68:100], in1=LP[:, 100:132], op=ALU.mult), reads=['CONST', 'SM0'], writes=['TMP'])
    S.op('dve', lambda h: h.reduce_sum(out=SM[:, 1:2], in_=TMP[:], axis=mybir.AxisListType.X), reads=['TMP'], writes=['SM0'])
    S.op('act', lambda h: h.activation(out=SM[:, 2:4], in_=SM[:, 0:2], func=AF.Exp), reads=['SM0'], writes=['SM1'])
    S.op('act', lambda h: h.activation(out=SM[:, 8:12], in_=LP[:, 0:4], func=AF.Exp), reads=['CONST'], writes=['ESINK'])
    S.op('dve', lambda h: h.tensor_tensor(out=SM[:, 4:5], in0=SM[:, 2:3], in1=SM[:, 3:4], op=ALU.subtract), reads=['SM1'], writes=['SM2'])
    S.op('dve', lambda h: h.tensor_tensor(out=SM[:, 4:5], in0=SM[:, 4:5], in1=LP[:, LP_LINIT:LP_LINIT + 1], op=ALU.add), reads=['SM2', 'CONST'], writes=['SM2'])
    S.op('dve', lambda h: h.tensor_scalar_mul(out=SM[:, 5:6], in0=SM[:, 4:5], scalar1=-1.0), reads=['SM2'], writes=['NEGLAM'])
    S.op('dve', lambda h: h.tensor_tensor(out=SM[:, 6:7], in0=LP[:, LP_SUBG:LP_SUBG + 1], in1=LP[:, LP_1MLINIT:LP_1MLINIT + 1], op=ALU.mult),
         reads=['CONST'], writes=['GSUB'])

    def po_next():
        i = cnt['po'] % 3
        cnt['po'] += 1
        return PO[i], ('po', i)

    def recip_den(po, pok, sinkcol=None):
        i = cnt['rc'] % 2
        cnt['rc'] += 1
        rc, rck = RC[i], ('rc', i)
        src, srck = po, pok
        if sinkcol is not None:
            S.op('dve', lambda h: h.tensor_scalar_add(out=DN[64:96, :], in0=po[64:96, :], scalar1=SM[64:96, sinkcol:sinkcol + 1]),
                 reads=[pok, 'ESINK'], writes=['DN'])
            src, srck = DN, 'DN'
        S.op('dve', lambda h: h.reciprocal(out=rc[0:32, :], in_=src[64:96, :]), reads=[srck], writes=[rck])
        S.op('dve', lambda h: h.reciprocal(out=rc[32:64, :], in_=src[64:96, :]), reads=[srck, rck], writes=[rck])
        return rc, rck

    def store_out(ob, obk, row0, g):
        S.op('pool', lambda h: h.dma_start(out=mixT[row0:row0 + 64, g * 512:(g + 1) * 512], in_=ob[0:64, :]),
             reads=[obk], dma="mx" + str(obk[1]))

    def epi_plain(po, pok, row0, g, sinkcol=None):
        rc, rck = recip_den(po, pok, sinkcol)
        i = cnt['outb'] % 2
        cnt['outb'] += 1
        ob, obk = OUTB[i], ('outb', i)
        S.op('dve', lambda h: h.tensor_tensor(out=ob[0:64, :], in0=po[0:64, :], in1=rc[0:64, :], op=ALU.mult),
             reads=[pok, rck], writes=[obk])
        store_out(ob, obk, row0, g)

    def epi_d_first(po, pok):
        rc, rck = recip_den(po, pok)
        S.op('dve', lambda h: h.tensor_tensor(out=A1[0:64, :], in0=po[0:64, :], in1=rc[0:64, :], op=ALU.mult),
             reads=[pok, rck], writes=['A1'])

    def epi_d_second(po, pok):
        rc, rck = recip_den(po, pok)
        S.op('dve', lambda h: h.tensor_tensor(out=A2[0:64, :], in0=po[0:64, :], in1=rc[0:64, :], op=ALU.mult),
             reads=[pok, rck], writes=['A2'])
        S.op('dve', lambda h: h.scalar_tensor_tensor(out=O1[0:64, :], in0=A2[0:64, :], scalar=SM[0:64, 5:6], in1=A1[0:64, :],
                                                      op0=ALU.mult, op1=ALU.add), reads=['A1', 'A2', 'NEGLAM'], writes=['O1'])
        S.op('pool', lambda h: h.tensor_tensor(out=SQd[0:64, :], in0=O1[0:64, :], in1=O1[0:64, :], op=ALU.mult), reads=['O1'], writes=['SQd'])
        S.op('pool', lambda h: h.tensor_copy(out=SQHd[0:64, :], in_=SQd[0:64, :]), reads=['SQd'], writes=['SQHd'])
        S.op('pool', lambda h: h.tensor_tensor(out=SQLd[0:64, :], in0=SQd[0:64, :], in1=SQHd[0:64, :], op=ALU.subtract),
             reads=['SQd', 'SQHd'], writes=['SQLd'])

    def epi_d_final(row0, g):
        S.op('pe', lambda h: h.matmul(PBC[0:64, :], lhsT=ONESB[0:64, 0:64], rhs=SQHd[0:64, :], start=True, stop=False),
             reads=['ONESB', 'SQHd'], writes=['pbc'])
        S.op('pe', lambda h: h.matmul(PBC[0:64, :], lhsT=ONESB[0:64, 0:64], rhs=SQLd[0:64, :], start=False, stop=True),
             reads=['ONESB', 'SQLd'], writes=['pbc'])
        S.op('act', lambda h: h.activation(out=Rr[0:64, :], in_=PBC[0:64, :], func=AF.Ln, bias=EPS, scale=1.0 / 64),
             reads=['pbc'], writes=['Rr'])
        S.op('act', lambda h: h.activation(out=Rr[0:64, :], in_=Rr[0:64, :], func=AF.Exp, scale=-0.5), reads=['Rr'], writes=['Rr'])
        i = cnt['outb'] % 2
        cnt['outb'] += 1
        ob, obk = OUTB[i], ('outb', i)
        S.op('dve', lambda h: h.scalar_tensor_tensor(out=ob[0:64, :], in0=O1[0:64, :], scalar=SM[0:64, 6:7], in1=Rr[0:64, :],
                                                      op0=ALU.mult, op1=ALU.mult), reads=['O1', 'GSUB', 'Rr'], writes=[obk])
        store_out(ob, obk, row0, g)

    def vload(slot, c0):
        return [lambda h, j=j: h.dma_start(out=VS[slot][:, j * 32:(j + 1) * 32, 0:64],
                                           in_=Vall[j][:, c0:c0 + 64].rearrange("(t p) c -> p t c", p=128)) for j in range(4)]

    def issue(fns, keys, sem):
        S.op('sp', lambda h: [f(h) for f in fns], writes=keys, dma=sem, ndma=len(fns))

    def load_B(hh, slot):
        ks = KS[slot]
        fns = []
        for j in range(4):
            fns.append(lambda h, j=j: h.dma_start(out=ks[0:64, j * TOK:(j + 1) * TOK], in_=KTall[j][KROW_BN + hh * 64:KROW_BN + (hh + 1) * 64, :]))
            fns.append(lambda h, j=j: h.dma_start(out=ks[64:96, j * TOK:(j + 1) * TOK], in_=KTall[j][KROW_BR:KROW_BR + 32, :]))
        issue(fns, [('K', slot)], f"k{slot}")
        issue(vload(slot, VCOL_B + hh * 64), [('V', slot)], f"v{slot}")
        issue([lambda h: h.dma_start(out=QS[slot][0:96, :], in_=QT[QROW_B + hh * 96:QROW_B + (hh + 1) * 96, :])], [('Q', slot)], f"q{slot}")

    def load_C(kv, slot):
        ks = KS[slot]
        fns = []
        for b in (0, 64):
            for j in range(4):
                fns.append(lambda h, j=j, b=b: h.dma_start(out=ks[b:b + 64, j * TOK:(j + 1) * TOK],
                                                            in_=KTall[j][KROW_C + kv * 64:KROW_C + (kv + 1) * 64, :]))
        issue(fns, [('K', slot)], f"k{slot}")
        issue(vload(slot, VCOL_C + kv * 64), [('V', slot)], f"v{slot}")
        issue([lambda h: h.dma_start(out=QS[slot][0:128, :], in_=QT[QROW_C + kv * 128:QROW_C + (kv + 1) * 128, :])], [('Q', slot)], f"q{slot}")

    def load_D(hh, slot):
        ks, qs = KS[slot], QS[slot]
        fns, qf = [], []
        for m in range(2):
            r0 = KROW_D + (hh * 2 + m) * 32
            for j in range(4):
                fns.append(lambda h, j=j, m=m, r0=r0: h.dma_start(out=ks[m * 64:m * 64 + 32, j * TOK:(j + 1) * TOK], in_=KTall[j][r0:r0 + 32, :]))
            fns.append(lambda h, m=m, r0=r0: h.dma_start(out=ks[m * 64:m * 64 + 32, 4 * TOK:5 * TOK], in_=KTall[0][r0:r0 + 32, :]))
            fns.append(lambda h, m=m: h.dma_start(out=ks[m * 64 + 32:m * 64 + 36, :], in_=kaug[hh * 4:(hh + 1) * 4, :]))
            q0 = QROW_D + (hh * 2 + m) * 32
            qf.append(lambda h, m=m, q0=q0: h.dma_start(out=qs[m * 64:m * 64 + 32, :], in_=QT[q0:q0 + 32, :]))
            qf.append(lambda h, m=m: h.dma_start(out=qs[m * 64 + 32:m * 64 + 36, :], in_=qaug[hh * 4:(hh + 1) * 4, :]))
        issue(fns, [('K', slot)], f"k{slot}")
        issue(vload(slot, VCOL_D + hh * 64), [('V', slot)], f"v{slot}")
        issue(qf, [('Q', slot)], f"q{slot}")

    def load_A(kv, slot):
        r0 = KROW_A + kv * 64
        c0 = VCOL_A + kv * 64
        KA, VA = KS[slot], VS[slot]
        fns = []
        for b in (0, 64):
            fns.append(lambda h, b=b: h.dma_start(out=KA[b:b + 64, 128:128 + TOK], in_=KTall[0][r0:r0 + 64, :]))
            fns.append(lambda h, b=b: h.dma_start(out=KA[b:b + 64, 0:128], in_=KTall[3][r0:r0 + 64, TOK - 128:TOK]))
            fns.append(lambda h, b=b: h.dma_start(out=KA[b:b + 64, 128 + TOK:256 + TOK], in_=KTall[1][r0:r0 + 64, 0:128]))
        issue(fns, [('K', slot)], f"k{slot}")
        vf = [lambda h: h.dma_start(out=VA[:, 1:33, 0:64], in_=Vall[0][:, c0:c0 + 64].rearrange("(t p) c -> p t c", p=128)),
              lambda h: h.dma_start(out=VA[:, 0, 0:64], in_=Vall[3][TOK - 128:TOK, c0:c0 + 64]),
              lambda h: h.dma_start(out=VA[:, 33, 0:64], in_=Vall[1][0:128, c0:c0 + 64])]
        issue(vf, [('V', slot)], f"v{slot}")
        issue([lambda h: h.dma_start(out=QS[slot][0:128, :], in_=QT[QROW_A + kv * 128:QROW_A + (kv + 1) * 128, :])], [('Q', slot)], f"q{slot}")

    groups = list(range(NG)) if AT_GROUPS is None else list(AT_GROUPS)

    def dense_steps(slot, maps, dtype_d=None):
        ks, vs, qs = KS[slot], VS[slot], QS[slot]
        rk = [('K', slot), ('Q', slot)]
        rv = [('V', slot), ('Vones', slot)]
        steps = []
        for g in groups:
            for mi, mp in enumerate(maps):
                po, pok = po_next()
                b, dk = mp['b'], mp['dk']
                for t0 in range(0, 128, 2):
                    tiles = []
                    for t in (t0, t0 + 1):
                        kind, col = 'plain', t * 128
                        if dtype_d is not None and t < 32:
                            if 4 * g <= t < 4 * g + 4:
                                kind = 'diag'
                            elif t >= 4 * g + 4:
                                col = 4 * TOK + t * 128
                        tiles.append((t, kind, col))
                    st = {}

                    def s_fn(ps, psk, tiles=tiles, b=b, dk=dk, g=g, hd=dtype_d):
                        for u, (t, kind, col) in enumerate(tiles):
                            pu = ps[:, u * 512:(u + 1) * 512]
                            if kind == 'plain':
                                S.op('pe', lambda h, pu=pu, col=col: h.matmul(pu, lhsT=ks[b:b + dk, col:col + 128],
                                                                              rhs=qs[b:b + dk, g * 512:(g + 1) * 512],
                                                                              start=True, stop=True), reads=rk, writes=[psk])
                            else:
                                j = t - 4 * g
                                S.op('pe', lambda h, pu=pu, col=col: h.matmul(pu, lhsT=ks[b:b + 32, col:col + 128],
                                                                              rhs=qs[b:b + 32, g * 512:(g + 1) * 512],
                                                                              start=True, stop=False), reads=rk, writes=[psk])
                                S.op('pe', lambda h, pu=pu, j=j: h.matmul(pu, lhsT=NSI[:, 4 + hd, :], rhs=DD[:, j, 0, :],
                                                                          start=False, stop=False), reads=['CONST'], writes=[psk])
                                S.op('pe', lambda h, pu=pu, j=j: h.matmul(pu, lhsT=NSI[:, 4 + hd, :], rhs=DD[:, j, 1, :],
                                                                          start=False, stop=True), reads=['CONST'], writes=[psk])
                    st['s'] = s_fn

                    def e_fn(ps, psk, pt, ptk):
                        S.op('act', lambda h: h.activation(out=pt[:, :], in_=ps[:, :], func=AF.Exp), reads=[psk], writes=[ptk])
                    st['e'] = e_fn

                    def v_fn(pt, ptk, t0=t0, po=po, pok=pok):
                        for u in range(2):
                            t = t0 + u
                            S.op('pe', lambda h, t=t, u=u: h.matmul(po[0:96, :], lhsT=vs[:, t, :], rhs=pt[:, u * 512:(u + 1) * 512],
                                                                   start=(t == 0), stop=(t == 127)),
                                 reads=rv + [ptk], writes=[pok])
                    st['v'] = v_fn
                    if t0 == 126:
                        st['post'] = (lambda po=po, pok=pok, mp=mp, g=g: mp['epi'](po, pok, g))
                        if mp.get('final') is not None:
                            st['defer'] = (lambda mp=mp, g=g: mp['final'](g))
                    steps.append(st)
        return steps

    def window_steps(kv, slot):
        qs = QS[slot]
        KA, VA = KS[slot], VS[slot]
        steps = []
        for hl in range(2):
            hd = kv * 2 + hl
            b = hl * 64
            po = pok = None
            for i0 in range(0, NTT, 2):
                if AT_GROUPS is not None and (i0 // 4) not in AT_GROUPS:
                    continue
                if i0 % 4 == 0:
                    po, pok = po_next()
                st = {}

                def s_fn(ps, psk, i0=i0, b=b, hd=hd):
                    for u in range(2):
                        i = i0 + u
                        for jj in range(3):
                            mi = jj
                            if i == 0 and jj == 0:
                                mi = 3
                            if i == NTT - 1 and jj == 2:
                                mi = 4
                            pu = ps[:, u * 512 + jj * 128:u * 512 + (jj + 1) * 128]
                            S.op('pe', lambda h, i=i, jj=jj, pu=pu: h.matmul(pu, lhsT=KA[b:b + 64, (i + jj) * 128:(i + jj + 1) * 128],
                                                                             rhs=qs[b:b + 64, i * 128:(i + 1) * 128], start=True, stop=False),
                                 reads=[('K', slot), ('Q', slot)], writes=[psk])
                            S.op('pe', lambda h, jj=jj, pu=pu: h.matmul(pu, lhsT=NSI[:, hd, :], rhs=WDIST[:, jj, :], start=False, stop=False),
                                 reads=['CONST'], writes=[psk])
                            S.op('pe', lambda h, mi=mi, pu=pu: h.matmul(pu, lhsT=IDB[:], rhs=WMASK[:, mi, :], start=False, stop=True),
                                 reads=['CONST', 'identa'], writes=[psk])
                st['s'] = s_fn

                def e_fn(ps, psk, pt, ptk):
                    S.op('act', lambda h: h.activation(out=pt.rearrange("p (a b) -> p a b", b=512)[:, :, 0:384],
                                                       in_=ps.rearrange("p (a b) -> p a b", b=512)[:, :, 0:384], func=AF.Exp),
                         reads=[psk], writes=[ptk])
                st['e'] = e_fn

                def v_fn(pt, ptk, i0=i0, po=po, pok=pok):
                    for u in range(2):
                        i = i0 + u
                        for jj in range(3):
                            S.op('pe', lambda h, i=i, u=u, jj=jj: h.matmul(po[0:96, (i % 4) * 128:(i % 4 + 1) * 128], lhsT=VA[:, i + jj, :],
                                                                          rhs=pt[:, u * 512 + jj * 128:u * 512 + (jj + 1) * 128],
                                                                          start=(jj == 0), stop=(jj == 2)),
                                 reads=[('V', slot), ('Vones', slot), ptk], writes=[pok])
                st['v'] = v_fn
                if i0 % 4 == 2:
                    st['post'] = (lambda po=po, pok=pok, hd=hd, g=i0 // 4: epi_plain(po, pok, hd * 64, g, sinkcol=8 + hd))
                steps.append(st)
        return steps

    units = []
    for kv in range(2):
        units.append(('A%d' % kv, lambda slot, kv=kv: load_A(kv, slot), lambda slot, kv=kv: window_steps(kv, slot)))
    for hh in range(4):
        units.append(('B%d' % hh, lambda slot, hh=hh: load_B(hh, slot),
                      lambda slot, hh=hh: dense_steps(slot, [dict(b=0, dk=96, epi=lambda po, pok, g, hh=hh: epi_plain(po, pok, 256 + hh * 64, g))])))
    for kv in range(2):
        units.append(('C%d' % kv, lambda slot, kv=kv: load_C(kv, slot),
                      lambda slot, kv=kv: dense_steps(slot, [
                          dict(b=hl * 64, dk=64, epi=lambda po, pok, g, hd=kv * 2 + hl: epi_plain(po, pok, 512 + hd * 64, g))
                          for hl in range(2)])))
    for hh in range(4):
        units.append(('D%d' % hh, lambda slot, hh=hh: load_D(hh, slot),
                      lambda slot, hh=hh: dense_steps(slot, [
                          dict(b=0, dk=36, epi=lambda po, pok, g: epi_d_first(po, pok)),
                          dict(b=64, dk=36, epi=lambda po, pok, g: epi_d_second(po, pok),
                               final=lambda g, hh=hh: epi_d_final(768 + hh * 64, g))], dtype_d=hh)))
    if AT_UNITS is not None:
        units = [u for u in units if u[0] in AT_UNITS]

    LOOK = 1
    DEFER = 3
    units[0][1](0)
    allsteps = []
    for ui, (name, loader, gen) in enumerate(units):
        steps = gen(ui % 2)
        if ui + 1 < len(units):
            steps[min(LOOK + 1, len(steps) - 1)]['pre'] = (lambda nxt=units[ui + 1][1], slot=(ui + 1) % 2: nxt(slot))
        allsteps += steps
    n = len(allsteps)
    deferred = []
    for i in range(n + LOOK):
        if i < n:
            st = allsteps[i]
            b, b3 = i % 2, i % 3
            if 'pre' in st:
                st['pre']()
            st['s'](PSS[b], ('pss', b))
            st['e'](PSS[b], ('pss', b), PT[b3], ('pt', b3))
        if i >= LOOK:
            st = allsteps[i - LOOK]
            b = (i - LOOK) % 3
            st['v'](PT[b], ('pt', b))
            if 'post' in st:
                st['post']()
            if 'defer' in st:
                deferred.append((i + DEFER, st['defer']))
        while deferred and deferred[0][0] <= i:
            deferred.pop(0)[1]()
    while deferred:
        deferred.pop(0)[1]()


def phase_o(C, xin, mixT, w_out, lng, lnb, xout, tag="o"):
    nc, S = C.nc, C.S
    P = {}
    WOUT = C.sb("WOUT" + tag, [128, 8, 1024], BF16)
    WST = [C.sb(f"WSTo{i}" + tag, [128, 1024], F32) for i in range(2)]
    MIX = [C.sb(f"MIX{i}" + tag, [128, 8, 512], BF16) for i in range(2)]
    P['XR'] = [C.sb(f"XR{i}" + tag, [128, 4, 1024], F32) for i in range(2)]
    P['G'] = C.sb("G" + tag, [128, 1, 1024], F32)
    P['B'] = C.sb("B" + tag, [128, 1, 1024], F32)
    P['ot'] = [C.sb(f"ot{i}" + tag, [128, 1024], F32) for i in range(2)]
    P['st'] = C.sb("st" + tag, [128, 4, 12], F32)
    P['mv'] = C.sb("mv" + tag, [128, 4, 2], F32)
    P['rstd'] = C.sb("rstd" + tag, [128, 4], F32)
    PY = [C.ps(f"py{i}" + tag, [128, 512], F32) for i in range(4)]
    S.op('sp', lambda h: [h.dma_start(out=P['G'][:, 0, :], in_=lng.partition_broadcast(128)),
                          h.dma_start(out=P['B'][:, 0, :], in_=lnb.partition_broadcast(128))],
         writes=['GB'], dma="gb", ndma=2)
    for k in range(8):
        wst = WST[k % 2]
        S.op('sp', lambda h, k=k, wst=wst: h.dma_start(out=wst[:], in_=w_out[k * 128:(k + 1) * 128, :]),
             writes=[('wst', k % 2)], dma=f"wst{k % 2}")
        S.op('dve', lambda h, k=k, wst=wst: h.tensor_copy(out=WOUT[:, k, :], in_=wst[:]),
             reads=[('wst', k % 2)], writes=['WOUT'])
    mix_r = mixT.rearrange("(k p) t -> p k t", p=128)
    n = 0
    for g in range(NG):
        XR = P['XR'][g % 2]
        rk = ('XR', g % 2)
        mx = MIX[g % 2]
        S.op('sp', lambda h, g=g, mx=mx: h.dma_start(out=mx[:], in_=mix_r[:, :, g * 512:(g + 1) * 512]),
             writes=[('mix', g % 2)], dma=f"mix{g % 2}")
        for t in range(4):
            tt = g * 4 + t
            S.op('sp', lambda h, t=t, tt=tt, XR=XR: h.dma_start(out=XR[:, t, :], in_=xin[tt * 128:(tt + 1) * 128, :]),
                 writes=[rk + (t,)], dma=f"xr{g % 2}{t}")
            S.op('act', lambda h, t=t, XR=XR: h.mul(out=XR[:, t, :], in_=XR[:, t, :], mul=ALPHA),
                 reads=[rk + (t,)], writes=[rk + (t,)])
            for hh in range(2):
                py = PY[n % 4]
                pk = ('py', n % 4)
                n += 1
                for k in range(8):
                    S.op('pe', lambda h, k=k, t=t, hh=hh, py=py, mx=mx: h.matmul(
                        py[:], lhsT=mx[:, k, t * 128:(t + 1) * 128], rhs=WOUT[:, k, hh * 512:(hh + 1) * 512],
                        start=(k == 0), stop=(k == 7)), reads=[('mix', g % 2), 'WOUT'], writes=[pk])
                S.op('dve', lambda h, t=t, hh=hh, py=py, XR=XR: h.tensor_tensor(
                    out=XR[:, t, hh * 512:(hh + 1) * 512], in0=py[:], in1=XR[:, t, hh * 512:(hh + 1) * 512], op=ALU.add),
                    reads=[pk, rk + (t,)], writes=[rk + (t,)])
        ln_tiles(C, P, g, xout, 0)


def _inv_freq(dim):
    return np.power(np.float32(10000.0), -(np.arange(0, dim, 2, dtype=np.float32) / np.float32(dim))).astype(np.float32)


def rope_tables(rank):
    pos = (rank * TOK + np.arange(TOK)).astype(np.int64)
    inv = _inv_freq(32)

    def cs(p):
        ang = (p.astype(np.float32)[:, None] * inv[None, :]).astype(np.float32).astype(np.float64)
        return np.cos(ang).T, np.sin(ang).T

    c_t, s_t = cs(pos)
    tbc = np.concatenate([c_t, c_t], 0).astype(np.float32)
    tbs = np.concatenate([-s_t, s_t], 0).astype(np.float32)
    c_r, s_r = cs(pos // 64)
    c_c, s_c = cs(pos % 64)
    c64 = np.concatenate([c_r, c_r, c_c, c_c], 0)
    s64 = np.concatenate([-s_r, s_r, -s_c, s_c], 0)
    tcc = np.concatenate([c64, c64], 0).astype(np.float32)
    tcs = np.concatenate([s64, s64], 0).astype(np.float32)
    return dict(tcc=np.ascontiguousarray(tcc), tcs=np.ascontiguousarray(tcs),
                tbc=np.ascontiguousarray(tbc), tbs=np.ascontiguousarray(tbs))


def _perm64(v):
    v = np.asarray(v)
    return np.concatenate([v[16:32], v[0:16], v[48:64], v[32:48]])


def par_table(mla_q_norm, mla_kv_norm, ax_q_norm, ax_k_norm):
    t = np.zeros((128, NPAR), np.float32)
    t[:, PAR_GQ0] = mla_q_norm[0:128]
    t[:, PAR_GQ1] = mla_q_norm[128:256]
    t[:, PAR_GKV] = mla_kv_norm
    t[:, PAR_AXQ] = np.concatenate([ax_q_norm, ax_q_norm])
    t[:, PAR_AXQP] = np.concatenate([_perm64(ax_q_norm), _perm64(ax_q_norm)])
    t[:, PAR_AXK] = np.concatenate([ax_k_norm, ax_k_norm])
    t[:, PAR_AXKP] = np.concatenate([_perm64(ax_k_norm), _perm64(ax_k_norm)])
    return t


def _bf(a):
    return np.ascontiguousarray(np.asarray(a, np.float32).astype(ml_dtypes.bfloat16))


def at_tables(rank):
    slopes = 2.0 ** (-(np.arange(8) + 1.0))
    t = np.arange(TOK)
    kaug = np.zeros((4, 4, KCOLS), np.float32)
    qaug = np.zeros((4, 4, TOK), np.float32)
    q_abs = rank * TOK + t
    for hd in range(4):
        s_ = slopes[4 + hd]
        qaug[hd, 0] = s_ * 128 * (q_abs // 128)
        qaug[hd, 1] = s_ * (q_abs % 128)
        qaug[hd, 2] = 1.0
        qaug[hd, 3] = 1.0
        for j in range(5):
            if j == 4:
                rj, sg = rank, -1.0
            else:
                rj = (rank + j) % 4
                sg = 1.0 if (j == 0 or rj < rank) else -1.0
            k_abs = rj * TOK + t
            sl = slice(j * TOK, (j + 1) * TOK)
            kaug[hd, 0, sl] = -sg
            kaug[hd, 1, sl] = -sg
            kaug[hd, 2, sl] = sg * s_ * 128 * (k_abs // 128)
            kaug[hd, 3, sl] = sg * s_ * (k_abs % 128)
    kr = np.arange(128)[:, None]
    qr = np.arange(512)[None, :]
    dd = np.zeros((128, 4, 2, 512), np.float32)
    for j in range(4):
        d = np.abs(qr - 128 * j - kr)
        dd[:, j, 0, :] = 16 * (d // 16)
        dd[:, j, 1, :] = d % 16
    nsi = np.zeros((128, 8, 128), np.float32)
    for i in range(8):
        nsi[:, i, :] = -slopes[i] * np.eye(128)
    q1 = np.arange(128)[None, :]
    wdist = np.zeros((128, 3, 128), np.float32)
    wmask = np.zeros((128, 5, 128), np.float32)
    for jj in range(3):
        d = np.abs(q1 - kr + 128 * (1 - jj))
        wdist[:, jj, :] = d
        wmask[:, jj, :] = np.where(d <= 128, 0.0, -1e30)
    wmask[:, 3, :] = wmask[:, 0, :] if rank > 0 else -1e30
    wmask[:, 4, :] = wmask[:, 2, :] if rank < 3 else -1e30
    return dict(kaug=_bf(kaug.reshape(16, KCOLS)), qaug=_bf(qaug.reshape(16, TOK)), ddiag=_bf(dd.reshape(128, -1)),
                nsi=_bf(nsi.reshape(128, -1)), wdist=_bf(wdist.reshape(128, -1)), wmask=_bf(wmask.reshape(128, -1)))


def lpar_table(sink, diff_lambda, diff_subln, l):
    t = np.zeros((128, NLP), np.float32)
    t[:, LP_SINK:LP_SINK + 4] = np.asarray(sink, np.float32)[None, :]
    t[:, LP_LAM:LP_LAM + 128] = np.asarray(diff_lambda, np.float32).reshape(1, 128)
    lam_init = 0.8 - 0.6 * math.exp(-0.3 * l)
    t[:, LP_LINIT] = lam_init
    t[:, LP_1MLINIT] = 1.0 - lam_init
    t[:, LP_SUBG] = np.concatenate([diff_subln, diff_subln])
    return t


def build_at_program():
    if 'at' in _CACHE:
        return _CACHE['at']
    C = Ctx()
    nc = C.nc
    ei = lambda n, sh, dt=BF16: nc.dram_tensor(n, list(sh), dt, kind="ExternalInput").ap()
    QT = ei("QT", [RQ, TOK])
    KTall = ei("KTall", [4, RK, TOK])
    Vall = ei("Vall", [4, TOK, RV])
    kaug, qaug = ei("kaug", [16, KCOLS]), ei("qaug", [16, TOK])
    ddiag, nsi = ei("ddiag", [128, 4 * 2 * 512]), ei("nsi", [128, 8 * 128])
    wdist, wmask = ei("wdist", [128, 3 * 128]), ei("wmask", [128, 5 * 128])
    lpar = ei("lpar", [128, NLP], F32)
    mixT = nc.dram_tensor("mixT", [D_MODEL, TOK], BF16, kind="ExternalOutput").ap()
    phase_at(C, QT, KTall, Vall, kaug, qaug, ddiag, nsi, wdist, wmask, lpar, mixT)
    C.finish()
    _CACHE['at'] = nc
    return nc


def run_at(pr_outs, inp, l):
    nc = build_at_program()
    lp = lpar_table(inp['win_sink'][l], inp['diff_lambda'][l], inp['diff_subln'][l], l)
    in_maps = []
    for c in range(8):
        b, r = c // 4, c % 4
        order = [b * 4 + (r + j) % 4 for j in range(4)]
        m = dict(QT=np.ascontiguousarray(pr_outs[c][0]),
                 KTall=np.ascontiguousarray(np.stack([pr_outs[o][1] for o in order])),
                 Vall=np.ascontiguousarray(np.stack([pr_outs[o][2] for o in order])), lpar=lp)
        m.update(at_tables(r))
        in_maps.append(m)
    res = run_bass_kernel_spmd(nc, in_maps, core_ids=list(range(8)))
    return [r["mixT"] for r in res.results]


def build_pr_program():
    if 'pr' in _CACHE:
        return _CACHE['pr']
    C = Ctx()
    nc = C.nc
    ei = lambda n, sh, dt=F32: nc.dram_tensor(n, list(sh), dt, kind="ExternalInput").ap()
    eo = lambda n, sh, dt=BF16: nc.dram_tensor(n, list(sh), dt, kind="ExternalOutput").ap()
    xin = ei("xin", [TOK, D_MODEL])
    w_in = ei("w_in", [D_MODEL, N_IN])
    w_uq = ei("w_uq", [256, 384])
    w_ukv = ei("w_ukv", [128, 512])
    par = ei("par", [128, NPAR])
    tcc, tcs = ei("tcc", [128, TOK]), ei("tcs", [128, TOK])
    tbc, tbs = ei("tbc", [32, TOK]), ei("tbs", [32, TOK])
    QT, KT, V = eo("QT", [RQ, TOK]), eo("KT", [RK, TOK]), eo("V", [TOK, RV])
    phase_pr(C, xin, w_in, w_uq, w_ukv, par, tcc, tcs, tbc, tbs, QT, KT, V)
    C.finish()
    _CACHE['pr'] = nc
    return nc


def run_pr(xs, inp, l):
    nc = build_pr_program()
    par = par_table(inp['mla_q_norm'][l], inp['mla_kv_norm'][l], inp['ax_q_norm'][l], inp['ax_k_norm'][l])
    in_maps = []
    for c in range(8):
        m = dict(xin=xs[c], w_in=np.ascontiguousarray(inp['w_in'][l]), w_uq=np.ascontiguousarray(inp['mla_w_uq'][l]),
                 w_ukv=np.ascontiguousarray(inp['mla_w_ukv'][l]), par=par)
        m.update(rope_tables(c % 4))
        in_maps.append(m)
    res = run_bass_kernel_spmd(nc, in_maps, core_ids=list(range(8)))
    return [(r["QT"], r["KT"], r["V"]) for r in res.results]


def build_ffn_program():
    if 'ffn' in _CACHE:
        return _CACHE['ffn']
    C = Ctx()
    nc = C.nc
    xin = nc.dram_tensor("xin", [TOK, D_MODEL], F32, kind="ExternalInput").ap()
    wgu = nc.dram_tensor("wgu", [D_MODEL, 2 * D_FF], F32, kind="ExternalInput").ap()
    wd = nc.dram_tensor("wd", [D_FF, D_MODEL], F32, kind="ExternalInput").ap()
    lng = nc.dram_tensor("lng", [D_MODEL], F32, kind="ExternalInput").ap()
    lnb = nc.dram_tensor("lnb", [D_MODEL], F32, kind="ExternalInput").ap()
    xout = nc.dram_tensor("xout", [TOK, D_MODEL], F32, kind="ExternalOutput").ap()
    phase_ffn(C, xin, xout, wgu, wd, lng, lnb)
    C.finish()
    _CACHE['ffn'] = nc
    return nc


def run_ffn(xs, wgu, wd, g, b):
    nc = build_ffn_program()
    in_maps = [dict(xin=xs[c], wgu=wgu, wd=wd, lng=g, lnb=b) for c in range(8)]
    res = run_bass_kernel_spmd(nc, in_maps, core_ids=list(range(8)))
    return [r["xout"] for r in res.results]


def build_o_program():
    if 'o' in _CACHE:
        return _CACHE['o']
    C = Ctx()
    nc = C.nc
    xin = nc.dram_tensor("xin", [TOK, D_MODEL], F32, kind="ExternalInput").ap()
    mixT = nc.dram_tensor("mixT", [D_MODEL, TOK], BF16, kind="ExternalInput").ap()
    w_out = nc.dram_tensor("w_out", [D_MODEL, D_MODEL], F32, kind="ExternalInput").ap()
    lng = nc.dram_tensor("lng", [D_MODEL], F32, kind="ExternalInput").ap()
    lnb = nc.dram_tensor("lnb", [D_MODEL], F32, kind="ExternalInput").ap()
    xout = nc.dram_tensor("xout", [TOK, D_MODEL], F32, kind="ExternalOutput").ap()
    phase_o(C, xin, mixT, w_out, lng, lnb, xout)
    C.finish()
    _CACHE['o'] = nc
    return nc


def run_o(xs, mixs, w_out, g, b):
    nc = build_o_program()
    in_maps = [dict(xin=xs[c], mixT=np.ascontiguousarray(mixs[c]), w_out=w_out, lng=g, lnb=b) for c in range(8)]
    res = run_bass_kernel_spmd(nc, in_maps, core_ids=list(range(8)))
    return [r["xout"] for r in res.results]


def _decl_inputs(nc, with_at, with_o_f2, with_f1_pr):
    ei = lambda n, sh, dt=F32: nc.dram_tensor(n, list(sh), dt, kind="ExternalInput").ap()
    T = {}
    T['x_in'] = ei("x_in", [TOK, D_MODEL])
    if with_at:
        T['QT'] = ei("QT", [RQ, TOK], BF16)
        T['KTall'] = ei("KTall", [4, RK, TOK], BF16)
        T['Vall'] = ei("Vall", [4, TOK, RV], BF16)
        T['kaug'], T['qaug'] = ei("kaug", [16, KCOLS], BF16), ei("qaug", [16, TOK], BF16)
        T['ddiag'], T['nsi'] = ei("ddiag", [128, 4 * 2 * 512], BF16), ei("nsi", [128, 8 * 128], BF16)
        T['wdist'], T['wmask'] = ei("wdist", [128, 3 * 128], BF16), ei("wmask", [128, 5 * 128], BF16)
        T['lpar'] = ei("lpar", [128, NLP])
    if with_o_f2:
        T['o_w'] = ei("o_w", [D_MODEL, D_MODEL])
        T['o_g'], T['o_b'] = ei("o_g", [D_MODEL]), ei("o_b", [D_MODEL])
        T['f2_wgu'], T['f2_wd'] = ei("f2_wgu", [D_MODEL, 2 * D_FF]), ei("f2_wd", [D_FF, D_MODEL])
        T['f2_g'], T['f2_b'] = ei("f2_g", [D_MODEL]), ei("f2_b", [D_MODEL])
    if with_f1_pr:
        T['f1_wgu'], T['f1_wd'] = ei("f1_wgu", [D_MODEL, 2 * D_FF]), ei("f1_wd", [D_FF, D_MODEL])
        T['f1_g'], T['f1_b'] = ei("f1_g", [D_MODEL]), ei("f1_b", [D_MODEL])
        T['w_in'] = ei("w_in", [D_MODEL, N_IN])
        T['w_uq'], T['w_ukv'] = ei("w_uq", [256, 384]), ei("w_ukv", [128, 512])
        T['par'] = ei("par", [128, NPAR])
        T['tcc'], T['tcs'] = ei("tcc", [128, TOK]), ei("tcs", [128, TOK])
        T['tbc'], T['tbs'] = ei("tbc", [32, TOK]), ei("tbs", [32, TOK])
    return T


def build_program(kind):
    if kind in _CACHE:
        return _CACHE[kind]
    C = Ctx()
    nc = C.nc
    with_at = kind in ('mid', 'last')
    with_f1 = kind in ('first', 'mid')
    T = _decl_inputs(nc, with_at, with_at, with_f1)
    eo = lambda n, sh, dt: nc.dram_tensor(n, list(sh), dt, kind="ExternalOutput").ap()
    x_cur = T['x_in']
    first = True
    if with_at:
        mixT = C.dram("mixT_i", [D_MODEL, TOK], BF16)
        phase_at(C, T['QT'], T['KTall'], T['Vall'], T['kaug'], T['qaug'], T['ddiag'], T['nsi'], T['wdist'], T['wmask'],
                 T['lpar'], mixT)
        C.new_phase()
        x2 = C.dram("x2_i", [TOK, D_MODEL], F32)
        phase_o(C, x_cur, mixT, T['o_w'], T['o_g'], T['o_b'], x2)
        C.new_phase()
        x3 = eo("x_out", [TOK, D_MODEL], F32) if kind == 'last' else C.dram("x3_i", [TOK, D_MODEL], F32)
        phase_ffn(C, x2, x3, T['f2_wgu'], T['f2_wd'], T['f2_g'], T['f2_b'], tag="f2")
        x_cur = x3
        first = False
    if with_f1:
        if not first:
            C.new_phase()
        x1 = eo("x_out", [TOK, D_MODEL], F32)
        phase_ffn(C, x_cur, x1, T['f1_wgu'], T['f1_wd'], T['f1_g'], T['f1_b'], tag="f1")
        C.new_phase()
        QT, KT, V = eo("QTo", [RQ, TOK], BF16), eo("KTo", [RK, TOK], BF16), eo("Vo", [TOK, RV], BF16)
        phase_pr(C, x1, T['w_in'], T['w_uq'], T['w_ukv'], T['par'], T['tcc'], T['tcs'], T['tbc'], T['tbs'], QT, KT, V)
    C.finish()
    _CACHE[kind] = nc
    return nc


_TAB = {}


def _tables():
    if not _TAB:
        _TAB['rope'] = [rope_tables(r) for r in range(4)]
        _TAB['at'] = [at_tables(r) for r in range(4)]
    return _TAB


def kernel(x, w_in, win_sink, mla_q_norm, mla_w_uq, mla_kv_norm, mla_w_ukv, ax_q_norm, ax_k_norm,
           diff_lambda, diff_subln, w_out, ffn_w_gu, ffn_w_down, ln_g, ln_b):
    inp = dict(x=x, w_in=w_in, win_sink=win_sink, mla_q_norm=mla_q_norm, mla_w_uq=mla_w_uq, mla_kv_norm=mla_kv_norm,
               mla_w_ukv=mla_w_ukv, ax_q_norm=ax_q_norm, ax_k_norm=ax_k_norm, diff_lambda=diff_lambda,
               diff_subln=diff_subln, w_out=w_out, ffn_w_gu=ffn_w_gu, ffn_w_down=ffn_w_down, ln_g=ln_g, ln_b=ln_b)
    inp = {k: np.asarray(v, np.float32) for k, v in inp.items()}
    ca = np.ascontiguousarray
    tabs = _tables()

    def f1pr_inputs(l, r):
        m = dict(f1_wgu=ca(inp['ffn_w_gu'][l, 0]), f1_wd=ca(inp['ffn_w_down'][l, 0]), f1_g=ca(inp['ln_g'][l, 0]),
                 f1_b=ca(inp['ln_b'][l, 0]), w_in=ca(inp['w_in'][l]), w_uq=ca(inp['mla_w_uq'][l]),
                 w_ukv=ca(inp['mla_w_ukv'][l]),
                 par=par_table(inp['mla_q_norm'][l], inp['mla_kv_norm'][l], inp['ax_q_norm'][l], inp['ax_k_norm'][l]))
        m.update(tabs['rope'][r])
        return m

    def at_inputs(l, c, qkv):
        b, r = c // 4, c % 4
        order = [b * 4 + (r + j) % 4 for j in range(4)]
        m = dict(QT=ca(qkv[c][0]), KTall=ca(np.stack([qkv[o][1] for o in order])),
                 Vall=ca(np.stack([qkv[o][2] for o in order])),
                 lpar=lpar_table(inp['win_sink'][l], inp['diff_lambda'][l], inp['diff_subln'][l], l),
                 o_w=ca(inp['w_out'][l]), o_g=ca(inp['ln_g'][l, 1]), o_b=ca(inp['ln_b'][l, 1]),
                 f2_wgu=ca(inp['ffn_w_gu'][l, 1]), f2_wd=ca(inp['ffn_w_down'][l, 1]), f2_g=ca(inp['ln_g'][l, 2]),
                 f2_b=ca(inp['ln_b'][l, 2]))
        m.update(tabs['at'][r])
        return m

    xs = [ca(inp['x'][c // 4, (c % 4) * TOK:(c % 4 + 1) * TOK]) for c in range(8)]
    nc = build_program('first')
    res = run_bass_kernel_spmd(nc, [dict(x_in=xs[c], **f1pr_inputs(0, c % 4)) for c in range(8)], core_ids=list(range(8)))
    xs = [r["x_out"] for r in res.results]
    qkv = [(r["QTo"], r["KTo"], r["Vo"]) for r in res.results]
    for l in range(DEPTH):
        lastl = (l == DEPTH - 1)
        nc = build_program('last' if lastl else 'mid')
        in_maps = []
        for c in range(8):
            m = dict(x_in=xs[c], **at_inputs(l, c, qkv))
            if not lastl:
                m.update(f1pr_inputs(l + 1, c % 4))
            in_maps.append(m)
        res = run_bass_kernel_spmd(nc, in_maps, core_ids=list(range(8)))
        xs = [r["x_out"] for r in res.results]
        if not lastl:
            qkv = [(r["QTo"], r["KTo"], r["Vo"]) for r in res.results]
    out = np.zeros((2, SEQ, D_MODEL), np.float32)
    for c in range(8):
        out[c // 4, (c % 4) * TOK:(c % 4 + 1) * TOK] = xs[c]
    return out
```

```python
import contextlib
import math
import numpy as np
import ml_dtypes
import concourse.bass as bass
import concourse.mybir as mybir
from concourse.bass_utils import run_bass_kernel_spmd

F32 = mybir.dt.float32
BF16 = mybir.dt.bfloat16
AF = mybir.ActivationFunctionType
ALU = mybir.AluOpType

D_MODEL = 1024
SEQ = 16384
DEPTH = 4
D_FF = 2816
NFC = D_FF // 128
TOK = 4096
NTT = TOK // 128
NG = TOK // 512
ALPHA = (2 * DEPTH) ** 0.25
EPS = 1e-5
N_IN = 2208

ENGS = ("pe", "act", "dve", "pool", "sp")
_CACHE = {}


class Sched:
    def __init__(self, nc):
        self.nc = nc
        self.ops = {e: [] for e in ENGS}
        self.lastw = {}
        self.readers = {}
        self.waited = {e: {} for e in ENGS}
        self.dma_cnt = {}
        self.dma_sems = []

    def _add_dep(self, eng, deps, tok):
        if tok is None:
            return
        kind, src, idx = tok
        if kind == 'eng' and src == eng and eng in ('pe', 'sp'):
            return
        w = self.waited[eng]
        k = (kind, src)
        if w.get(k, -1) >= idx:
            return
        w[k] = idx
        deps.append(tok)

    def op(self, eng, fn, reads=(), writes=(), dma=None, ndma=1):
        deps = []
        for r in reads:
            self._add_dep(eng, deps, self.lastw.get(r))
        for w in writes:
            self._add_dep(eng, deps, self.lastw.get(w))
            for t in self.readers.get(w, ()):
                self._add_dep(eng, deps, t)
        idx = len(self.ops[eng])
        self.ops[eng].append(dict(fn=fn, deps=deps, sig=False, dma=dma, ndma=ndma))
        if dma is not None:
            if dma not in self.dma_cnt:
                self.dma_cnt[dma] = 0
                self.dma_sems.append(dma)
            self.dma_cnt[dma] += ndma
            tok = ('dma', dma, self.dma_cnt[dma])
        else:
            tok = ('eng', eng, idx)
        for r in reads:
            self.readers.setdefault(r, []).append(tok)
        for w in writes:
            self.lastw[w] = tok
            self.readers[w] = []
        return tok

    def prepare(self):
        for e in ENGS:
            for rec in self.ops[e]:
                for kind, src, idx in rec['deps']:
                    if kind == 'eng':
                        self.ops[src][idx]['sig'] = True
            for rec in reversed(self.ops[e]):
                if rec['dma'] is None:
                    rec['sig'] = True
                    break
        self.sigcount = {}
        for e in ENGS:
            c = 0
            arr = []
            for rec in self.ops[e]:
                if rec['sig']:
                    c += 1
                arr.append(c)
            self.sigcount[e] = arr

    def run(self, e, h, esem, dpool, ebase, dbase):
        dsem = {k: dpool[i] for i, k in enumerate(self.dma_sems)}
        db = {k: dbase[i] for i, k in enumerate(self.dma_sems)}
        for rec in self.ops[e]:
            for kind, src, idx in rec['deps']:
                if kind == 'eng':
                    h.wait_ge(esem[src], ebase[src] + self.sigcount[src][idx])
                else:
                    h.wait_ge(dsem[src], db[src] + 16 * idx)
            r = rec['fn'](h)
            if rec['dma'] is not None:
                rl = r if isinstance(r, (list, tuple)) else [r]
                assert len(rl) == rec['ndma'], (len(rl), rec['ndma'])
                for ins in rl:
                    ins.then_inc(dsem[rec['dma']], 16)
            elif rec['sig']:
                r.then_inc(esem[e], 1)
        for e2 in ENGS:
            if self.sigcount[e2] and self.sigcount[e2][-1] > 0:
                h.wait_ge(esem[e2], ebase[e2] + self.sigcount[e2][-1])
        for k in self.dma_sems:
            h.wait_ge(dsem[k], db[k] + 16 * self.dma_cnt[k])


NUM_DEV = None
ARENA_WORDS = 48 * 1024 - 512


class Ctx:
    def __init__(self):
        self.nc = bass.Bass("TRN2", target_bir_lowering=False, num_devices=NUM_DEV)
        self.st = contextlib.ExitStack()
        self.arena = self.st.enter_context(self.nc.sbuf_tensor("arena", [128, ARENA_WORDS], F32))
        self.psum = self.st.enter_context(self.nc.psum_tensor("psum_all", [128, 8 * 512], F32))
        self.phases = []
        self.new_phase()

    def new_phase(self):
        self.S = Sched(self.nc)
        self.phases.append(self.S)
        self.off = 0
        self.bank = 0

    @staticmethod
    def _view(ap, shape, dt):
        n = int(np.prod(shape[1:]))
        if dt != F32:
            ap = ap.bitcast(dt)
        ap = ap[:, 0:n]
        if len(shape) == 3:
            ap = ap.rearrange("p (a b) -> p a b", b=shape[2])
        elif len(shape) == 4:
            ap = ap.rearrange("p (a b c) -> p a b c", b=shape[2], c=shape[3])
        return ap

    def sb(self, name, shape, dt):
        assert shape[0] == 128
        n = int(np.prod(shape[1:]))
        words = (n * (2 if dt == BF16 else 4) + 3) // 4
        assert self.off + words <= ARENA_WORDS, (name, self.off, words)
        ap = self.arena[:, self.off:self.off + words]
        self.off += words
        return self._view(ap, shape, dt)

    def ps(self, name, shape, dt):
        n = int(np.prod(shape[1:]))
        nb = (n * (2 if dt == BF16 else 4) + 2047) // 2048
        assert self.bank + nb <= 8, name
        ap = self.psum[:, self.bank * 512:(self.bank + nb) * 512]
        self.bank += nb
        return self._view(ap, shape, dt)

    def dram(self, name, shape, dt, kind="Internal"):
        return self.nc.dram_tensor(name, list(shape), dt, kind=kind).ap()

    def finish(self):
        nc = self.nc
        for S in self.phases:
            S.prepare()
        ndp = max(len(S.dma_sems) for S in self.phases)
        nph = len(self.phases)
        with contextlib.ExitStack() as st:
            esem = {e: st.enter_context(nc.semaphore("s_" + e)) for e in ENGS}
            dpool = [st.enter_context(nc.semaphore(f"d_{i}")) for i in range(ndp)]
            block = st.enter_context(nc.Block())
            ebases, dbases = [], []
            eb = {e: 0 for e in ENGS}
            dbv = [0] * ndp
            for S in self.phases:
                ebases.append(dict(eb))
                dbases.append(list(dbv))
                for e in ENGS:
                    eb[e] += S.sigcount[e][-1] if S.sigcount[e] else 0
                for i, k in enumerate(S.dma_sems):
                    dbv[i] += 16 * S.dma_cnt[k]

            def run(e, h):
                for pi, S in enumerate(self.phases):
                    S.run(e, h, esem, dpool, ebases[pi], dbases[pi])

            @block.tensor
            def _(h):
                run('pe', h)

            @block.scalar
            def _(h):
                run('act', h)

            @block.vector
            def _(h):
                run('dve', h)

            @block.gpsimd
            def _(h):
                run('pool', h)

            @block.sync
            def _(h):
                run('sp', h)
        self.st.close()
        return nc


def make_ident(C, name="ident"):
    S = C.S
    idf = C.sb(name + "_f", [128, 128], F32)
    idb = C.sb(name, [128, 128], BF16)
    S.op('pool', lambda h: h.memset(idf[:], 1.0), writes=[name + '_f'])
    S.op('pool', lambda h: h.affine_select(out=idf[:], in_=idf[:], pattern=[[-1, 128]],
                                            compare_op=ALU.is_equal, fill=0.0, base=0,
                                            channel_multiplier=1),
         reads=[name + '_f'], writes=[name + '_f'])
    S.op('pool', lambda h: h.tensor_copy(out=idb[:], in_=idf[:]), reads=[name + '_f'], writes=[name])
    return idb, idf


def load_xT_group(C, P, g, xin, scale_resid=True):
    S = C.S
    XR = P['XR'][g % 2]
    rk = ('XR', g % 2)
    for t in range(4):
        tt = g * 4 + t
        S.op('sp', lambda h, t=t, tt=tt: h.dma_start(out=XR[:, t, :], in_=xin[tt * 128:(tt + 1) * 128, :]),
             writes=[rk + (t,)], dma=f"xr{g % 2}{t}")
        xb = P['xb'][t % 2]
        S.op('dve', lambda h, t=t, xb=xb: h.tensor_copy(out=xb[:], in_=XR[:, t, :]),
             reads=[rk + (t,)], writes=[('xb', t % 2)])
        if scale_resid:
            S.op('act', lambda h, t=t: h.mul(out=XR[:, t, :], in_=XR[:, t, :], mul=ALPHA),
                 reads=[rk + (t,)], writes=[rk + (t,)])
        pT = P['pT']
        for k in range(8):
            S.op('pe', lambda h, k=k, xb=xb: h.transpose(out=pT[:, k, :], in_=xb[:, k * 128:(k + 1) * 128],
                                                          identity=P['ident'][:]),
                 reads=[('xb', t % 2), 'ident'], writes=['pT'])
        S.op('act', lambda h, t=t: h.copy(out=P['XT'][:, :, t * 128:(t + 1) * 128], in_=pT[:, :, :]),
             reads=['pT'], writes=['XT'])


def ln_tiles(C, P, g, xout, lnidx):
    S = C.S
    XR = P['XR'][g % 2]
    rk = ('XR', g % 2)
    st, mv, rstd = P['st'], P['mv'], P['rstd']
    for t in range(4):
        for hh in range(2):
            S.op('dve', lambda h, t=t, hh=hh: h.bn_stats(out=st[:, t, hh * 6:(hh + 1) * 6],
                                                          in_=XR[:, t, hh * 512:(hh + 1) * 512]),
                 reads=[rk + (t,)], writes=[('st', t)])
        S.op('dve', lambda h, t=t: h.bn_aggr(out=mv[:, t, :], in_=st[:, t, :]),
             reads=[('st', t)], writes=['mv'])
    S.op('dve', lambda h: h.tensor_scalar_add(out=rstd[:, :], in0=mv[:, :, 1], scalar1=EPS),
         reads=['mv'], writes=['rstd'])
    S.op('act', lambda h: h.sqrt(out=rstd[:, :], in_=rstd[:, :]), reads=['rstd'], writes=['rstd'])
    S.op('dve', lambda h: h.reciprocal(out=rstd[:, :], in_=rstd[:, :]), reads=['rstd'], writes=['rstd'])
    for t in range(4):
        tt = g * 4 + t
        ot = P['ot'][t % 2]
        ok = ('ot', t % 2)
        S.op('dve', lambda h, t=t, ot=ot: h.tensor_scalar(out=ot[:], in0=XR[:, t, :], scalar1=mv[:, t, 0:1],
                                                           scalar2=rstd[:, t:t + 1], op0=ALU.subtract,
                                                           op1=ALU.mult),
             reads=[rk + (t,), 'mv', 'rstd'], writes=[ok])
        S.op('pool', lambda h, ot=ot: h.tensor_tensor(out=ot[:], in0=ot[:], in1=P['G'][:, lnidx, :], op=ALU.mult),
             reads=[ok, 'GB'], writes=[ok])
        S.op('pool', lambda h, ot=ot: h.tensor_tensor(out=ot[:], in0=ot[:], in1=P['B'][:, lnidx, :], op=ALU.add),
             reads=[ok, 'GB'], writes=[ok])
        S.op('pool', lambda h, tt=tt, ot=ot: h.dma_start(out=xout[tt * 128:(tt + 1) * 128, :], in_=ot[:]),
             reads=[ok], dma=f"xo{t % 2}")


def phase_ffn(C, xin, xout, wgu, wd, lng, lnb, tag="f", dbg=None):
    nc, S = C.nc, C.S
    P = {}
    P['ident'], _ = make_ident(C, "ident" + tag)
    S.lastw['ident'] = S.lastw["ident" + tag]
    WD = C.sb("WD" + tag, [128, NFC, 1024], BF16)
    P['XT'] = C.sb("XT" + tag, [128, 8, 512], BF16)
    AT = C.sb("AT" + tag, [128, NFC, 512], BF16)
    P['XR'] = [C.sb(f"XR{i}" + tag, [128, 4, 1024], F32) for i in range(2)]
    WG = [C.sb(f"WG{i}" + tag, [128, 8, 256], BF16) for i in range(3)]
    STG = [C.sb(f"STG{i}" + tag, [128, 8, 256], F32) for i in range(2)]
    WDS = [C.sb(f"WDS{i}" + tag, [128, 1024], F32) for i in range(2)]
    P['G'] = C.sb("G" + tag, [128, 1, 1024], F32)
    P['B'] = C.sb("B" + tag, [128, 1, 1024], F32)
    P['xb'] = [C.sb(f"xb{i}" + tag, [128, 1024], BF16) for i in range(2)]
    SG = [C.sb(f"sg{i}" + tag, [128, 512], F32) for i in range(2)]
    P['ot'] = [C.sb(f"ot{i}" + tag, [128, 1024], F32) for i in range(2)]
    P['st'] = C.sb("st" + tag, [128, 4, 12], F32)
    P['mv'] = C.sb("mv" + tag, [128, 4, 2], F32)
    P['rstd'] = C.sb("rstd" + tag, [128, 4], F32)
    P['pT'] = C.ps("pT" + tag, [128, 8, 128], BF16)
    PG = [C.ps(f"pg{i}" + tag, [128, 512], F32) for i in range(2)]
    PU = [C.ps(f"pu{i}" + tag, [128, 512], F32) for i in range(2)]
    PY = [C.ps(f"py{i}" + tag, [128, 512], F32) for i in range(2)]
    WGS = C.dram("wgs" + tag, [NFC, 128, 2048], BF16)

    S.op('sp', lambda h: [h.dma_start(out=P['G'][:, 0, :], in_=lng.partition_broadcast(128)),
                          h.dma_start(out=P['B'][:, 0, :], in_=lnb.partition_broadcast(128))],
         writes=['GB'], dma="gb", ndma=2)
    wgu_r = wgu.rearrange("(k p) n -> p k n", p=128)
    for c in range(NFC):
        stg = STG[c % 2]
        S.op('sp', lambda h, c=c, stg=stg: [
            h.dma_start(out=stg[:, :, 0:128], in_=wgu_r[:, :, c * 128:(c + 1) * 128]),
            h.dma_start(out=stg[:, :, 128:256], in_=wgu_r[:, :, D_FF + c * 128:D_FF + (c + 1) * 128])],
            writes=[('stg', c % 2)], dma=f"stg{c % 2}", ndma=2)
        wg = WG[c % 3]
        S.op('pool', lambda h, stg=stg, wg=wg: h.tensor_copy(out=wg[:], in_=stg[:]),
             reads=[('stg', c % 2)], writes=[('wg', c % 3)])
        S.op('pool', lambda h, c=c, wg=wg: h.dma_start(out=WGS[c].rearrange("p (k n) -> p k n", k=8), in_=wg[:]),
             reads=[('wg', c % 3)], writes=[('wgs', c)], dma=f"wgsw{c % 3}")
    for c in range(NFC):
        wds = WDS[c % 2]
        S.op('sp', lambda h, c=c, wds=wds: h.dma_start(out=wds[:], in_=wd[c * 128:(c + 1) * 128, :]),
             writes=[('wds', c % 2)], dma=f"wds{c % 2}")
        S.op('dve', lambda h, c=c, wds=wds: h.tensor_copy(out=WD[:, c, :], in_=wds[:]),
             reads=[('wds', c % 2)], writes=['WD'])

    for g in range(NG):
        load_xT_group(C, P, g, xin)
        XR = P['XR'][g % 2]
        rk = ('XR', g % 2)
        for c in range(NFC):
            wg = WG[c % 3]
            S.op('sp', lambda h, c=c, wg=wg: h.dma_start(out=wg[:], in_=WGS[c].rearrange("p (k n) -> p k n", k=8)),
                 reads=[('wgs', c)], writes=[('wg', c % 3)], dma=f"wg{c % 3}")
            pg, pu = PG[c % 2], PU[c % 2]
            for k in range(8):
                S.op('pe', lambda h, k=k, wg=wg, pg=pg: h.matmul(pg[:], lhsT=wg[:, k, 0:128], rhs=P['XT'][:, k, :],
                                                                 start=(k == 0), stop=(k == 7)),
                     reads=[('wg', c % 3), 'XT'], writes=[('pg', c % 2)])
            for k in range(8):
                S.op('pe', lambda h, k=k, wg=wg, pu=pu: h.matmul(pu[:], lhsT=wg[:, k, 128:256], rhs=P['XT'][:, k, :],
                                                                 start=(k == 0), stop=(k == 7)),
                     reads=[('wg', c % 3), 'XT'], writes=[('pu', c % 2)])
            sg = SG[c % 2]
            S.op('act', lambda h, pg=pg, sg=sg: h.activation(out=sg[:], in_=pg[:], func=AF.Silu),
                 reads=[('pg', c % 2)], writes=[('sg', c % 2)])
            S.op('dve', lambda h, c=c, sg=sg, pu=pu: h.tensor_tensor(out=AT[:, c, :], in0=sg[:], in1=pu[:], op=ALU.mult),
                 reads=[('sg', c % 2), ('pu', c % 2)], writes=['AT'])
        if dbg is not None and g == 0:
            S.op('sp', lambda h: h.dma_start(out=dbg['xt'].rearrange("p (k n) -> p k n", k=8), in_=P['XT'][:]), reads=['XT'], dma='dbg1')
            S.op('sp', lambda h: h.dma_start(out=dbg['at'].rearrange("p (k n) -> p k n", k=NFC), in_=AT[:]), reads=['AT'], dma='dbg2')
            S.op('sp', lambda h: h.dma_start(out=dbg['wd'].rearrange("p (k n) -> p k n", k=NFC), in_=WD[:]), reads=['WD'], dma='dbg3')
            S.op('sp', lambda h: h.dma_start(out=dbg['wg'].rearrange("p (k n) -> p k n", k=8), in_=WG[(NFC - 1) % 3][:]), reads=[('wg', (NFC - 1) % 3)], dma='dbg4')
        for t in range(4):
            for hh in range(2):
                py = PY[hh]
                for c in range(NFC):
                    S.op('pe', lambda h, c=c, t=t, hh=hh, py=py: h.matmul(
                        py[:], lhsT=AT[:, c, t * 128:(t + 1) * 128], rhs=WD[:, c, hh * 512:(hh + 1) * 512],
                        start=(c == 0), stop=(c == NFC - 1)),
                        reads=['AT', 'WD'], writes=[('py', hh)])
                S.op('dve', lambda h, t=t, hh=hh, py=py, XR=XR: h.scalar_tensor_tensor(
                    out=XR[:, t, hh * 512:(hh + 1) * 512], in0=py[:], scalar=0.5,
                    in1=XR[:, t, hh * 512:(hh + 1) * 512], op0=ALU.mult, op1=ALU.add),
                    reads=[('py', hh), rk + (t,)], writes=[rk + (t,)])
        ln_tiles(C, P, g, xout, 0)


QROW_A, QROW_B, QROW_C, QROW_D, RQ = 0, 256, 640, 896, 1152
KROW_A, KROW_BN, KROW_BR, KROW_C, KROW_D, RK = 0, 128, 384, 416, 544, 800
VCOL_A, VCOL_C, VCOL_D, VCOL_B, RV = 0, 128, 256, 512, 768
SC_A = 64 ** -0.5
SC_B = 96 ** -0.5
SC_C = 64 ** -0.5
SC_D = 32 ** -0.5
PR_SKIP = set()
PAR_GQ0, PAR_GQ1, PAR_GKV, PAR_AXQ, PAR_AXQP, PAR_AXK, PAR_AXKP = 0, 1, 2, 3, 4, 5, 6
NPAR = 8


def phase_pr(C, xin, w_in, w_uq, w_ukv, par, tcc, tcs, tbc, tbs, QT, KT, V, tag="p"):
    nc, S = C.nc, C.S
    P = {}
    P['ident'], _ = make_ident(C, "ident" + tag)
    S.lastw['ident'] = S.lastw["ident" + tag]
    P['XT'] = C.sb("XT" + tag, [128, 8, 512], BF16)
    P['XR'] = [C.sb(f"XR{i}" + tag, [128, 4, 1024], F32) for i in range(2)]
    P['xb'] = [C.sb(f"xb{i}" + tag, [128, 1024], BF16) for i in range(2)]
    P['pT'] = C.ps("pT" + tag, [128, 8, 128], BF16)
    WIN = C.sb("WIN" + tag, [128, 8, N_IN], BF16)
    WINP = C.sb("WINP" + tag, [128, 8, 480], BF16)
    WINV = C.sb("WINV" + tag, [128, 8, 512], BF16)
    WST = [C.sb(f"WST{i}" + tag, [128, N_IN], F32) for i in range(2)]
    WUQ = C.sb("WUQ" + tag, [128, 2, 384], BF16)
    WUQP = C.sb("WUQP" + tag, [128, 2, 384], BF16)
    WUKVK = C.sb("WUKVK" + tag, [128, 256], BF16)
    WUKVV = C.sb("WUKVV" + tag, [128, 256], BF16)
    PAR = C.sb("PAR" + tag, [128, NPAR], F32)
    ONESF = C.sb("ONESF" + tag, [128, 128], BF16)
    BDF = C.sb("BDF" + tag, [128, 128], BF16)
    SQH = [C.sb(f"SQH{i}" + tag, [128, 512], BF16) for i in range(2)]
    SQL = [C.sb(f"SQL{i}" + tag, [128, 512], BF16) for i in range(2)]
    TCC = [C.sb(f"TCC{i}" + tag, [128, 512], F32) for i in range(2)]
    TCS = [C.sb(f"TCS{i}" + tag, [128, 512], F32) for i in range(2)]
    TBC = [C.sb(f"TBC{i}" + tag, [128, 512], F32) for i in range(2)]
    TBS = [C.sb(f"TBS{i}" + tag, [128, 512], F32) for i in range(2)]
    SQ = [C.sb(f"SQ{i}" + tag, [128, 512], F32) for i in range(2)]
    R = C.sb("R" + tag, [128, 512], F32)
    T1 = C.sb("T1" + tag, [128, 512], F32)
    T2 = C.sb("T2" + tag, [128, 512], F32)
    CQN = C.sb("CQN" + tag, [128, 2, 512], BF16)
    CKVN = C.sb("CKVN" + tag, [128, 512], BF16)
    OB = [C.sb(f"OB{i}" + tag, [128, 512], BF16) for i in range(4)]
    VO = [C.sb(f"VO{i}" + tag, [128, RV], BF16) for i in range(2)]
    PA = [C.ps(f"pa{i}" + tag, [128, 512], F32) for i in range(4)]
    PB = [C.ps(f"pb{i}" + tag, [128, 512], F32) for i in range(2)]
    st_ = dict(pa=0, ob=0, pb=0)

    S.op('sp', lambda h: h.dma_start(out=PAR[:], in_=par[:, :]), writes=['PAR'], dma="par")
    S.op('pool', lambda h: h.memset(ONESF[:], 1.0), writes=['ONESF'])
    S.op('pool', lambda h: h.memset(BDF[:], 0.0), writes=['BDF'])
    S.op('pool', lambda h: h.memset(BDF[0:64, 0:64], 1.0), reads=['BDF'], writes=['BDF'])
    S.op('pool', lambda h: h.memset(BDF[64:128, 64:128], 1.0), reads=['BDF'], writes=['BDF'])
    for k in range(8):
        wst = WST[k % 2]
        S.op('sp', lambda h, k=k, wst=wst: h.dma_start(out=wst[:], in_=w_in[k * 128:(k + 1) * 128, :]),
             writes=[('wst', k % 2)], dma=f"wst{k % 2}")
        S.op('dve', lambda h, k=k, wst=wst: h.tensor_copy(out=WIN[:, k, :], in_=wst[:]),
             reads=[('wst', k % 2)], writes=[('WIN', k)])
        src = WIN[:, k, 928:1312].rearrange("p (a b c) -> p a b c", b=2, c=16)
        dst = WINP[:, k, 0:384].rearrange("p (a b c) -> p a b c", b=2, c=16)
        S.op('pool', lambda h, src=src, dst=dst: h.tensor_copy(out=dst[:, :, 0, :], in_=src[:, :, 1, :]),
             reads=[('WIN', k)], writes=[('WINP', k)])
        S.op('pool', lambda h, src=src, dst=dst: h.tensor_copy(out=dst[:, :, 1, :], in_=src[:, :, 0, :]),
             reads=[('WIN', k)], writes=[('WINP', k)])
        S.op('pool', lambda h, k=k: h.tensor_copy(out=WINP[:, k, 384:448], in_=WIN[:, k, 832:896]),
             reads=[('WIN', k)], writes=[('WINP', k)])
        S.op('pool', lambda h, k=k: h.tensor_copy(out=WINP[:, k, 448:464], in_=WIN[:, k, 912:928]),
             reads=[('WIN', k)], writes=[('WINP', k)])
        S.op('pool', lambda h, k=k: h.tensor_copy(out=WINP[:, k, 464:480], in_=WIN[:, k, 896:912]),
             reads=[('WIN', k)], writes=[('WINP', k)])
        S.op('pool', lambda h, k=k: h.tensor_copy(out=WINV[:, k, 0:128], in_=WIN[:, k, 384:512]),
             reads=[('WIN', k)], writes=[('WINV', k)])
        S.op('pool', lambda h, k=k: h.tensor_copy(out=WINV[:, k, 128:256], in_=WIN[:, k, 1312:1440]),
             reads=[('WIN', k)], writes=[('WINV', k)])
        S.op('pool', lambda h, k=k: h.tensor_copy(out=WINV[:, k, 256:512], in_=WIN[:, k, 1952:2208]),
             reads=[('WIN', k)], writes=[('WINV', k)])
    for j in range(2):
        wst = WST[j % 2]
        S.op('sp', lambda h, j=j, wst=wst: h.dma_start(out=wst[:, 0:384], in_=w_uq[j * 128:(j + 1) * 128, :]),
             writes=[('wst', j % 2)], dma=f"wst{j % 2}")
        S.op('dve', lambda h, j=j, wst=wst: h.tensor_copy(out=WUQ[:, j, :], in_=wst[:, 0:384]),
             reads=[('wst', j % 2)], writes=['WUQ'])
        sv = WUQ[:, j, :].rearrange("p (a b) -> p a b", b=96)
        dv = WUQP[:, j, :].rearrange("p (a b) -> p a b", b=96)
        S.op('pool', lambda h, sv=sv, dv=dv: h.tensor_copy(out=dv[:, :, 0:64], in_=sv[:, :, 0:64]),
             reads=['WUQ'], writes=['WUQP'])
        S.op('pool', lambda h, sv=sv, dv=dv: h.tensor_copy(out=dv[:, :, 64:80], in_=sv[:, :, 80:96]),
             reads=['WUQ'], writes=['WUQP'])
        S.op('pool', lambda h, sv=sv, dv=dv: h.tensor_copy(out=dv[:, :, 80:96], in_=sv[:, :, 64:80]),
             reads=['WUQ'], writes=['WUQP'])
    wst = WST[0]
    S.op('sp', lambda h: h.dma_start(out=WST[0][:, 0:512], in_=w_ukv[:, :]), writes=[('wst', 0)], dma="wst0")
    sv = WST[0][:, 0:512].rearrange("p (a b) -> p a b", b=128)
    S.op('dve', lambda h, sv=sv: h.tensor_copy(out=WUKVK[:, :].rearrange("p (a b) -> p a b", b=64), in_=sv[:, :, 0:64]),
         reads=[('wst', 0)], writes=['WUKV'])
    S.op('dve', lambda h, sv=sv: h.tensor_copy(out=WUKVV[:, :].rearrange("p (a b) -> p a b", b=64), in_=sv[:, :, 64:128]),
         reads=[('wst', 0)], writes=['WUKV'])
    WKEYS = [('WIN', k) for k in range(8)] + [('WINP', k) for k in range(8)] + [('WINV', k) for k in range(8)]

    def pa_next():
        i = st_['pa'] % 4
        st_['pa'] += 1
        return PA[i], ('pa', i)

    def pb_next():
        i = st_['pb'] % 2
        st_['pb'] += 1
        return PB[i], ('pb', i)

    def ob_next():
        i = st_['ob'] % 4
        st_['ob'] += 1
        return OB[i], ('ob', i)

    def fm(W, c0, m):
        pa, pk = pa_next()
        for k in range(8):
            S.op('pe', lambda h, k=k, pa=pa: h.matmul(pa[0:m, :], lhsT=W[:, k, c0:c0 + m], rhs=P['XT'][:, k, :],
                                                       start=(k == 0), stop=(k == 7)),
                 reads=WKEYS + ['XT'], writes=[pk])
        return pa, pk

    def store(ob, ok, lo, hi, dst, row0):
        g = st_['g']
        S.op('sp', lambda h: h.dma_start(out=dst[row0:row0 + (hi - lo), g * 512:(g + 1) * 512], in_=ob[lo:hi, :]),
             reads=[ok], dma="st" + ok[0] + str(ok[1]))

    def rms_scale(ssps, sskey, inv_n, lnscale):
        S.op('act', lambda h: h.activation(out=R[:], in_=ssps[:], func=AF.Ln, bias=EPS, scale=inv_n),
             reads=[sskey], writes=['R'])
        S.op('act', lambda h: h.activation(out=R[:], in_=R[:], func=AF.Exp, bias=lnscale, scale=-0.5),
             reads=['R'], writes=['R'])

    def sumsq(srcs, lhsT, lkey):
        pb, pbk = pb_next()
        n = len(srcs)
        for j, (ps_, psk) in enumerate(srcs):
            S.op('act', lambda h, ps_=ps_, j=j: h.activation(out=SQ[j][:], in_=ps_[:], func=AF.Square),
                 reads=[psk], writes=[('SQ', j)])
            S.op('dve', lambda h, j=j: h.tensor_copy(out=SQH[j][:], in_=SQ[j][:]), reads=[('SQ', j)], writes=[('SQH', j)])
            S.op('dve', lambda h, j=j: h.tensor_tensor(out=SQL[j][:], in0=SQ[j][:], in1=SQH[j][:], op=ALU.subtract),
                 reads=[('SQ', j), ('SQH', j)], writes=[('SQL', j)])
        for j in range(n):
            S.op('pe', lambda h, pb=pb, j=j: h.matmul(pb[:], lhsT=lhsT[:], rhs=SQH[j][:], start=(j == 0), stop=False),
                 reads=[lkey, ('SQH', j)], writes=[pbk])
            S.op('pe', lambda h, pb=pb, j=j: h.matmul(pb[:], lhsT=lhsT[:], rhs=SQL[j][:], start=False, stop=(j == n - 1)),
                 reads=[lkey, ('SQL', j)], writes=[pbk])
        return pb, pbk

    for g in range(NG):
        st_['g'] = g
        load_xT_group(C, P, g, xin, scale_resid=False)
        tcc_, tcs_, tbc_, tbs_ = TCC[g % 2], TCS[g % 2], TBC[g % 2], TBS[g % 2]
        tk = ('tab', g % 2)
        S.op('sp', lambda h, g=g, a=tcc_, b=tcs_, c=tbc_, d=tbs_: [
            h.dma_start(out=a[:], in_=tcc[:, g * 512:(g + 1) * 512]),
            h.dma_start(out=b[:], in_=tcs[:, g * 512:(g + 1) * 512]),
            h.dma_start(out=c[64:96, :], in_=tbc[:, g * 512:(g + 1) * 512]),
            h.dma_start(out=d[64:96, :], in_=tbs[:, g * 512:(g + 1) * 512])],
            writes=[tk], dma=f"tab{g % 2}", ndma=4)

        for (c0, scale, dst, row0) in [] if 'AD' in PR_SKIP else [(0, SC_A, QT, QROW_A), (128, SC_A, QT, QROW_A + 128),
                                       (256, None, KT, KROW_A),
                                       (1440, SC_D, QT, QROW_D), (1568, SC_D, QT, QROW_D + 128),
                                       (1696, None, KT, KROW_D), (1824, None, KT, KROW_D + 128)]:
            pa, pk = fm(WIN, c0, 128)
            ob, ok = ob_next()
            if scale is None:
                S.op('act', lambda h, pa=pa, ob=ob: h.copy(out=ob[:], in_=pa[:]), reads=[pk], writes=[ok])
            else:
                S.op('act', lambda h, pa=pa, ob=ob, scale=scale: h.mul(out=ob[:], in_=pa[:], mul=scale),
                     reads=[pk], writes=[ok])
            store(ob, ok, 0, 128, dst, row0)

        for (c0, cp0, gcol, gpcol, lnsc, dst, row0) in [] if 'C' in PR_SKIP else [
                (928, 0, PAR_AXQ, PAR_AXQP, math.log(SC_C), QT, QROW_C),
                (1056, 128, PAR_AXQ, PAR_AXQP, math.log(SC_C), QT, QROW_C + 128),
                (1184, 256, PAR_AXK, PAR_AXKP, 0.0, KT, KROW_C)]:
            pm, pmk = fm(WIN, c0, 128)
            pp, ppk = fm(WINP, cp0, 128)
            pb, pbk = sumsq([(pm, pmk)], BDF, 'BDF')
            rms_scale(pb, pbk, 1.0 / 64, lnsc)
            S.op('dve', lambda h, pm=pm, gcol=gcol, a=tcc_: h.scalar_tensor_tensor(
                out=T1[:], in0=pm[:], scalar=PAR[:, gcol:gcol + 1], in1=a[:], op0=ALU.mult, op1=ALU.mult),
                reads=[pmk, 'PAR', tk], writes=['T1'])
            S.op('dve', lambda h, pp=pp, gpcol=gpcol, b=tcs_: h.scalar_tensor_tensor(
                out=T2[:], in0=pp[:], scalar=PAR[:, gpcol:gpcol + 1], in1=b[:], op0=ALU.mult, op1=ALU.mult),
                reads=[ppk, 'PAR', tk], writes=['T2'])
            S.op('dve', lambda h: h.tensor_tensor(out=T1[:], in0=T1[:], in1=T2[:], op=ALU.add),
                 reads=['T1', 'T2'], writes=['T1'])
            ob, ok = ob_next()
            S.op('dve', lambda h, ob=ob: h.tensor_tensor(out=ob[:], in0=T1[:], in1=R[:], op=ALU.mult),
                 reads=['T1', 'R'], writes=[ok])
            store(ob, ok, 0, 128, dst, row0)

        if 'B' in PR_SKIP:
            continue
        pcq = []
        for j in range(2):
            pa, pk = fm(WIN, 512 + j * 128, 128)
            pcq.append((pa, pk))
        pb, pbk = sumsq(pcq, ONESF, 'ONESF')
        rms_scale(pb, pbk, 1.0 / 256, 0.0)
        for j in range(2):
            pa, pk = pcq[j]
            S.op('dve', lambda h, pa=pa, j=j: h.scalar_tensor_tensor(
                out=CQN[:, j, :], in0=pa[:], scalar=PAR[:, PAR_GQ0 + j:PAR_GQ0 + j + 1], in1=R[:],
                op0=ALU.mult, op1=ALU.mult), reads=[pk, 'PAR', 'R'], writes=['CQN'])
        for hh in range(4):
            pm, pmk = pa_next()
            pp, ppk = pa_next()
            for (ps_, psk, W) in ((pm, pmk, WUQ), (pp, ppk, WUQP)):
                for j in range(2):
                    S.op('pe', lambda h, ps_=ps_, W=W, j=j, hh=hh: h.matmul(
                        ps_[0:96, :], lhsT=W[:, j, hh * 96:(hh + 1) * 96], rhs=CQN[:, j, :],
                        start=(j == 0), stop=(j == 1)), reads=['WUQ', 'WUQP', 'CQN'], writes=[psk])
            ob, ok = ob_next()
            S.op('act', lambda h, pm=pm, ob=ob: h.mul(out=ob[0:64, :], in_=pm[0:64, :], mul=SC_B),
                 reads=[pmk], writes=[ok])
            S.op('dve', lambda h, pm=pm, c=tbc_: h.scalar_tensor_tensor(
                out=T1[64:96, :], in0=pm[64:96, :], scalar=SC_B, in1=c[64:96, :], op0=ALU.mult, op1=ALU.mult),
                reads=[pmk, tk], writes=['T1'])
            S.op('dve', lambda h, pp=pp, d=tbs_: h.scalar_tensor_tensor(
                out=T2[64:96, :], in0=pp[64:96, :], scalar=SC_B, in1=d[64:96, :], op0=ALU.mult, op1=ALU.mult),
                reads=[ppk, tk], writes=['T2'])
            S.op('dve', lambda h, ob=ob: h.tensor_tensor(out=ob[64:96, :], in0=T1[64:96, :], in1=T2[64:96, :], op=ALU.add),
                 reads=['T1', 'T2', ok], writes=[ok])
            store(ob, ok, 0, 96, QT, QROW_B + hh * 96)
        pkv, pkvk = fm(WIN, 768, 128)
        pb, pbk = sumsq([(pkv, pkvk)], ONESF, 'ONESF')
        rms_scale(pb, pbk, 1.0 / 128, 0.0)
        S.op('dve', lambda h, pkv=pkv: h.scalar_tensor_tensor(
            out=CKVN[:], in0=pkv[:], scalar=PAR[:, PAR_GKV:PAR_GKV + 1], in1=R[:], op0=ALU.mult, op1=ALU.mult),
            reads=[pkvk, 'PAR', 'R'], writes=['CKVN'])
        for jj in range(2):
            pa, pk = pa_next()
            S.op('pe', lambda h, pa=pa, jj=jj: h.matmul(pa[:], lhsT=WUKVK[:, jj * 128:(jj + 1) * 128], rhs=CKVN[:],
                                                         start=True, stop=True), reads=['WUKV', 'CKVN'], writes=[pk])
            ob, ok = ob_next()
            S.op('act', lambda h, pa=pa, ob=ob: h.copy(out=ob[:], in_=pa[:]), reads=[pk], writes=[ok])
            store(ob, ok, 0, 128, KT, KROW_BN + jj * 128)
        pm, pmk = fm(WIN, 832, 96)
        pp, ppk = fm(WINP, 384, 96)
        S.op('dve', lambda h, pm=pm, c=tbc_: h.tensor_tensor(out=T1[64:96, :], in0=pm[64:96, :], in1=c[64:96, :], op=ALU.mult),
             reads=[pmk, tk], writes=['T1'])
        S.op('dve', lambda h, pp=pp, d=tbs_: h.tensor_tensor(out=T2[64:96, :], in0=pp[64:96, :], in1=d[64:96, :], op=ALU.mult),
             reads=[ppk, tk], writes=['T2'])
        ob, ok = ob_next()
        S.op('dve', lambda h, ob=ob: h.tensor_tensor(out=ob[64:96, :], in0=T1[64:96, :], in1=T2[64:96, :], op=ALU.add),
             reads=['T1', 'T2'], writes=[ok])
        store(ob, ok, 64, 96, KT, KROW_BR)
        for tl in range(4):
            pa, pk = pa_next()
            for k in range(8):
                S.op('pe', lambda h, pa=pa, k=k, tl=tl: h.matmul(pa[:], lhsT=P['XT'][:, k, tl * 128:(tl + 1) * 128],
                                                                 rhs=WINV[:, k, :], start=(k == 0), stop=(k == 7)),
                     reads=WKEYS + ['XT'], writes=[pk])
            pb, pbk = pb_next()
            S.op('pe', lambda h, pb=pb, tl=tl: h.matmul(pb[:, 0:256], lhsT=CKVN[:, tl * 128:(tl + 1) * 128], rhs=WUKVV[:],
                                                        start=True, stop=True), reads=['CKVN', 'WUKV'], writes=[pbk])
            vo = VO[tl % 2]
            vk = ('vo', tl % 2)
            S.op('act', lambda h, pa=pa, vo=vo: h.copy(out=vo[:, 0:512], in_=pa[:]), reads=[pk], writes=[vk])
            S.op('dve', lambda h, pb=pb, vo=vo: h.tensor_copy(out=vo[:, 512:768], in_=pb[:, 0:256]), reads=[pbk, vk], writes=[vk])
            tt = g * 4 + tl
            S.op('sp', lambda h, vo=vo, tt=tt: h.dma_start(out=V[tt * 128:(tt + 1) * 128, :], in_=vo[:]),
                 reads=[vk], dma=f"vo{tl % 2}")


NLP = 136
LP_SINK, LP_LAM, LP_LINIT, LP_1MLINIT, LP_SUBG = 0, 4, 132, 133, 134
KCOLS = 5 * TOK
B_FULLK = False
D_FULLK = True
AT_UNITS = None
AT_GROUPS = None


def phase_at(C, QT, KTall, Vall, kaug, qaug, ddiag, nsi, wdist, wmask, lpar, mixT, tag="a"):
    nc, S = C.nc, C.S
    IDB, _ = make_ident(C, "identa")
    KS = [C.sb(f"KS{i}", [128, KCOLS], BF16) for i in range(2)]
    VS = [C.sb(f"VS{i}", [128, 128, 96], BF16) for i in range(2)]
    QS = [C.sb(f"QS{i}", [128, TOK], BF16) for i in range(2)]
    QD = C.sb("QD", [128, TOK], BF16)
    DD = C.sb("DD", [128, 4, 2, 512], BF16)
    NSI = C.sb("NSI", [128, 8, 128], BF16)
    WDIST = C.sb("WDIST", [128, 3, 128], BF16)
    WMASK = C.sb("WMASK", [128, 5, 128], BF16)
    LP = C.sb("LP", [128, NLP], F32)
    SM = C.sb("SM", [128, 16], F32)
    TMP = C.sb("TMPa", [128, 32], F32)
    ONESB = C.sb("ONESBa", [128, 64], BF16)
    PT = [C.sb(f"PT{i}", [128, 1024], BF16) for i in range(3)]
    RC = [C.sb(f"RC{i}", [128, 512], F32) for i in range(2)]
    A1 = C.sb("A1", [128, 512], F32)
    DN = A1
    A2 = C.sb("A2", [128, 512], F32)
    O1 = A2
    SQd = C.sb("SQd", [128, 512], F32)
    SQHd = C.sb("SQHd", [128, 512], BF16)
    SQLd = C.sb("SQLd", [128, 512], BF16)
    Rr = C.sb("Rr", [128, 512], F32)
    OUTB = [C.sb(f"OUTB{i}", [128, 512], BF16) for i in range(2)]
    PSS = [C.ps(f"pss{i}", [128, 1024], F32) for i in range(2)]
    PO = [C.ps(f"po{i}", [128, 512], F32) for i in range(3)]
    PBC = C.ps("pbc", [128, 512], F32)
    cnt = dict(po=0, rc=0, outb=0)

    S.op('sp', lambda h: [h.dma_start(out=DD[:].rearrange("p a b c -> p (a b c)"), in_=ddiag[:, :]),
                          h.dma_start(out=NSI[:].rearrange("p a b -> p (a b)"), in_=nsi[:, :]),
                          h.dma_start(out=WDIST[:].rearrange("p a b -> p (a b)"), in_=wdist[:, :]),
                          h.dma_start(out=WMASK[:].rearrange("p a b -> p (a b)"), in_=wmask[:, :]),
                          h.dma_start(out=LP[:], in_=lpar[:, :])],
         writes=['CONST'], dma="const", ndma=5)
    S.op('pool', lambda h: h.memset(ONESB[:], 1.0), writes=['ONESB'])
    for i in range(2):
        S.op('pool', lambda h, i=i: h.memset(VS[i][:, :, 64:96], 1.0), writes=[('Vones', i)])
        if D_FULLK or B_FULLK:
            S.op('pool', lambda h, i=i: h.memset(KS[i][:, :], 0.0), writes=[('K', i)])
    S.op('dve', lambda h: h.tensor_tensor(out=TMP[:], in0=LP[:, 4:36], in1=LP[:, 36:68], op=ALU.mult), reads=['CONST'], writes=['TMP'])
    S.op('dve', lambda h: h.reduce_sum(out=SM[:, 0:1], in_=TMP[:], axis=mybir.AxisListType.X), reads=['TMP'], writes=['SM0'])
    S.op('dve', lambda h: h.tensor_tensor(out=TMP[:], in0=LP[:, 68:100], in1=LP[:, 100:132], op=ALU.mult), reads=['CONST', 'SM0'], writes=['TMP'])
    S.op('dve', lambda h: h.reduce_sum(out=SM[:, 1:2], in_=TMP[:], axis=mybir.AxisListType.X), reads=['TMP'], writes=['SM0'])
    S.op('act', lambda h: h.activation(out=SM[:, 2:4], in_=SM[:, 0:2], func=AF.Exp), reads=['SM0'], writes=['SM1'])
    S.op('act', lambda h: h.activation(out=SM[:, 8:12], in_=LP[:, 0:4], func=AF.Exp), reads=['CONST'], writes=['ESINK'])
    S.op('dve', lambda h: h.tensor_tensor(out=SM[:, 4:5], in0=SM[:, 2:3], in1=SM[:, 3:4], op=ALU.subtract), reads=['SM1'], writes=['SM2'])
    S.op('dve', lambda h: h.tensor_tensor(out=SM[:, 4:5], in0=SM[:, 4:5], in1=LP[:, LP_LINIT:LP_LINIT + 1], op=ALU.add), reads=['SM2', 'CONST'], writes=['SM2'])
    S.op('dve', lambda h: h.tensor_scalar_mul(out=SM[:, 5:6], in0=SM[:, 4:5], scalar1=-1.0), reads=['SM2'], writes=['NEGLAM'])
    S.op('dve', lambda h: h.tensor_tensor(out=SM[:, 6:7], in0=LP[:, LP_SUBG:LP_SUBG + 1], in1=LP[:, LP_1MLINIT:LP_1MLINIT + 1], op=ALU.mult),
         reads=['CONST'], writes=['GSUB'])

    def po_next():
        i = cnt['po'] % 3
        cnt['po'] += 1
        return PO[i], ('po', i)

    def recip_den(po, pok, sinkcol=None):
        i = cnt['rc'] % 2
        cnt['rc'] += 1
        rc, rck = RC[i], ('rc', i)
        src, srck = po, pok
        if sinkcol is not None:
            S.op('dve', lambda h: h.tensor_scalar_add(out=DN[64:96, :], in0=po[64:96, :], scalar1=SM[64:96, sinkcol:sinkcol + 1]),
                 reads=[pok, 'ESINK'], writes=['DN'])
            src, srck = DN, 'DN'
        S.op('dve', lambda h: h.reciprocal(out=rc[0:32, :], in_=src[64:96, :]), reads=[srck], writes=[rck])
        S.op('dve', lambda h: h.reciprocal(out=rc[32:64, :], in_=src[64:96, :]), reads=[srck, rck], writes=[rck])
        return rc, rck

    def store_out(ob, obk, row0, g):
        S.op('pool', lambda h: h.dma_start(out=mixT[row0:row0 + 64, g * 512:(g + 1) * 512], in_=ob[0:64, :]),
             reads=[obk], dma="mx" + str(obk[1]))

    def epi_plain(po, pok, row0, g, sinkcol=None):
        rc, rck = recip_den(po, pok, sinkcol)
        i = cnt['outb'] % 2
        cnt['outb'] += 1
        ob, obk = OUTB[i], ('outb', i)
        S.op('dve', lambda h: h.tensor_tensor(out=ob[0:64, :], in0=po[0:64, :], in1=rc[0:64, :], op=ALU.mult),
             reads=[pok, rck], writes=[obk])
        store_out(ob, obk, row0, g)

    def epi_d_first(po, pok):
        rc, rck = recip_den(po, pok)
        S.op('dve', lambda h: h.tensor_tensor(out=A1[0:64, :], in0=po[0:64, :], in1=rc[0:64, :], op=ALU.mult),
             reads=[pok, rck], writes=['A1'])

    def epi_d_second(po, pok):
        rc, rck = recip_den(po, pok)
        S.op('dve', lambda h: h.tensor_tensor(out=A2[0:64, :], in0=po[0:64, :], in1=rc[0:64, :], op=ALU.mult),
             reads=[pok, rck], writes=['A2'])
        S.op('dve', lambda h: h.scalar_tensor_tensor(out=O1[0:64, :], in0=A2[0:64, :], scalar=SM[0:64, 5:6], in1=A1[0:64, :],
                                                      op0=ALU.mult, op1=ALU.add), reads=['A1', 'A2', 'NEGLAM'], writes=['O1'])
        S.op('pool', lambda h: h.tensor_tensor(out=SQd[0:64, :], in0=O1[0:64, :], in1=O1[0:64, :], op=ALU.mult), reads=['O1'], writes=['SQd'])
        S.op('pool', lambda h: h.tensor_copy(out=SQHd[0:64, :], in_=SQd[0:64, :]), reads=['SQd'], writes=['SQHd'])
        S.op('pool', lambda h: h.tensor_tensor(out=SQLd[0:64, :], in0=SQd[0:64, :], in1=SQHd[0:64, :], op=ALU.subtract),
             reads=['SQd', 'SQHd'], writes=['SQLd'])

    def epi_d_final(row0, g):
        S.op('pe', lambda h: h.matmul(PBC[0:64, :], lhsT=ONESB[0:64, 0:64], rhs=SQHd[0:64, :], start=True, stop=False),
             reads=['ONESB', 'SQHd'], writes=['pbc'])
        S.op('pe', lambda h: h.matmul(PBC[0:64, :], lhsT=ONESB[0:64, 0:64], rhs=SQLd[0:64, :], start=False, stop=True),
             reads=['ONESB', 'SQLd'], writes=['pbc'])
        S.op('act', lambda h: h.activation(out=Rr[0:64, :], in_=PBC[0:64, :], func=AF.Ln, bias=EPS, scale=1.0 / 64),
             reads=['pbc'], writes=['Rr'])
        S.op('act', lambda h: h.activation(out=Rr[0:64, :], in_=Rr[0:64, :], func=AF.Exp, scale=-0.5), reads=['Rr'], writes=['Rr'])
        i = cnt['outb'] % 2
        cnt['outb'] += 1
        ob, obk = OUTB[i], ('outb', i)
        S.op('dve', lambda h: h.scalar_tensor_tensor(out=ob[0:64, :], in0=O1[0:64, :], scalar=SM[0:64, 6:7], in1=Rr[0:64, :],
                                                      op0=ALU.mult, op1=ALU.mult), reads=['O1', 'GSUB', 'Rr'], writes=[obk])
        store_out(ob, obk, row0, g)

    def vload(slot, c0):
        return [lambda h, j=j: h.dma_start(out=VS[slot][:, j * 32:(j + 1) * 32, 0:64],
                                           in_=Vall[j][:, c0:c0 + 64].rearrange("(t p) c -> p t c", p=128)) for j in range(4)]

    def issue(fns, keys, sem):
        S.op('sp', lambda h: [f(h) for f in fns], writes=keys, dma=sem, ndma=len(fns))

    def load_B(hh, slot):
        ks = KS[slot]
        fns = []
        for j in range(4):
            fns.append(lambda h, j=j: h.dma_start(out=ks[0:64, j * TOK:(j + 1) * TOK], in_=KTall[j][KROW_BN + hh * 64:KROW_BN + (hh + 1) * 64, :]))
            fns.append(lambda h, j=j: h.dma_start(out=ks[64:96, j * TOK:(j + 1) * TOK], in_=KTall[j][KROW_BR:KROW_BR + 32, :]))
        issue(fns, [('K', slot)], f"k{slot}")
        issue(vload(slot, VCOL_B + hh * 64), [('V', slot)], f"v{slot}")
        if B_FULLK:
            S.op('pool', lambda h: h.memset(QS[slot][:, :], 0.0), writes=[('Q', slot)])
        issue([lambda h: h.dma_start(out=QS[slot][0:96, :], in_=QT[QROW_B + hh * 96:QROW_B + (hh + 1) * 96, :])], [('Q', slot)], f"q{slot}")

    def load_C(kv, slot):
        ks = KS[slot]
        fns = []
        for b in (0, 64):
            for j in range(4):
                fns.append(lambda h, j=j, b=b: h.dma_start(out=ks[b:b + 64, j * TOK:(j + 1) * TOK],
                                                            in_=KTall[j][KROW_C + kv * 64:KROW_C + (kv + 1) * 64, :]))
        issue(fns, [('K', slot)], f"k{slot}")
        issue(vload(slot, VCOL_C + kv * 64), [('V', slot)], f"v{slot}")
        issue([lambda h: h.dma_start(out=QS[slot][0:128, :], in_=QT[QROW_C + kv * 128:QROW_C + (kv + 1) * 128, :])], [('Q', slot)], f"q{slot}")

    def load_D(hh, slot):
        ks, qs = KS[slot], QS[slot]
        fns, qf = [], []
        for m in range(2):
            r0 = KROW_D + (hh * 2 + m) * 32
            for j in range(4):
                fns.append(lambda h, j=j, m=m, r0=r0: h.dma_start(out=ks[m * 64:m * 64 + 32, j * TOK:(j + 1) * TOK], in_=KTall[j][r0:r0 + 32, :]))
            fns.append(lambda h, m=m, r0=r0: h.dma_start(out=ks[m * 64:m * 64 + 32, 4 * TOK:5 * TOK], in_=KTall[0][r0:r0 + 32, :]))
            fns.append(lambda h, m=m: h.dma_start(out=ks[m * 64 + 32:m * 64 + 36, :], in_=kaug[hh * 4:(hh + 1) * 4, :]))
        issue(fns, [('K', slot)], f"k{slot}")
        issue(vload(slot, VCOL_D + hh * 64), [('V', slot)], f"v{slot}")
        q0 = QROW_D + (hh * 2) * 32
        if D_FULLK:
            S.op('pool', lambda h: h.memset(qs[:, :], 0.0), writes=[('Q', slot)])
            qf.append(lambda h: h.dma_start(out=qs[0:32, :], in_=QT[q0:q0 + 32, :]))
            qf.append(lambda h: h.dma_start(out=qs[32:36, :], in_=qaug[hh * 4:(hh + 1) * 4, :]))
        else:
            for m in range(2):
                qf.append(lambda h, m=m: h.dma_start(out=qs[m * 64:m * 64 + 32, :], in_=QT[q0 + m * 32:q0 + m * 32 + 32, :]))
                qf.append(lambda h, m=m: h.dma_start(out=qs[m * 64 + 32:m * 64 + 36, :], in_=qaug[hh * 4:(hh + 1) * 4, :]))
        issue(qf, [('Q', slot)], f"q{slot}")

    def load_D_late(hh):
        q1 = QROW_D + (hh * 2 + 1) * 32
        S.op('pool', lambda h: h.memset(QD[:, :], 0.0), writes=['QD'])
        issue([lambda h: h.dma_start(out=QD[64:96, :], in_=QT[q1:q1 + 32, :]),
               lambda h: h.dma_start(out=QD[96:100, :], in_=qaug[hh * 4:(hh + 1) * 4, :])], ['QD'], "qd")

    def load_A(kv, slot):
        r0 = KROW_A + kv * 64
        c0 = VCOL_A + kv * 64
        KA, VA = KS[slot], VS[slot]
        fns = []
        for b in (0, 64):
            fns.append(lambda h, b=b: h.dma_start(out=KA[b:b + 64, 128:128 + TOK], in_=KTall[0][r0:r0 + 64, :]))
            fns.append(lambda h, b=b: h.dma_start(out=KA[b:b + 64, 0:128], in_=KTall[3][r0:r0 + 64, TOK - 128:TOK]))
            fns.append(lambda h, b=b: h.dma_start(out=KA[b:b + 64, 128 + TOK:256 + TOK], in_=KTall[1][r0:r0 + 64, 0:128]))
        issue(fns, [('K', slot)], f"k{slot}")
        vf = [lambda h: h.dma_start(out=VA[:, 1:33, 0:64], in_=Vall[0][:, c0:c0 + 64].rearrange("(t p) c -> p t c", p=128)),
              lambda h: h.dma_start(out=VA[:, 0, 0:64], in_=Vall[3][TOK - 128:TOK, c0:c0 + 64]),
              lambda h: h.dma_start(out=VA[:, 33, 0:64], in_=Vall[1][0:128, c0:c0 + 64])]
        issue(vf, [('V', slot)], f"v{slot}")
        issue([lambda h: h.dma_start(out=QS[slot][0:128, :], in_=QT[QROW_A + kv * 128:QROW_A + (kv + 1) * 128, :])], [('Q', slot)], f"q{slot}")

    groups = list(range(NG)) if AT_GROUPS is None else list(AT_GROUPS)

    def dense_steps(slot, maps, dtype_d=None):
        ks, vs, qs = KS[slot], VS[slot], QS[slot]
        rk = [('K', slot), ('Q', slot)]
        rv = [('V', slot), ('Vones', slot)]
        steps = []
        for g in groups:
            for mi, mp in enumerate(maps):
                po, pok = po_next()
                b, dk = mp['b'], mp['dk']
                qm = mp['q'] if mp.get('q') is not None else qs
                fullk = mp.get('fullk', False)
                rkm = rk + (['QD'] if mp.get('q') is not None else [])
                for t0 in range(0, 128, 2):
                    tiles = []
                    for t in (t0, t0 + 1):
                        kind, col = 'plain', t * 128
                        if dtype_d is not None and t < 32:
                            if 4 * g <= t < 4 * g + 4:
                                kind = 'diag'
                            elif t >= 4 * g + 4:
                                col = 4 * TOK + t * 128
                        tiles.append((t, kind, col))
                    st = {}

                    def s_fn(ps, psk, tiles=tiles, b=b, dk=dk, g=g, hd=dtype_d, qm=qm, fullk=fullk, rkm=rkm):
                        for u, (t, kind, col) in enumerate(tiles):
                            pu = ps[:, u * 512:(u + 1) * 512]
                            if kind == 'plain':
                                lo, hi = (0, 128) if fullk else (b, b + dk)
                                S.op('pe', lambda h, pu=pu, col=col, lo=lo, hi=hi: h.matmul(pu, lhsT=ks[lo:hi, col:col + 128],
                                                                                            rhs=qm[lo:hi, g * 512:(g + 1) * 512],
                                                                                            start=True, stop=True), reads=rkm, writes=[psk])
                            else:
                                j = t - 4 * g
                                S.op('pe', lambda h, pu=pu, col=col: h.matmul(pu, lhsT=ks[b:b + 32, col:col + 128],
                                                                              rhs=qm[b:b + 32, g * 512:(g + 1) * 512],
                                                                              start=True, stop=False), reads=rkm, writes=[psk])
                                S.op('pe', lambda h, pu=pu, j=j: h.matmul(pu, lhsT=NSI[:, 4 + hd, :], rhs=DD[:, j, 0, :],
                                                                          start=False, stop=False), reads=['CONST'], writes=[psk])
                                S.op('pe', lambda h, pu=pu, j=j: h.matmul(pu, lhsT=NSI[:, 4 + hd, :], rhs=DD[:, j, 1, :],
                                                                          start=False, stop=True), reads=['CONST'], writes=[psk])
                    st['s'] = s_fn

                    def e_fn(ps, psk, pt, ptk):
                        S.op('act', lambda h: h.activation(out=pt[:, :], in_=ps[:, :], func=AF.Exp), reads=[psk], writes=[ptk])
                    st['e'] = e_fn

                    def v_fn(pt, ptk, t0=t0, po=po, pok=pok):
                        for u in range(2):
                            t = t0 + u
                            S.op('pe', lambda h, t=t, u=u: h.matmul(po[0:96, :], lhsT=vs[:, t, :], rhs=pt[:, u * 512:(u + 1) * 512],
                                                                   start=(t == 0), stop=(t == 127)),
                                 reads=rv + [ptk], writes=[pok])
                    st['v'] = v_fn
                    if t0 == 126:
                        st['post'] = (lambda po=po, pok=pok, mp=mp, g=g: mp['epi'](po, pok, g))
                        if mp.get('final') is not None:
                            st['defer'] = (lambda mp=mp, g=g: mp['final'](g))
                    steps.append(st)
        return steps

    def window_steps(kv, slot):
        qs = QS[slot]
        KA, VA = KS[slot], VS[slot]
        steps = []
        for hl in range(2):
            hd = kv * 2 + hl
            b = hl * 64
            po = pok = None
            for i0 in range(0, NTT, 2):
                if AT_GROUPS is not None and (i0 // 4) not in AT_GROUPS:
                    continue
                if i0 % 4 == 0:
                    po, pok = po_next()
                st = {}

                def s_fn(ps, psk, i0=i0, b=b, hd=hd):
                    for u in range(2):
                        i = i0 + u
                        for jj in range(3):
                            mi = jj
                            if i == 0 and jj == 0:
                                mi = 3
                            if i == NTT - 1 and jj == 2:
                                mi = 4
                            pu = ps[:, u * 512 + jj * 128:u * 512 + (jj + 1) * 128]
                            S.op('pe', lambda h, i=i, jj=jj, pu=pu: h.matmul(pu, lhsT=KA[b:b + 64, (i + jj) * 128:(i + jj + 1) * 128],
                                                                             rhs=qs[b:b + 64, i * 128:(i + 1) * 128], start=True, stop=False),
                                 reads=[('K', slot), ('Q', slot)], writes=[psk])
                            S.op('pe', lambda h, jj=jj, pu=pu: h.matmul(pu, lhsT=NSI[:, hd, :], rhs=WDIST[:, jj, :], start=False, stop=False),
                                 reads=['CONST'], writes=[psk])
                            S.op('pe', lambda h, mi=mi, pu=pu: h.matmul(pu, lhsT=IDB[:], rhs=WMASK[:, mi, :], start=False, stop=True),
                                 reads=['CONST', 'identa'], writes=[psk])
                st['s'] = s_fn

                def e_fn(ps, psk, pt, ptk):
                    S.op('act', lambda h: h.activation(out=pt.rearrange("p (a b) -> p a b", b=512)[:, :, 0:384],
                                                       in_=ps.rearrange("p (a b) -> p a b", b=512)[:, :, 0:384], func=AF.Exp),
                         reads=[psk], writes=[ptk])
                st['e'] = e_fn

                def v_fn(pt, ptk, i0=i0, po=po, pok=pok):
                    for u in range(2):
                        i = i0 + u
                        for jj in range(3):
                            S.op('pe', lambda h, i=i, u=u, jj=jj: h.matmul(po[0:96, (i % 4) * 128:(i % 4 + 1) * 128], lhsT=VA[:, i + jj, :],
                                                                          rhs=pt[:, u * 512 + jj * 128:u * 512 + (jj + 1) * 128],
                                                                          start=(jj == 0), stop=(jj == 2)),
                                 reads=[('V', slot), ('Vones', slot), ptk], writes=[pok])
                st['v'] = v_fn
                if i0 % 4 == 2:
                    st['post'] = (lambda po=po, pok=pok, hd=hd, g=i0 // 4: epi_plain(po, pok, hd * 64, g, sinkcol=8 + hd))
                steps.append(st)
        return steps

    units = []
    for kv in range(2):
        units.append(('A%d' % kv, lambda slot, kv=kv: load_A(kv, slot), lambda slot, kv=kv: window_steps(kv, slot), None))
    for hh in range(4):
        units.append(('B%d' % hh, lambda slot, hh=hh: load_B(hh, slot),
                      lambda slot, hh=hh: dense_steps(slot, [dict(b=0, dk=96, fullk=B_FULLK, epi=lambda po, pok, g, hh=hh: epi_plain(po, pok, 256 + hh * 64, g))]), None))
    for kv in range(2):
        units.append(('C%d' % kv, lambda slot, kv=kv: load_C(kv, slot),
                      lambda slot, kv=kv: dense_steps(slot, [
                          dict(b=hl * 64, dk=64, epi=lambda po, pok, g, hd=kv * 2 + hl: epi_plain(po, pok, 512 + hd * 64, g))
                          for hl in range(2)]), None))
    for hh in range(4):
        units.append(('D%d' % hh, lambda slot, hh=hh: load_D(hh, slot),
                      lambda slot, hh=hh: dense_steps(slot, [
                          dict(b=0, dk=36, fullk=D_FULLK, epi=lambda po, pok, g: epi_d_first(po, pok)),
                          dict(b=64, dk=36, fullk=D_FULLK, q=(QD if D_FULLK else None), epi=lambda po, pok, g: epi_d_second(po, pok),
                               final=lambda g, hh=hh: epi_d_final(768 + hh * 64, g))], dtype_d=hh),
                      (lambda hh=hh: load_D_late(hh)) if D_FULLK else None))
    if AT_UNITS is not None:
        units = [u for u in units if u[0] in AT_UNITS]

    LOOK = 1
    DEFER = 3
    units[0][1](0)
    allsteps = []
    for ui, (name, loader, gen, late) in enumerate(units):
        steps = gen(ui % 2)
        if late is not None:
            steps[0]['pre0'] = late
        if ui + 1 < len(units):
            steps[min(LOOK + 1, len(steps) - 1)]['pre'] = (lambda nxt=units[ui + 1][1], slot=(ui + 1) % 2: nxt(slot))
        allsteps += steps
    n = len(allsteps)
    deferred = []
    for i in range(n + LOOK):
        if i < n:
            st = allsteps[i]
            b, b3 = i % 2, i % 3
            if 'pre0' in st:
                st['pre0']()
            if 'pre' in st:
                st['pre']()
            st['s'](PSS[b], ('pss', b))
            st['e'](PSS[b], ('pss', b), PT[b3], ('pt', b3))
        if i >= LOOK:
            st = allsteps[i - LOOK]
            b = (i - LOOK) % 3
            st['v'](PT[b], ('pt', b))
            if 'post' in st:
                st['post']()
            if 'defer' in st:
                deferred.append((i + DEFER, st['defer']))
        while deferred and deferred[0][0] <= i:
            deferred.pop(0)[1]()
    while deferred:
        deferred.pop(0)[1]()


def phase_o(C, xin, mixT, w_out, lng, lnb, xout, tag="o"):
    nc, S = C.nc, C.S
    P = {}
    WOUT = C.sb("WOUT" + tag, [128, 8, 1024], BF16)
    WST = [C.sb(f"WSTo{i}" + tag, [128, 1024], F32) for i in range(2)]
    MIX = [C.sb(f"MIX{i}" + tag, [128, 8, 512], BF16) for i in range(2)]
    P['XR'] = [C.sb(f"XR{i}" + tag, [128, 4, 1024], F32) for i in range(2)]
    P['G'] = C.sb("G" + tag, [128, 1, 1024], F32)
    P['B'] = C.sb("B" + tag, [128, 1, 1024], F32)
    P['ot'] = [C.sb(f"ot{i}" + tag, [128, 1024], F32) for i in range(2)]
    P['st'] = C.sb("st" + tag, [128, 4, 12], F32)
    P['mv'] = C.sb("mv" + tag, [128, 4, 2], F32)
    P['rstd'] = C.sb("rstd" + tag, [128, 4], F32)
    PY = [C.ps(f"py{i}" + tag, [128, 512], F32) for i in range(4)]
    S.op('sp', lambda h: [h.dma_start(out=P['G'][:, 0, :], in_=lng.partition_broadcast(128)),
                          h.dma_start(out=P['B'][:, 0, :], in_=lnb.partition_broadcast(128))],
         writes=['GB'], dma="gb", ndma=2)
    for k in range(8):
        wst = WST[k % 2]
        S.op('sp', lambda h, k=k, wst=wst: h.dma_start(out=wst[:], in_=w_out[k * 128:(k + 1) * 128, :]),
             writes=[('wst', k % 2)], dma=f"wst{k % 2}")
        S.op('dve', lambda h, k=k, wst=wst: h.tensor_copy(out=WOUT[:, k, :], in_=wst[:]),
             reads=[('wst', k % 2)], writes=['WOUT'])
    mix_r = mixT.rearrange("(k p) t -> p k t", p=128)
    n = 0
    for g in range(NG):
        XR = P['XR'][g % 2]
        rk = ('XR', g % 2)
        mx = MIX[g % 2]
        S.op('sp', lambda h, g=g, mx=mx: h.dma_start(out=mx[:], in_=mix_r[:, :, g * 512:(g + 1) * 512]),
             writes=[('mix', g % 2)], dma=f"mix{g % 2}")
        for t in range(4):
            tt = g * 4 + t
            S.op('sp', lambda h, t=t, tt=tt, XR=XR: h.dma_start(out=XR[:, t, :], in_=xin[tt * 128:(tt + 1) * 128, :]),
                 writes=[rk + (t,)], dma=f"xr{g % 2}{t}")
            S.op('act', lambda h, t=t, XR=XR: h.mul(out=XR[:, t, :], in_=XR[:, t, :], mul=ALPHA),
                 reads=[rk + (t,)], writes=[rk + (t,)])
            for hh in range(2):
                py = PY[n % 4]
                pk = ('py', n % 4)
                n += 1
                for k in range(8):
                    S.op('pe', lambda h, k=k, t=t, hh=hh, py=py, mx=mx: h.matmul(
                        py[:], lhsT=mx[:, k, t * 128:(t + 1) * 128], rhs=WOUT[:, k, hh * 512:(hh + 1) * 512],
                        start=(k == 0), stop=(k == 7)), reads=[('mix', g % 2), 'WOUT'], writes=[pk])
                S.op('dve', lambda h, t=t, hh=hh, py=py, XR=XR: h.tensor_tensor(
                    out=XR[:, t, hh * 512:(hh + 1) * 512], in0=py[:], in1=XR[:, t, hh * 512:(hh + 1) * 512], op=ALU.add),
                    reads=[pk, rk + (t,)], writes=[rk + (t,)])
        ln_tiles(C, P, g, xout, 0)


def _inv_freq(dim):
    return np.power(np.float32(10000.0), -(np.arange(0, dim, 2, dtype=np.float32) / np.float32(dim))).astype(np.float32)


def rope_tables(rank):
    pos = (rank * TOK + np.arange(TOK)).astype(np.int64)
    inv = _inv_freq(32)

    def cs(p):
        ang = (p.astype(np.float32)[:, None] * inv[None, :]).astype(np.float32).astype(np.float64)
        return np.cos(ang).T, np.sin(ang).T

    c_t, s_t = cs(pos)
    tbc = np.concatenate([c_t, c_t], 0).astype(np.float32)
    tbs = np.concatenate([-s_t, s_t], 0).astype(np.float32)
    c_r, s_r = cs(pos // 64)
    c_c, s_c = cs(pos % 64)
    c64 = np.concatenate([c_r, c_r, c_c, c_c], 0)
    s64 = np.concatenate([-s_r, s_r, -s_c, s_c], 0)
    tcc = np.concatenate([c64, c64], 0).astype(np.float32)
    tcs = np.concatenate([s64, s64], 0).astype(np.float32)
    return dict(tcc=np.ascontiguousarray(tcc), tcs=np.ascontiguousarray(tcs),
                tbc=np.ascontiguousarray(tbc), tbs=np.ascontiguousarray(tbs))


def _perm64(v):
    v = np.asarray(v)
    return np.concatenate([v[16:32], v[0:16], v[48:64], v[32:48]])


def par_table(mla_q_norm, mla_kv_norm, ax_q_norm, ax_k_norm):
    t = np.zeros((128, NPAR), np.float32)
    t[:, PAR_GQ0] = mla_q_norm[0:128]
    t[:, PAR_GQ1] = mla_q_norm[128:256]
    t[:, PAR_GKV] = mla_kv_norm
    t[:, PAR_AXQ] = np.concatenate([ax_q_norm, ax_q_norm])
    t[:, PAR_AXQP] = np.concatenate([_perm64(ax_q_norm), _perm64(ax_q_norm)])
    t[:, PAR_AXK] = np.concatenate([ax_k_norm, ax_k_norm])
    t[:, PAR_AXKP] = np.concatenate([_perm64(ax_k_norm), _perm64(ax_k_norm)])
    return t


def _bf(a):
    return np.ascontiguousarray(np.asarray(a, np.float32).astype(ml_dtypes.bfloat16))


def at_tables(rank):
    slopes = 2.0 ** (-(np.arange(8) + 1.0))
    t = np.arange(TOK)
    kaug = np.zeros((4, 4, KCOLS), np.float32)
    qaug = np.zeros((4, 4, TOK), np.float32)
    q_abs = rank * TOK + t
    for hd in range(4):
        s_ = slopes[4 + hd]
        qaug[hd, 0] = s_ * 128 * (q_abs // 128)
        qaug[hd, 1] = s_ * (q_abs % 128)
        qaug[hd, 2] = 1.0
        qaug[hd, 3] = 1.0
        for j in range(5):
            if j == 4:
                rj, sg = rank, -1.0
            else:
                rj = (rank + j) % 4
                sg = 1.0 if (j == 0 or rj < rank) else -1.0
            k_abs = rj * TOK + t
            sl = slice(j * TOK, (j + 1) * TOK)
            kaug[hd, 0, sl] = -sg
            kaug[hd, 1, sl] = -sg
            kaug[hd, 2, sl] = sg * s_ * 128 * (k_abs // 128)
            kaug[hd, 3, sl] = sg * s_ * (k_abs % 128)
    kr = np.arange(128)[:, None]
    qr = np.arange(512)[None, :]
    dd = np.zeros((128, 4, 2, 512), np.float32)
    for j in range(4):
        d = np.abs(qr - 128 * j - kr)
        dd[:, j, 0, :] = 16 * (d // 16)
        dd[:, j, 1, :] = d % 16
    nsi = np.zeros((128, 8, 128), np.float32)
    for i in range(8):
        nsi[:, i, :] = -slopes[i] * np.eye(128)
    q1 = np.arange(128)[None, :]
    wdist = np.zeros((128, 3, 128), np.float32)
    wmask = np.zeros((128, 5, 128), np.float32)
    for jj in range(3):
        d = np.abs(q1 - kr + 128 * (1 - jj))
        wdist[:, jj, :] = d
        wmask[:, jj, :] = np.where(d <= 128, 0.0, -1e30)
    wmask[:, 3, :] = wmask[:, 0, :] if rank > 0 else -1e30
    wmask[:, 4, :] = wmask[:, 2, :] if rank < 3 else -1e30
    return dict(kaug=_bf(kaug.reshape(16, KCOLS)), qaug=_bf(qaug.reshape(16, TOK)), ddiag=_bf(dd.reshape(128, -1)),
                nsi=_bf(nsi.reshape(128, -1)), wdist=_bf(wdist.reshape(128, -1)), wmask=_bf(wmask.reshape(128, -1)))


def lpar_table(sink, diff_lambda, diff_subln, l):
    t = np.zeros((128, NLP), np.float32)
    t[:, LP_SINK:LP_SINK + 4] = np.asarray(sink, np.float32)[None, :]
    t[:, LP_LAM:LP_LAM + 128] = np.asarray(diff_lambda, np.float32).reshape(1, 128)
    lam_init = 0.8 - 0.6 * math.exp(-0.3 * l)
    t[:, LP_LINIT] = lam_init
    t[:, LP_1MLINIT] = 1.0 - lam_init
    t[:, LP_SUBG] = np.concatenate([diff_subln, diff_subln])
    return t


def build_at_program():
    if 'at' in _CACHE:
        return _CACHE['at']
    C = Ctx()
    nc = C.nc
    ei = lambda n, sh, dt=BF16: nc.dram_tensor(n, list(sh), dt, kind="ExternalInput").ap()
    QT = ei("QT", [RQ, TOK])
    KTall = ei("KTall", [4, RK, TOK])
    Vall = ei("Vall", [4, TOK, RV])
    kaug, qaug = ei("kaug", [16, KCOLS]), ei("qaug", [16, TOK])
    ddiag, nsi = ei("ddiag", [128, 4 * 2 * 512]), ei("nsi", [128, 8 * 128])
    wdist, wmask = ei("wdist", [128, 3 * 128]), ei("wmask", [128, 5 * 128])
    lpar = ei("lpar", [128, NLP], F32)
    mixT = nc.dram_tensor("mixT", [D_MODEL, TOK], BF16, kind="ExternalOutput").ap()
    phase_at(C, QT, KTall, Vall, kaug, qaug, ddiag, nsi, wdist, wmask, lpar, mixT)
    C.finish()
    _CACHE['at'] = nc
    return nc


def run_at(pr_outs, inp, l):
    nc = build_at_program()
    lp = lpar_table(inp['win_sink'][l], inp['diff_lambda'][l], inp['diff_subln'][l], l)
    in_maps = []
    for c in range(8):
        b, r = c // 4, c % 4
        order = [b * 4 + (r + j) % 4 for j in range(4)]
        m = dict(QT=np.ascontiguousarray(pr_outs[c][0]),
                 KTall=np.ascontiguousarray(np.stack([pr_outs[o][1] for o in order])),
                 Vall=np.ascontiguousarray(np.stack([pr_outs[o][2] for o in order])), lpar=lp)
        m.update(at_tables(r))
        in_maps.append(m)
    import os
    res = run_bass_kernel_spmd(nc, in_maps, core_ids=list(range(8)), trace=bool(os.environ.get("AT_TRACE")))
    if os.environ.get("AT_TRACE"):
        print("AT exec_time_ns", res.exec_time_ns)
    return [r["mixT"] for r in res.results]


def build_pr_program():
    if 'pr' in _CACHE:
        return _CACHE['pr']
    C = Ctx()
    nc = C.nc
    ei = lambda n, sh, dt=F32: nc.dram_tensor(n, list(sh), dt, kind="ExternalInput").ap()
    eo = lambda n, sh, dt=BF16: nc.dram_tensor(n, list(sh), dt, kind="ExternalOutput").ap()
    xin = ei("xin", [TOK, D_MODEL])
    w_in = ei("w_in", [D_MODEL, N_IN])
    w_uq = ei("w_uq", [256, 384])
    w_ukv = ei("w_ukv", [128, 512])
    par = ei("par", [128, NPAR])
    tcc, tcs = ei("tcc", [128, TOK]), ei("tcs", [128, TOK])
    tbc, tbs = ei("tbc", [32, TOK]), ei("tbs", [32, TOK])
    QT, KT, V = eo("QT", [RQ, TOK]), eo("KT", [RK, TOK]), eo("V", [TOK, RV])
    phase_pr(C, xin, w_in, w_uq, w_ukv, par, tcc, tcs, tbc, tbs, QT, KT, V)
    C.finish()
    _CACHE['pr'] = nc
    return nc


def run_pr(xs, inp, l):
    nc = build_pr_program()
    par = par_table(inp['mla_q_norm'][l], inp['mla_kv_norm'][l], inp['ax_q_norm'][l], inp['ax_k_norm'][l])
    in_maps = []
    for c in range(8):
        m = dict(xin=xs[c], w_in=np.ascontiguousarray(inp['w_in'][l]), w_uq=np.ascontiguousarray(inp['mla_w_uq'][l]),
                 w_ukv=np.ascontiguousarray(inp['mla_w_ukv'][l]), par=par)
        m.update(rope_tables(c % 4))
        in_maps.append(m)
    res = run_bass_kernel_spmd(nc, in_maps, core_ids=list(range(8)))
    return [(r["QT"], r["KT"], r["V"]) for r in res.results]


def build_ffn_program():
    if 'ffn' in _CACHE:
        return _CACHE['ffn']
    C = Ctx()
    nc = C.nc
    xin = nc.dram_tensor("xin", [TOK, D_MODEL], F32, kind="ExternalInput").ap()
    wgu = nc.dram_tensor("wgu", [D_MODEL, 2 * D_FF], F32, kind="ExternalInput").ap()
    wd = nc.dram_tensor("wd", [D_FF, D_MODEL], F32, kind="ExternalInput").ap()
    lng = nc.dram_tensor("lng", [D_MODEL], F32, kind="ExternalInput").ap()
    lnb = nc.dram_tensor("lnb", [D_MODEL], F32, kind="ExternalInput").ap()
    xout = nc.dram_tensor("xout", [TOK, D_MODEL], F32, kind="ExternalOutput").ap()
    phase_ffn(C, xin, xout, wgu, wd, lng, lnb)
    C.finish()
    _CACHE['ffn'] = nc
    return nc


def run_ffn(xs, wgu, wd, g, b):
    nc = build_ffn_program()
    in_maps = [dict(xin=xs[c], wgu=wgu, wd=wd, lng=g, lnb=b) for c in range(8)]
    res = run_bass_kernel_spmd(nc, in_maps, core_ids=list(range(8)))
    return [r["xout"] for r in res.results]


def build_o_program():
    if 'o' in _CACHE:
        return _CACHE['o']
    C = Ctx()
    nc = C.nc
    xin = nc.dram_tensor("xin", [TOK, D_MODEL], F32, kind="ExternalInput").ap()
    mixT = nc.dram_tensor("mixT", [D_MODEL, TOK], BF16, kind="ExternalInput").ap()
    w_out = nc.dram_tensor("w_out", [D_MODEL, D_MODEL], F32, kind="ExternalInput").ap()
    lng = nc.dram_tensor("lng", [D_MODEL], F32, kind="ExternalInput").ap()
    lnb = nc.dram_tensor("lnb", [D_MODEL], F32, kind="ExternalInput").ap()
    xout = nc.dram_tensor("xout", [TOK, D_MODEL], F32, kind="ExternalOutput").ap()
    phase_o(C, xin, mixT, w_out, lng, lnb, xout)
    C.finish()
    _CACHE['o'] = nc
    return nc


def run_o(xs, mixs, w_out, g, b):
    nc = build_o_program()
    in_maps = [dict(xin=xs[c], mixT=np.ascontiguousarray(mixs[c]), w_out=w_out, lng=g, lnb=b) for c in range(8)]
    res = run_bass_kernel_spmd(nc, in_maps, core_ids=list(range(8)))
    return [r["xout"] for r in res.results]


def _decl_inputs(nc, with_at, with_o_f2, with_f1_pr):
    ei = lambda n, sh, dt=F32: nc.dram_tensor(n, list(sh), dt, kind="ExternalInput").ap()
    T = {}
    T['x_in'] = ei("x_in", [TOK, D_MODEL])
    if with_at:
        T['QT'] = ei("QT", [RQ, TOK], BF16)
        T['KTall'] = ei("KTall", [4, RK, TOK], BF16)
        T['Vall'] = ei("Vall", [4, TOK, RV], BF16)
        T['kaug'], T['qaug'] = ei("kaug", [16, KCOLS], BF16), ei("qaug", [16, TOK], BF16)
        T['ddiag'], T['nsi'] = ei("ddiag", [128, 4 * 2 * 512], BF16), ei("nsi", [128, 8 * 128], BF16)
        T['wdist'], T['wmask'] = ei("wdist", [128, 3 * 128], BF16), ei("wmask", [128, 5 * 128], BF16)
        T['lpar'] = ei("lpar", [128, NLP])
    if with_o_f2:
        T['o_w'] = ei("o_w", [D_MODEL, D_MODEL])
        T['o_g'], T['o_b'] = ei("o_g", [D_MODEL]), ei("o_b", [D_MODEL])
        T['f2_wgu'], T['f2_wd'] = ei("f2_wgu", [D_MODEL, 2 * D_FF]), ei("f2_wd", [D_FF, D_MODEL])
        T['f2_g'], T['f2_b'] = ei("f2_g", [D_MODEL]), ei("f2_b", [D_MODEL])
    if with_f1_pr:
        T['f1_wgu'], T['f1_wd'] = ei("f1_wgu", [D_MODEL, 2 * D_FF]), ei("f1_wd", [D_FF, D_MODEL])
        T['f1_g'], T['f1_b'] = ei("f1_g", [D_MODEL]), ei("f1_b", [D_MODEL])
        T['w_in'] = ei("w_in", [D_MODEL, N_IN])
        T['w_uq'], T['w_ukv'] = ei("w_uq", [256, 384]), ei("w_ukv", [128, 512])
        T['par'] = ei("par", [128, NPAR])
        T['tcc'], T['tcs'] = ei("tcc", [128, TOK]), ei("tcs", [128, TOK])
        T['tbc'], T['tbs'] = ei("tbc", [32, TOK]), ei("tbs", [32, TOK])
    return T


def build_program(kind):
    if kind in _CACHE:
        return _CACHE[kind]
    C = Ctx()
    nc = C.nc
    with_at = kind in ('mid', 'last')
    with_f1 = kind in ('first', 'mid')
    T = _decl_inputs(nc, with_at, with_at, with_f1)
    eo = lambda n, sh, dt: nc.dram_tensor(n, list(sh), dt, kind="ExternalOutput").ap()
    x_cur = T['x_in']
    first = True
    if with_at:
        mixT = C.dram("mixT_i", [D_MODEL, TOK], BF16)
        phase_at(C, T['QT'], T['KTall'], T['Vall'], T['kaug'], T['qaug'], T['ddiag'], T['nsi'], T['wdist'], T['wmask'],
                 T['lpar'], mixT)
        C.new_phase()
        x2 = C.dram("x2_i", [TOK, D_MODEL], F32)
        phase_o(C, x_cur, mixT, T['o_w'], T['o_g'], T['o_b'], x2)
        C.new_phase()
        x3 = eo("x_out", [TOK, D_MODEL], F32) if kind == 'last' else C.dram("x3_i", [TOK, D_MODEL], F32)
        phase_ffn(C, x2, x3, T['f2_wgu'], T['f2_wd'], T['f2_g'], T['f2_b'], tag="f2")
        x_cur = x3
        first = False
    if with_f1:
        if not first:
            C.new_phase()
        x1 = eo("x_out", [TOK, D_MODEL], F32)
        phase_ffn(C, x_cur, x1, T['f1_wgu'], T['f1_wd'], T['f1_g'], T['f1_b'], tag="f1")
        C.new_phase()
        QT, KT, V = eo("QTo", [RQ, TOK], BF16), eo("KTo", [RK, TOK], BF16), eo("Vo", [TOK, RV], BF16)
        phase_pr(C, x1, T['w_in'], T['w_uq'], T['w_ukv'], T['par'], T['tcc'], T['tcs'], T['tbc'], T['tbs'], QT, KT, V)
    C.finish()
    _CACHE[kind] = nc
    return nc


_TAB = {}


def _tables():
    if not _TAB:
        _TAB['rope'] = [rope_tables(r) for r in range(4)]
        _TAB['at'] = [at_tables(r) for r in range(4)]
    return _TAB


def kernel(x, w_in, win_sink, mla_q_norm, mla_w_uq, mla_kv_norm, mla_w_ukv, ax_q_norm, ax_k_norm,
           diff_lambda, diff_subln, w_out, ffn_w_gu, ffn_w_down, ln_g, ln_b):
    inp = dict(x=x, w_in=w_in, win_sink=win_sink, mla_q_norm=mla_q_norm, mla_w_uq=mla_w_uq, mla_kv_norm=mla_kv_norm,
               mla_w_ukv=mla_w_ukv, ax_q_norm=ax_q_norm, ax_k_norm=ax_k_norm, diff_lambda=diff_lambda,
               diff_subln=diff_subln, w_out=w_out, ffn_w_gu=ffn_w_gu, ffn_w_down=ffn_w_down, ln_g=ln_g, ln_b=ln_b)
    inp = {k: np.asarray(v, np.float32) for k, v in inp.items()}
    ca = np.ascontiguousarray
    tabs = _tables()

    def f1pr_inputs(l, r):
        m = dict(f1_wgu=ca(inp['ffn_w_gu'][l, 0]), f1_wd=ca(inp['ffn_w_down'][l, 0]), f1_g=ca(inp['ln_g'][l, 0]),
                 f1_b=ca(inp['ln_b'][l, 0]), w_in=ca(inp['w_in'][l]), w_uq=ca(inp['mla_w_uq'][l]),
                 w_ukv=ca(inp['mla_w_ukv'][l]),
                 par=par_table(inp['mla_q_norm'][l], inp['mla_kv_norm'][l], inp['ax_q_norm'][l], inp['ax_k_norm'][l]))
        m.update(tabs['rope'][r])
        return m

    def at_inputs(l, c, qkv):
        b, r = c // 4, c % 4
        order = [b * 4 + (r + j) % 4 for j in range(4)]
        m = dict(QT=ca(qkv[c][0]), KTall=ca(np.stack([qkv[o][1] for o in order])),
                 Vall=ca(np.stack([qkv[o][2] for o in order])),
                 lpar=lpar_table(inp['win_sink'][l], inp['diff_lambda'][l], inp['diff_subln'][l], l),
                 o_w=ca(inp['w_out'][l]), o_g=ca(inp['ln_g'][l, 1]), o_b=ca(inp['ln_b'][l, 1]),
                 f2_wgu=ca(inp['ffn_w_gu'][l, 1]), f2_wd=ca(inp['ffn_w_down'][l, 1]), f2_g=ca(inp['ln_g'][l, 2]),
                 f2_b=ca(inp['ln_b'][l, 2]))
        m.update(tabs['at'][r])
        return m

    xs = [ca(inp['x'][c // 4, (c % 4) * TOK:(c % 4 + 1) * TOK]) for c in range(8)]
    nc = build_program('first')
    res = run_bass_kernel_spmd(nc, [dict(x_in=xs[c], **f1pr_inputs(0, c % 4)) for c in range(8)], core_ids=list(range(8)))
    xs = [r["x_out"] for r in res.results]
    qkv = [(r["QTo"], r["KTo"], r["Vo"]) for r in res.results]
    for l in range(DEPTH):
        lastl = (l == DEPTH - 1)
        nc = build_program('last' if lastl else 'mid')
        in_maps = []
        for c in range(8):
            m = dict(x_in=xs[c], **at_inputs(l, c, qkv))
            if not lastl:
                m.update(f1pr_inputs(l + 1, c % 4))
            in_maps.append(m)
        res = run_bass_kernel_spmd(nc, in_maps, core_ids=list(range(8)))
        xs = [r["x_out"] for r in res.results]
        if not lastl:
            qkv = [(r["QTo"], r["KTo"], r["Vo"]) for r in res.results]
    out = np.zeros((2, SEQ, D_MODEL), np.float32)
    for c in range(8):
        out[c // 4, (c % 4) * TOK:(c % 4 + 1) * TOK] = xs[c]
    return out
```

```python
import contextlib
import math
import numpy as np
import ml_dtypes
import concourse.bass as bass
import concourse.mybir as mybir
from concourse.bass_utils import run_bass_kernel_spmd

F32 = mybir.dt.float32
BF16 = mybir.dt.bfloat16
AF = mybir.ActivationFunctionType
ALU = mybir.AluOpType

D_MODEL = 1024
SEQ = 16384
DEPTH = 4
D_FF = 2816
NFC = D_FF // 128
TOK = 4096
NTT = TOK // 128
NG = TOK // 512
ALPHA = (2 * DEPTH) ** 0.25
EPS = 1e-5
N_IN = 2208

ENGS = ("pe", "act", "dve", "pool", "sp")
_CACHE = {}


class Sched:
    def __init__(self, nc):
        self.nc = nc
        self.ops = {e: [] for e in ENGS}
        self.lastw = {}
        self.readers = {}
        self.waited = {e: {} for e in ENGS}
        self.dma_cnt = {}
        self.dma_sems = []

    def _add_dep(self, eng, deps, tok):
        if tok is None:
            return
        kind, src, idx = tok
        if kind == 'eng' and src == eng and eng in ('pe', 'sp'):
            return
        w = self.waited[eng]
        k = (kind, src)
        if w.get(k, -1) >= idx:
            return
        w[k] = idx
        deps.append(tok)

    def op(self, eng, fn, reads=(), writes=(), dma=None, ndma=1):
        deps = []
        for r in reads:
            self._add_dep(eng, deps, self.lastw.get(r))
        for w in writes:
            self._add_dep(eng, deps, self.lastw.get(w))
            for t in self.readers.get(w, ()):
                self._add_dep(eng, deps, t)
        idx = len(self.ops[eng])
        self.ops[eng].append(dict(fn=fn, deps=deps, sig=False, dma=dma, ndma=ndma))
        if dma is not None:
            if dma not in self.dma_cnt:
                self.dma_cnt[dma] = 0
                self.dma_sems.append(dma)
            self.dma_cnt[dma] += ndma
            tok = ('dma', dma, self.dma_cnt[dma])
        else:
            tok = ('eng', eng, idx)
        for r in reads:
            self.readers.setdefault(r, []).append(tok)
        for w in writes:
            self.lastw[w] = tok
            self.readers[w] = []
        return tok

    def prepare(self):
        for e in ENGS:
            for rec in self.ops[e]:
                for kind, src, idx in rec['deps']:
                    if kind == 'eng':
                        self.ops[src][idx]['sig'] = True
            for rec in reversed(self.ops[e]):
                if rec['dma'] is None:
                    rec['sig'] = True
                    break
        self.sigcount = {}
        for e in ENGS:
            c = 0
            arr = []
            for rec in self.ops[e]:
                if rec['sig']:
                    c += 1
                arr.append(c)
            self.sigcount[e] = arr

    def run(self, e, h, esem, dpool, ebase, dbase):
        dsem = {k: dpool[i] for i, k in enumerate(self.dma_sems)}
        db = {k: dbase[i] for i, k in enumerate(self.dma_sems)}
        for rec in self.ops[e]:
            for kind, src, idx in rec['deps']:
                if kind == 'eng':
                    h.wait_ge(esem[src], ebase[src] + self.sigcount[src][idx])
                else:
                    h.wait_ge(dsem[src], db[src] + 16 * idx)
            r = rec['fn'](h)
            if rec['dma'] is not None:
                rl = r if isinstance(r, (list, tuple)) else [r]
                assert len(rl) == rec['ndma'], (len(rl), rec['ndma'])
                for ins in rl:
                    ins.then_inc(dsem[rec['dma']], 16)
            elif rec['sig']:
                r.then_inc(esem[e], 1)
        for e2 in ENGS:
            if self.sigcount[e2] and self.sigcount[e2][-1] > 0:
                h.wait_ge(esem[e2], ebase[e2] + self.sigcount[e2][-1])
        for k in self.dma_sems:
            h.wait_ge(dsem[k], db[k] + 16 * self.dma_cnt[k])


NUM_DEV = None
ARENA_WORDS = 48 * 1024 - 512


class Ctx:
    def __init__(self):
        self.nc = bass.Bass("TRN2", target_bir_lowering=False, num_devices=NUM_DEV)
        self.st = contextlib.ExitStack()
        self.arena = self.st.enter_context(self.nc.sbuf_tensor("arena", [128, ARENA_WORDS], F32))
        self.banks = [self.st.enter_context(self.nc.psum_tensor(f"bank{i}", [128, 512], F32)) for i in range(8)]
        self.phases = []
        self.new_phase()

    def new_phase(self):
        self.S = Sched(self.nc)
        self.phases.append(self.S)
        self.off = 0
        self.bank = 0

    @staticmethod
    def _view(ap, shape, dt):
        n = int(np.prod(shape[1:]))
        if dt != F32:
            ap = ap.bitcast(dt)
        ap = ap[:, 0:n]
        if len(shape) == 3:
            ap = ap.rearrange("p (a b) -> p a b", b=shape[2])
        elif len(shape) == 4:
            ap = ap.rearrange("p (a b c) -> p a b c", b=shape[2], c=shape[3])
        return ap

    def sb(self, name, shape, dt):
        assert shape[0] == 128
        n = int(np.prod(shape[1:]))
        words = (n * (2 if dt == BF16 else 4) + 3) // 4
        assert self.off + words <= ARENA_WORDS, (name, self.off, words)
        ap = self.arena[:, self.off:self.off + words]
        self.off += words
        return self._view(ap, shape, dt)

    def ps(self, name, shape, dt):
        assert self.bank < 8, name
        ap = self.banks[self.bank][:, :]
        self.bank += 1
        return self._view(ap, shape, dt)

    def dram(self, name, shape, dt, kind="Internal"):
        return self.nc.dram_tensor(name, list(shape), dt, kind=kind).ap()

    def finish(self):
        nc = self.nc
        for S in self.phases:
            S.prepare()
        ndp = max(len(S.dma_sems) for S in self.phases)
        nph = len(self.phases)
        with contextlib.ExitStack() as st:
            esem = {e: st.enter_context(nc.semaphore("s_" + e)) for e in ENGS}
            dpool = [st.enter_context(nc.semaphore(f"d_{i}")) for i in range(ndp)]
            block = st.enter_context(nc.Block())
            ebases, dbases = [], []
            eb = {e: 0 for e in ENGS}
            dbv = [0] * ndp
            for S in self.phases:
                ebases.append(dict(eb))
                dbases.append(list(dbv))
                for e in ENGS:
                    eb[e] += S.sigcount[e][-1] if S.sigcount[e] else 0
                for i, k in enumerate(S.dma_sems):
                    dbv[i] += 16 * S.dma_cnt[k]

            def run(e, h):
                for pi, S in enumerate(self.phases):
                    S.run(e, h, esem, dpool, ebases[pi], dbases[pi])

            @block.tensor
            def _(h):
                run('pe', h)

            @block.scalar
            def _(h):
                run('act', h)

            @block.vector
            def _(h):
                run('dve', h)

            @block.gpsimd
            def _(h):
                run('pool', h)

            @block.sync
            def _(h):
                run('sp', h)
        self.st.close()
        return nc


def make_ident(C, name="ident"):
    S = C.S
    idf = C.sb(name + "_f", [128, 128], F32)
    idb = C.sb(name, [128, 128], BF16)
    S.op('pool', lambda h: h.memset(idf[:], 1.0), writes=[name + '_f'])
    S.op('pool', lambda h: h.affine_select(out=idf[:], in_=idf[:], pattern=[[-1, 128]],
                                            compare_op=ALU.is_equal, fill=0.0, base=0,
                                            channel_multiplier=1),
         reads=[name + '_f'], writes=[name + '_f'])
    S.op('pool', lambda h: h.tensor_copy(out=idb[:], in_=idf[:]), reads=[name + '_f'], writes=[name])
    return idb, idf


def load_xT_group(C, P, g, xin, scale_resid=True):
    S = C.S
    XR = P['XR'][g % 2]
    rk = ('XR', g % 2)
    for t in range(4):
        tt = g * 4 + t
        S.op('sp', lambda h, t=t, tt=tt: h.dma_start(out=XR[:, t, :], in_=xin[tt * 128:(tt + 1) * 128, :]),
             writes=[rk + (t,)], dma=f"xr{g % 2}{t}")
        xb = P['xb'][t % 2]
        S.op('dve', lambda h, t=t, xb=xb: h.tensor_copy(out=xb[:], in_=XR[:, t, :]),
             reads=[rk + (t,)], writes=[('xb', t % 2)])
        if scale_resid:
            S.op('act', lambda h, t=t: h.mul(out=XR[:, t, :], in_=XR[:, t, :], mul=ALPHA),
                 reads=[rk + (t,)], writes=[rk + (t,)])
        pT = P['pT']
        for k in range(8):
            S.op('pe', lambda h, k=k, xb=xb: h.transpose(out=pT[:, k, :], in_=xb[:, k * 128:(k + 1) * 128],
                                                          identity=P['ident'][:]),
                 reads=[('xb', t % 2), 'ident'], writes=['pT'])
        S.op('act', lambda h, t=t: h.copy(out=P['XT'][:, :, t * 128:(t + 1) * 128], in_=pT[:, :, :]),
             reads=['pT'], writes=['XT'])


def ln_tiles(C, P, g, xout, lnidx):
    S = C.S
    XR = P['XR'][g % 2]
    rk = ('XR', g % 2)
    st, mv, rstd = P['st'], P['mv'], P['rstd']
    for t in range(4):
        for hh in range(2):
            S.op('dve', lambda h, t=t, hh=hh: h.bn_stats(out=st[:, t, hh * 6:(hh + 1) * 6],
                                                          in_=XR[:, t, hh * 512:(hh + 1) * 512]),
                 reads=[rk + (t,)], writes=[('st', t)])
        S.op('dve', lambda h, t=t: h.bn_aggr(out=mv[:, t, :], in_=st[:, t, :]),
             reads=[('st', t)], writes=['mv'])
    S.op('dve', lambda h: h.tensor_scalar_add(out=rstd[:, :], in0=mv[:, :, 1], scalar1=EPS),
         reads=['mv'], writes=['rstd'])
    S.op('act', lambda h: h.sqrt(out=rstd[:, :], in_=rstd[:, :]), reads=['rstd'], writes=['rstd'])
    S.op('dve', lambda h: h.reciprocal(out=rstd[:, :], in_=rstd[:, :]), reads=['rstd'], writes=['rstd'])
    for t in range(4):
        tt = g * 4 + t
        ot = P['ot'][t % 2]
        ok = ('ot', t % 2)
        S.op('dve', lambda h, t=t, ot=ot: h.tensor_scalar(out=ot[:], in0=XR[:, t, :], scalar1=mv[:, t, 0:1],
                                                           scalar2=rstd[:, t:t + 1], op0=ALU.subtract,
                                                           op1=ALU.mult),
             reads=[rk + (t,), 'mv', 'rstd'], writes=[ok])
        S.op('pool', lambda h, ot=ot: h.tensor_tensor(out=ot[:], in0=ot[:], in1=P['G'][:, lnidx, :], op=ALU.mult),
             reads=[ok, 'GB'], writes=[ok])
        S.op('pool', lambda h, ot=ot: h.tensor_tensor(out=ot[:], in0=ot[:], in1=P['B'][:, lnidx, :], op=ALU.add),
             reads=[ok, 'GB'], writes=[ok])
        S.op('pool', lambda h, tt=tt, ot=ot: h.dma_start(out=xout[tt * 128:(tt + 1) * 128, :], in_=ot[:]),
             reads=[ok], dma=f"xo{t % 2}")


def phase_ffn(C, xin, xout, wgu, wd, lng, lnb, tag="f", dbg=None):
    nc, S = C.nc, C.S
    P = {}
    P['ident'], _ = make_ident(C, "ident" + tag)
    S.lastw['ident'] = S.lastw["ident" + tag]
    WD = C.sb("WD" + tag, [128, NFC, 1024], BF16)
    P['XT'] = C.sb("XT" + tag, [128, 8, 512], BF16)
    AT = C.sb("AT" + tag, [128, NFC, 512], BF16)
    P['XR'] = [C.sb(f"XR{i}" + tag, [128, 4, 1024], F32) for i in range(2)]
    WG = [C.sb(f"WG{i}" + tag, [128, 8, 256], BF16) for i in range(3)]
    STG = [C.sb(f"STG{i}" + tag, [128, 8, 256], F32) for i in range(2)]
    WDS = [C.sb(f"WDS{i}" + tag, [128, 1024], F32) for i in range(2)]
    P['G'] = C.sb("G" + tag, [128, 1, 1024], F32)
    P['B'] = C.sb("B" + tag, [128, 1, 1024], F32)
    P['xb'] = [C.sb(f"xb{i}" + tag, [128, 1024], BF16) for i in range(2)]
    SG = [C.sb(f"sg{i}" + tag, [128, 512], F32) for i in range(2)]
    P['ot'] = [C.sb(f"ot{i}" + tag, [128, 1024], F32) for i in range(2)]
    P['st'] = C.sb("st" + tag, [128, 4, 12], F32)
    P['mv'] = C.sb("mv" + tag, [128, 4, 2], F32)
    P['rstd'] = C.sb("rstd" + tag, [128, 4], F32)
    P['pT'] = C.ps("pT" + tag, [128, 8, 128], BF16)
    PG = [C.ps(f"pg{i}" + tag, [128, 512], F32) for i in range(2)]
    PU = [C.ps(f"pu{i}" + tag, [128, 512], F32) for i in range(2)]
    PY = [C.ps(f"py{i}" + tag, [128, 512], F32) for i in range(2)]
    WGS = C.dram("wgs" + tag, [NFC, 128, 2048], BF16)

    S.op('sp', lambda h: [h.dma_start(out=P['G'][:, 0, :], in_=lng.partition_broadcast(128)),
                          h.dma_start(out=P['B'][:, 0, :], in_=lnb.partition_broadcast(128))],
         writes=['GB'], dma="gb", ndma=2)
    wgu_r = wgu.rearrange("(k p) n -> p k n", p=128)
    for c in range(NFC):
        stg = STG[c % 2]
        S.op('sp', lambda h, c=c, stg=stg: [
            h.dma_start(out=stg[:, :, 0:128], in_=wgu_r[:, :, c * 128:(c + 1) * 128]),
            h.dma_start(out=stg[:, :, 128:256], in_=wgu_r[:, :, D_FF + c * 128:D_FF + (c + 1) * 128])],
            writes=[('stg', c % 2)], dma=f"stg{c % 2}", ndma=2)
        wg = WG[c % 3]
        S.op('pool', lambda h, stg=stg, wg=wg: h.tensor_copy(out=wg[:], in_=stg[:]),
             reads=[('stg', c % 2)], writes=[('wg', c % 3)])
        S.op('pool', lambda h, c=c, wg=wg: h.dma_start(out=WGS[c].rearrange("p (k n) -> p k n", k=8), in_=wg[:]),
             reads=[('wg', c % 3)], writes=[('wgs', c)], dma=f"wgsw{c % 3}")
    for c in range(NFC):
        wds = WDS[c % 2]
        S.op('sp', lambda h, c=c, wds=wds: h.dma_start(out=wds[:], in_=wd[c * 128:(c + 1) * 128, :]),
             writes=[('wds', c % 2)], dma=f"wds{c % 2}")
        S.op('dve', lambda h, c=c, wds=wds: h.tensor_copy(out=WD[:, c, :], in_=wds[:]),
             reads=[('wds', c % 2)], writes=['WD'])

    for g in range(NG):
        load_xT_group(C, P, g, xin)
        XR = P['XR'][g % 2]
        rk = ('XR', g % 2)
        for c in range(NFC):
            wg = WG[c % 3]
            S.op('sp', lambda h, c=c, wg=wg: h.dma_start(out=wg[:], in_=WGS[c].rearrange("p (k n) -> p k n", k=8)),
                 reads=[('wgs', c)], writes=[('wg', c % 3)], dma=f"wg{c % 3}")
            pg, pu = PG[c % 2], PU[c % 2]
            for k in range(8):
                S.op('pe', lambda h, k=k, wg=wg, pg=pg: h.matmul(pg[:], lhsT=wg[:, k, 0:128], rhs=P['XT'][:, k, :],
                                                                 start=(k == 0), stop=(k == 7)),
                     reads=[('wg', c % 3), 'XT'], writes=[('pg', c % 2)])
            for k in range(8):
                S.op('pe', lambda h, k=k, wg=wg, pu=pu: h.matmul(pu[:], lhsT=wg[:, k, 128:256], rhs=P['XT'][:, k, :],
                                                                 start=(k == 0), stop=(k == 7)),
                     reads=[('wg', c % 3), 'XT'], writes=[('pu', c % 2)])
            sg = SG[c % 2]
            S.op('act', lambda h, pg=pg, sg=sg: h.activation(out=sg[:], in_=pg[:], func=AF.Silu),
                 reads=[('pg', c % 2)], writes=[('sg', c % 2)])
            S.op('dve', lambda h, c=c, sg=sg, pu=pu: h.tensor_tensor(out=AT[:, c, :], in0=sg[:], in1=pu[:], op=ALU.mult),
                 reads=[('sg', c % 2), ('pu', c % 2)], writes=['AT'])
        if dbg is not None and g == 0:
            S.op('sp', lambda h: h.dma_start(out=dbg['xt'].rearrange("p (k n) -> p k n", k=8), in_=P['XT'][:]), reads=['XT'], dma='dbg1')
            S.op('sp', lambda h: h.dma_start(out=dbg['at'].rearrange("p (k n) -> p k n", k=NFC), in_=AT[:]), reads=['AT'], dma='dbg2')
            S.op('sp', lambda h: h.dma_start(out=dbg['wd'].rearrange("p (k n) -> p k n", k=NFC), in_=WD[:]), reads=['WD'], dma='dbg3')
            S.op('sp', lambda h: h.dma_start(out=dbg['wg'].rearrange("p (k n) -> p k n", k=8), in_=WG[(NFC - 1) % 3][:]), reads=[('wg', (NFC - 1) % 3)], dma='dbg4')
        for t in range(4):
            for hh in range(2):
                py = PY[hh]
                for c in range(NFC):
                    S.op('pe', lambda h, c=c, t=t, hh=hh, py=py: h.matmul(
                        py[:], lhsT=AT[:, c, t * 128:(t + 1) * 128], rhs=WD[:, c, hh * 512:(hh + 1) * 512],
                        start=(c == 0), stop=(c == NFC - 1)),
                        reads=['AT', 'WD'], writes=[('py', hh)])
                S.op('dve', lambda h, t=t, hh=hh, py=py, XR=XR: h.scalar_tensor_tensor(
                    out=XR[:, t, hh * 512:(hh + 1) * 512], in0=py[:], scalar=0.5,
                    in1=XR[:, t, hh * 512:(hh + 1) * 512], op0=ALU.mult, op1=ALU.add),
                    reads=[('py', hh), rk + (t,)], writes=[rk + (t,)])
        ln_tiles(C, P, g, xout, 0)


QROW_A, QROW_B, QROW_C, QROW_D, RQ = 0, 256, 640, 896, 1152
KROW_A, KROW_BN, KROW_BR, KROW_C, KROW_D, RK = 0, 128, 384, 416, 544, 800
VCOL_A, VCOL_C, VCOL_D, VCOL_B, RV = 0, 128, 256, 512, 768
SC_A = 64 ** -0.5
SC_B = 96 ** -0.5
SC_C = 64 ** -0.5
SC_D = 32 ** -0.5
PR_SKIP = set()
PAR_GQ0, PAR_GQ1, PAR_GKV, PAR_AXQ, PAR_AXQP, PAR_AXK, PAR_AXKP = 0, 1, 2, 3, 4, 5, 6
NPAR = 8


def phase_pr(C, xin, w_in, w_uq, w_ukv, par, tcc, tcs, tbc, tbs, QT, KT, V, tag="p"):
    nc, S = C.nc, C.S
    P = {}
    P['ident'], _ = make_ident(C, "ident" + tag)
    S.lastw['ident'] = S.lastw["ident" + tag]
    P['XT'] = C.sb("XT" + tag, [128, 8, 512], BF16)
    P['XR'] = [C.sb(f"XR{i}" + tag, [128, 4, 1024], F32) for i in range(2)]
    P['xb'] = [C.sb(f"xb{i}" + tag, [128, 1024], BF16) for i in range(2)]
    P['pT'] = C.ps("pT" + tag, [128, 8, 128], BF16)
    WIN = C.sb("WIN" + tag, [128, 8, N_IN], BF16)
    WINP = C.sb("WINP" + tag, [128, 8, 480], BF16)
    WINV = C.sb("WINV" + tag, [128, 8, 512], BF16)
    WST = [C.sb(f"WST{i}" + tag, [128, N_IN], F32) for i in range(2)]
    WUQ = C.sb("WUQ" + tag, [128, 2, 384], BF16)
    WUQP = C.sb("WUQP" + tag, [128, 2, 384], BF16)
    WUKVK = C.sb("WUKVK" + tag, [128, 256], BF16)
    WUKVV = C.sb("WUKVV" + tag, [128, 256], BF16)
    PAR = C.sb("PAR" + tag, [128, NPAR], F32)
    ONESF = C.sb("ONESF" + tag, [128, 128], BF16)
    BDF = C.sb("BDF" + tag, [128, 128], BF16)
    SQH = [C.sb(f"SQH{i}" + tag, [128, 512], BF16) for i in range(2)]
    SQL = [C.sb(f"SQL{i}" + tag, [128, 512], BF16) for i in range(2)]
    TCC = [C.sb(f"TCC{i}" + tag, [128, 512], F32) for i in range(2)]
    TCS = [C.sb(f"TCS{i}" + tag, [128, 512], F32) for i in range(2)]
    TBC = [C.sb(f"TBC{i}" + tag, [128, 512], F32) for i in range(2)]
    TBS = [C.sb(f"TBS{i}" + tag, [128, 512], F32) for i in range(2)]
    SQ = [C.sb(f"SQ{i}" + tag, [128, 512], F32) for i in range(2)]
    R = C.sb("R" + tag, [128, 512], F32)
    T1 = C.sb("T1" + tag, [128, 512], F32)
    T2 = C.sb("T2" + tag, [128, 512], F32)
    CQN = C.sb("CQN" + tag, [128, 2, 512], BF16)
    CKVN = C.sb("CKVN" + tag, [128, 512], BF16)
    OB = [C.sb(f"OB{i}" + tag, [128, 512], BF16) for i in range(4)]
    VO = [C.sb(f"VO{i}" + tag, [128, RV], BF16) for i in range(2)]
    PA = [C.ps(f"pa{i}" + tag, [128, 512], F32) for i in range(4)]
    PB = [C.ps(f"pb{i}" + tag, [128, 512], F32) for i in range(2)]
    st_ = dict(pa=0, ob=0, pb=0)

    S.op('sp', lambda h: h.dma_start(out=PAR[:], in_=par[:, :]), writes=['PAR'], dma="par")
    S.op('pool', lambda h: h.memset(ONESF[:], 1.0), writes=['ONESF'])
    S.op('pool', lambda h: h.memset(BDF[:], 0.0), writes=['BDF'])
    S.op('pool', lambda h: h.memset(BDF[0:64, 0:64], 1.0), reads=['BDF'], writes=['BDF'])
    S.op('pool', lambda h: h.memset(BDF[64:128, 64:128], 1.0), reads=['BDF'], writes=['BDF'])
    for k in range(8):
        wst = WST[k % 2]
        S.op('sp', lambda h, k=k, wst=wst: h.dma_start(out=wst[:], in_=w_in[k * 128:(k + 1) * 128, :]),
             writes=[('wst', k % 2)], dma=f"wst{k % 2}")
        S.op('dve', lambda h, k=k, wst=wst: h.tensor_copy(out=WIN[:, k, :], in_=wst[:]),
             reads=[('wst', k % 2)], writes=[('WIN', k)])
        src = WIN[:, k, 928:1312].rearrange("p (a b c) -> p a b c", b=2, c=16)
        dst = WINP[:, k, 0:384].rearrange("p (a b c) -> p a b c", b=2, c=16)
        S.op('pool', lambda h, src=src, dst=dst: h.tensor_copy(out=dst[:, :, 0, :], in_=src[:, :, 1, :]),
             reads=[('WIN', k)], writes=[('WINP', k)])
        S.op('pool', lambda h, src=src, dst=dst: h.tensor_copy(out=dst[:, :, 1, :], in_=src[:, :, 0, :]),
             reads=[('WIN', k)], writes=[('WINP', k)])
        S.op('pool', lambda h, k=k: h.tensor_copy(out=WINP[:, k, 384:448], in_=WIN[:, k, 832:896]),
             reads=[('WIN', k)], writes=[('WINP', k)])
        S.op('pool', lambda h, k=k: h.tensor_copy(out=WINP[:, k, 448:464], in_=WIN[:, k, 912:928]),
             reads=[('WIN', k)], writes=[('WINP', k)])
        S.op('pool', lambda h, k=k: h.tensor_copy(out=WINP[:, k, 464:480], in_=WIN[:, k, 896:912]),
             reads=[('WIN', k)], writes=[('WINP', k)])
        S.op('pool', lambda h, k=k: h.tensor_copy(out=WINV[:, k, 0:128], in_=WIN[:, k, 384:512]),
             reads=[('WIN', k)], writes=[('WINV', k)])
        S.op('pool', lambda h, k=k: h.tensor_copy(out=WINV[:, k, 128:256], in_=WIN[:, k, 1312:1440]),
             reads=[('WIN', k)], writes=[('WINV', k)])
        S.op('pool', lambda h, k=k: h.tensor_copy(out=WINV[:, k, 256:512], in_=WIN[:, k, 1952:2208]),
             reads=[('WIN', k)], writes=[('WINV', k)])
    for j in range(2):
        wst = WST[j % 2]
        S.op('sp', lambda h, j=j, wst=wst: h.dma_start(out=wst[:, 0:384], in_=w_uq[j * 128:(j + 1) * 128, :]),
             writes=[('wst', j % 2)], dma=f"wst{j % 2}")
        S.op('dve', lambda h, j=j, wst=wst: h.tensor_copy(out=WUQ[:, j, :], in_=wst[:, 0:384]),
             reads=[('wst', j % 2)], writes=['WUQ'])
        sv = WUQ[:, j, :].rearrange("p (a b) -> p a b", b=96)
        dv = WUQP[:, j, :].rearrange("p (a b) -> p a b", b=96)
        S.op('pool', lambda h, sv=sv, dv=dv: h.tensor_copy(out=dv[:, :, 0:64], in_=sv[:, :, 0:64]),
             reads=['WUQ'], writes=['WUQP'])
        S.op('pool', lambda h, sv=sv, dv=dv: h.tensor_copy(out=dv[:, :, 64:80], in_=sv[:, :, 80:96]),
             reads=['WUQ'], writes=['WUQP'])
        S.op('pool', lambda h, sv=sv, dv=dv: h.tensor_copy(out=dv[:, :, 80:96], in_=sv[:, :, 64:80]),
             reads=['WUQ'], writes=['WUQP'])
    wst = WST[0]
    S.op('sp', lambda h: h.dma_start(out=WST[0][:, 0:512], in_=w_ukv[:, :]), writes=[('wst', 0)], dma="wst0")
    sv = WST[0][:, 0:512].rearrange("p (a b) -> p a b", b=128)
    S.op('dve', lambda h, sv=sv: h.tensor_copy(out=WUKVK[:, :].rearrange("p (a b) -> p a b", b=64), in_=sv[:, :, 0:64]),
         reads=[('wst', 0)], writes=['WUKV'])
    S.op('dve', lambda h, sv=sv: h.tensor_copy(out=WUKVV[:, :].rearrange("p (a b) -> p a b", b=64), in_=sv[:, :, 64:128]),
         reads=[('wst', 0)], writes=['WUKV'])
    WKEYS = [('WIN', k) for k in range(8)] + [('WINP', k) for k in range(8)] + [('WINV', k) for k in range(8)]

    def pa_next():
        i = st_['pa'] % 4
        st_['pa'] += 1
        return PA[i], ('pa', i)

    def pb_next():
        i = st_['pb'] % 2
        st_['pb'] += 1
        return PB[i], ('pb', i)

    def ob_next():
        i = st_['ob'] % 4
        st_['ob'] += 1
        return OB[i], ('ob', i)

    def fm(W, c0, m):
        pa, pk = pa_next()
        for k in range(8):
            S.op('pe', lambda h, k=k, pa=pa: h.matmul(pa[0:m, :], lhsT=W[:, k, c0:c0 + m], rhs=P['XT'][:, k, :],
                                                       start=(k == 0), stop=(k == 7)),
                 reads=WKEYS + ['XT'], writes=[pk])
        return pa, pk

    def store(ob, ok, lo, hi, dst, row0):
        g = st_['g']
        S.op('sp', lambda h: h.dma_start(out=dst[row0:row0 + (hi - lo), g * 512:(g + 1) * 512], in_=ob[lo:hi, :]),
             reads=[ok], dma="st" + ok[0] + str(ok[1]))

    def rms_scale(ssps, sskey, inv_n, lnscale):
        S.op('act', lambda h: h.activation(out=R[:], in_=ssps[:], func=AF.Ln, bias=EPS, scale=inv_n),
             reads=[sskey], writes=['R'])
        S.op('act', lambda h: h.activation(out=R[:], in_=R[:], func=AF.Exp, bias=lnscale, scale=-0.5),
             reads=['R'], writes=['R'])

    def sumsq(srcs, lhsT, lkey):
        pb, pbk = pb_next()
        n = len(srcs)
        for j, (ps_, psk) in enumerate(srcs):
            S.op('act', lambda h, ps_=ps_, j=j: h.activation(out=SQ[j][:], in_=ps_[:], func=AF.Square),
                 reads=[psk], writes=[('SQ', j)])
            S.op('dve', lambda h, j=j: h.tensor_copy(out=SQH[j][:], in_=SQ[j][:]), reads=[('SQ', j)], writes=[('SQH', j)])
            S.op('dve', lambda h, j=j: h.tensor_tensor(out=SQL[j][:], in0=SQ[j][:], in1=SQH[j][:], op=ALU.subtract),
                 reads=[('SQ', j), ('SQH', j)], writes=[('SQL', j)])
        for j in range(n):
            S.op('pe', lambda h, pb=pb, j=j: h.matmul(pb[:], lhsT=lhsT[:], rhs=SQH[j][:], start=(j == 0), stop=False),
                 reads=[lkey, ('SQH', j)], writes=[pbk])
            S.op('pe', lambda h, pb=pb, j=j: h.matmul(pb[:], lhsT=lhsT[:], rhs=SQL[j][:], start=False, stop=(j == n - 1)),
                 reads=[lkey, ('SQL', j)], writes=[pbk])
        return pb, pbk

    for g in range(NG):
        st_['g'] = g
        load_xT_group(C, P, g, xin, scale_resid=False)
        tcc_, tcs_, tbc_, tbs_ = TCC[g % 2], TCS[g % 2], TBC[g % 2], TBS[g % 2]
        tk = ('tab', g % 2)
        S.op('sp', lambda h, g=g, a=tcc_, b=tcs_, c=tbc_, d=tbs_: [
            h.dma_start(out=a[:], in_=tcc[:, g * 512:(g + 1) * 512]),
            h.dma_start(out=b[:], in_=tcs[:, g * 512:(g + 1) * 512]),
            h.dma_start(out=c[64:96, :], in_=tbc[:, g * 512:(g + 1) * 512]),
            h.dma_start(out=d[64:96, :], in_=tbs[:, g * 512:(g + 1) * 512])],
            writes=[tk], dma=f"tab{g % 2}", ndma=4)

        for (c0, scale, dst, row0) in [] if 'AD' in PR_SKIP else [(0, SC_A, QT, QROW_A), (128, SC_A, QT, QROW_A + 128),
                                       (256, None, KT, KROW_A),
                                       (1440, SC_D, QT, QROW_D), (1568, SC_D, QT, QROW_D + 128),
                                       (1696, None, KT, KROW_D), (1824, None, KT, KROW_D + 128)]:
            pa, pk = fm(WIN, c0, 128)
            ob, ok = ob_next()
            if scale is None:
                S.op('act', lambda h, pa=pa, ob=ob: h.copy(out=ob[:], in_=pa[:]), reads=[pk], writes=[ok])
            else:
                S.op('act', lambda h, pa=pa, ob=ob, scale=scale: h.mul(out=ob[:], in_=pa[:], mul=scale),
                     reads=[pk], writes=[ok])
            store(ob, ok, 0, 128, dst, row0)

        for (c0, cp0, gcol, gpcol, lnsc, dst, row0) in [] if 'C' in PR_SKIP else [
                (928, 0, PAR_AXQ, PAR_AXQP, math.log(SC_C), QT, QROW_C),
                (1056, 128, PAR_AXQ, PAR_AXQP, math.log(SC_C), QT, QROW_C + 128),
                (1184, 256, PAR_AXK, PAR_AXKP, 0.0, KT, KROW_C)]:
            pm, pmk = fm(WIN, c0, 128)
            pp, ppk = fm(WINP, cp0, 128)
            pb, pbk = sumsq([(pm, pmk)], BDF, 'BDF')
            rms_scale(pb, pbk, 1.0 / 64, lnsc)
            S.op('dve', lambda h, pm=pm, gcol=gcol, a=tcc_: h.scalar_tensor_tensor(
                out=T1[:], in0=pm[:], scalar=PAR[:, gcol:gcol + 1], in1=a[:], op0=ALU.mult, op1=ALU.mult),
                reads=[pmk, 'PAR', tk], writes=['T1'])
            S.op('dve', lambda h, pp=pp, gpcol=gpcol, b=tcs_: h.scalar_tensor_tensor(
                out=T2[:], in0=pp[:], scalar=PAR[:, gpcol:gpcol + 1], in1=b[:], op0=ALU.mult, op1=ALU.mult),
                reads=[ppk, 'PAR', tk], writes=['T2'])
            S.op('dve', lambda h: h.tensor_tensor(out=T1[:], in0=T1[:], in1=T2[:], op=ALU.add),
                 reads=['T1', 'T2'], writes=['T1'])
            ob, ok = ob_next()
            S.op('dve', lambda h, ob=ob: h.tensor_tensor(out=ob[:], in0=T1[:], in1=R[:], op=ALU.mult),
                 reads=['T1', 'R'], writes=[ok])
            store(ob, ok, 0, 128, dst, row0)

        if 'B' in PR_SKIP:
            continue
        pcq = []
        for j in range(2):
            pa, pk = fm(WIN, 512 + j * 128, 128)
            pcq.append((pa, pk))
        pb, pbk = sumsq(pcq, ONESF, 'ONESF')
        rms_scale(pb, pbk, 1.0 / 256, 0.0)
        for j in range(2):
            pa, pk = pcq[j]
            S.op('dve', lambda h, pa=pa, j=j: h.scalar_tensor_tensor(
                out=CQN[:, j, :], in0=pa[:], scalar=PAR[:, PAR_GQ0 + j:PAR_GQ0 + j + 1], in1=R[:],
                op0=ALU.mult, op1=ALU.mult), reads=[pk, 'PAR', 'R'], writes=['CQN'])
        for hh in range(4):
            pm, pmk = pa_next()
            pp, ppk = pa_next()
            for (ps_, psk, W) in ((pm, pmk, WUQ), (pp, ppk, WUQP)):
                for j in range(2):
                    S.op('pe', lambda h, ps_=ps_, W=W, j=j, hh=hh: h.matmul(
                        ps_[0:96, :], lhsT=W[:, j, hh * 96:(hh + 1) * 96], rhs=CQN[:, j, :],
                        start=(j == 0), stop=(j == 1)), reads=['WUQ', 'WUQP', 'CQN'], writes=[psk])
            ob, ok = ob_next()
            S.op('act', lambda h, pm=pm, ob=ob: h.mul(out=ob[0:64, :], in_=pm[0:64, :], mul=SC_B),
                 reads=[pmk], writes=[ok])
            S.op('dve', lambda h, pm=pm, c=tbc_: h.scalar_tensor_tensor(
                out=T1[64:96, :], in0=pm[64:96, :], scalar=SC_B, in1=c[64:96, :], op0=ALU.mult, op1=ALU.mult),
                reads=[pmk, tk], writes=['T1'])
            S.op('dve', lambda h, pp=pp, d=tbs_: h.scalar_tensor_tensor(
                out=T2[64:96, :], in0=pp[64:96, :], scalar=SC_B, in1=d[64:96, :], op0=ALU.mult, op1=ALU.mult),
                reads=[ppk, tk], writes=['T2'])
            S.op('dve', lambda h, ob=ob: h.tensor_tensor(out=ob[64:96, :], in0=T1[64:96, :], in1=T2[64:96, :], op=ALU.add),
                 reads=['T1', 'T2', ok], writes=[ok])
            store(ob, ok, 0, 96, QT, QROW_B + hh * 96)
        pkv, pkvk = fm(WIN, 768, 128)
        pb, pbk = sumsq([(pkv, pkvk)], ONESF, 'ONESF')
        rms_scale(pb, pbk, 1.0 / 128, 0.0)
        S.op('dve', lambda h, pkv=pkv: h.scalar_tensor_tensor(
            out=CKVN[:], in0=pkv[:], scalar=PAR[:, PAR_GKV:PAR_GKV + 1], in1=R[:], op0=ALU.mult, op1=ALU.mult),
            reads=[pkvk, 'PAR', 'R'], writes=['CKVN'])
        for jj in range(2):
            pa, pk = pa_next()
            S.op('pe', lambda h, pa=pa, jj=jj: h.matmul(pa[:], lhsT=WUKVK[:, jj * 128:(jj + 1) * 128], rhs=CKVN[:],
                                                         start=True, stop=True), reads=['WUKV', 'CKVN'], writes=[pk])
            ob, ok = ob_next()
            S.op('act', lambda h, pa=pa, ob=ob: h.copy(out=ob[:], in_=pa[:]), reads=[pk], writes=[ok])
            store(ob, ok, 0, 128, KT, KROW_BN + jj * 128)
        pm, pmk = fm(WIN, 832, 96)
        pp, ppk = fm(WINP, 384, 96)
        S.op('dve', lambda h, pm=pm, c=tbc_: h.tensor_tensor(out=T1[64:96, :], in0=pm[64:96, :], in1=c[64:96, :], op=ALU.mult),
             reads=[pmk, tk], writes=['T1'])
        S.op('dve', lambda h, pp=pp, d=tbs_: h.tensor_tensor(out=T2[64:96, :], in0=pp[64:96, :], in1=d[64:96, :], op=ALU.mult),
             reads=[ppk, tk], writes=['T2'])
        ob, ok = ob_next()
        S.op('dve', lambda h, ob=ob: h.tensor_tensor(out=ob[64:96, :], in0=T1[64:96, :], in1=T2[64:96, :], op=ALU.add),
             reads=['T1', 'T2'], writes=[ok])
        store(ob, ok, 64, 96, KT, KROW_BR)
        for tl in range(4):
            pa, pk = pa_next()
            for k in range(8):
                S.op('pe', lambda h, pa=pa, k=k, tl=tl: h.matmul(pa[:], lhsT=P['XT'][:, k, tl * 128:(tl + 1) * 128],
                                                                 rhs=WINV[:, k, :], start=(k == 0), stop=(k == 7)),
                     reads=WKEYS + ['XT'], writes=[pk])
            pb, pbk = pb_next()
            S.op('pe', lambda h, pb=pb, tl=tl: h.matmul(pb[:, 0:256], lhsT=CKVN[:, tl * 128:(tl + 1) * 128], rhs=WUKVV[:],
                                                        start=True, stop=True), reads=['CKVN', 'WUKV'], writes=[pbk])
            vo = VO[tl % 2]
            vk = ('vo', tl % 2)
            S.op('act', lambda h, pa=pa, vo=vo: h.copy(out=vo[:, 0:512], in_=pa[:]), reads=[pk], writes=[vk])
            S.op('dve', lambda h, pb=pb, vo=vo: h.tensor_copy(out=vo[:, 512:768], in_=pb[:, 0:256]), reads=[pbk, vk], writes=[vk])
            tt = g * 4 + tl
            S.op('sp', lambda h, vo=vo, tt=tt: h.dma_start(out=V[tt * 128:(tt + 1) * 128, :], in_=vo[:]),
                 reads=[vk], dma=f"vo{tl % 2}")


NLP = 136
LP_SINK, LP_LAM, LP_LINIT, LP_1MLINIT, LP_SUBG = 0, 4, 132, 133, 134
KCOLS = 5 * TOK
AT_UNITS = None
AT_GROUPS = None


def phase_at(C, QT, KTall, Vall, kaug, qaug, ddiag, nsi, wdist, wmask, lpar, mixT, tag="a"):
    nc, S = C.nc, C.S
    IDB, _ = make_ident(C, "identa")
    KS = [C.sb(f"KS{i}", [128, KCOLS], BF16) for i in range(2)]
    VS = [C.sb(f"VS{i}", [128, 128, 96], BF16) for i in range(2)]
    QS = [C.sb(f"QS{i}", [128, TOK], BF16) for i in range(2)]
    QD = C.sb("QD", [128, TOK], BF16)
    DD = C.sb("DD", [128, 4, 2, 512], BF16)
    NSI = C.sb("NSI", [128, 8, 128], BF16)
    WDIST = C.sb("WDIST", [128, 3, 128], BF16)
    WMASK = C.sb("WMASK", [128, 5, 128], BF16)
    LP = C.sb("LP", [128, NLP], F32)
    SM = C.sb("SM", [128, 16], F32)
    TMP = C.sb("TMPa", [128, 32], F32)
    ONESB = C.sb("ONESBa", [128, 64], BF16)
    PT = [C.sb(f"PT{i}", [128, 512], BF16) for i in range(3)]
    RC = [C.sb(f"RC{i}", [128, 512], F32) for i in range(2)]
    A1 = C.sb("A1", [128, 512], F32)
    DN = A1
    A2 = C.sb("A2", [128, 512], F32)
    O1 = A2
    SQd = C.sb("SQd", [128, 512], F32)
    SQHd = C.sb("SQHd", [128, 512], BF16)
    SQLd = C.sb("SQLd", [128, 512], BF16)
    Rr = C.sb("Rr", [128, 512], F32)
    OUTB = [C.sb(f"OUTB{i}", [128, 512], BF16) for i in range(2)]
    PSS = [C.ps(f"pss{i}", [128, 512], F32) for i in range(3)]
    PO = [C.ps(f"po{i}", [128, 512], F32) for i in range(4)]
    PBC = C.ps("pbc", [128, 512], F32)
    cnt = dict(po=0, rc=0, outb=0)

    S.op('sp', lambda h: [h.dma_start(out=DD[:].rearrange("p a b c -> p (a b c)"), in_=ddiag[:, :]),
                          h.dma_start(out=NSI[:].rearrange("p a b -> p (a b)"), in_=nsi[:, :]),
                          h.dma_start(out=WDIST[:].rearrange("p a b -> p (a b)"), in_=wdist[:, :]),
                          h.dma_start(out=WMASK[:].rearrange("p a b -> p (a b)"), in_=wmask[:, :]),
                          h.dma_start(out=LP[:], in_=lpar[:, :])],
         writes=['CONST'], dma="const", ndma=5)
    S.op('pool', lambda h: h.memset(ONESB[:], 1.0), writes=['ONESB'])
    for i in range(2):
        S.op('pool', lambda h, i=i: h.memset(VS[i][:, :, 64:96], 1.0), writes=[('Vones', i)])
        S.op('pool', lambda h, i=i: h.memset(KS[i][:, :], 0.0), writes=[('K', i)])
    S.op('dve', lambda h: h.tensor_tensor(out=TMP[:], in0=LP[:, 4:36], in1=LP[:, 36:68], op=ALU.mult), reads=['CONST'], writes=['TMP'])
    S.op('dve', lambda h: h.reduce_sum(out=SM[:, 0:1], in_=TMP[:], axis=mybir.AxisListType.X), reads=['TMP'], writes=['SM0'])
    S.op('dve', lambda h: h.tensor_tensor(out=TMP[:], in0=LP[:, 68:100], in1=LP[:, 100:132], op=ALU.mult), reads=['CONST', 'SM0'], writes=['TMP'])
    S.op('dve', lambda h: h.reduce_sum(out=SM[:, 1:2], in_=TMP[:], axis=mybir.AxisListType.X), reads=['TMP'], writes=['SM0'])
    S.op('act', lambda h: h.activation(out=SM[:, 2:4], in_=SM[:, 0:2], func=AF.Exp), reads=['SM0'], writes=['SM1'])
    S.op('act', lambda h: h.activation(out=SM[:, 8:12], in_=LP[:, 0:4], func=AF.Exp), reads=['CONST'], writes=['ESINK'])
    S.op('dve', lambda h: h.tensor_tensor(out=SM[:, 4:5], in0=SM[:, 2:3], in1=SM[:, 3:4], op=ALU.subtract), reads=['SM1'], writes=['SM2'])
    S.op('dve', lambda h: h.tensor_tensor(out=SM[:, 4:5], in0=SM[:, 4:5], in1=LP[:, LP_LINIT:LP_LINIT + 1], op=ALU.add), reads=['SM2', 'CONST'], writes=['SM2'])
    S.op('dve', lambda h: h.tensor_scalar_mul(out=SM[:, 5:6], in0=SM[:, 4:5], scalar1=-1.0), reads=['SM2'], writes=['NEGLAM'])
    S.op('dve', lambda h: h.tensor_tensor(out=SM[:, 6:7], in0=LP[:, LP_SUBG:LP_SUBG + 1], in1=LP[:, LP_1MLINIT:LP_1MLINIT + 1], op=ALU.mult),
         reads=['CONST'], writes=['GSUB'])

    def po_next():
        i = cnt['po'] % 4
        cnt['po'] += 1
        return PO[i], ('po', i)

    def recip_den(po, pok, sinkcol=None):
        i = cnt['rc'] % 2
        cnt['rc'] += 1
        rc, rck = RC[i], ('rc', i)
        src, srck = po, pok
        if sinkcol is not None:
            S.op('dve', lambda h: h.tensor_scalar_add(out=DN[64:96, :], in0=po[64:96, :], scalar1=SM[64:96, sinkcol:sinkcol + 1]),
                 reads=[pok, 'ESINK'], writes=['DN'])
            src, srck = DN, 'DN'
        S.op('dve', lambda h: h.reciprocal(out=rc[0:32, :], in_=src[64:96, :]), reads=[srck], writes=[rck])
        S.op('dve', lambda h: h.reciprocal(out=rc[32:64, :], in_=src[64:96, :]), reads=[srck, rck], writes=[rck])
        return rc, rck

    def store_out(ob, obk, row0, g):
        S.op('pool', lambda h: h.dma_start(out=mixT[row0:row0 + 64, g * 512:(g + 1) * 512], in_=ob[0:64, :]),
             reads=[obk], dma="mx" + str(obk[1]))

    def epi_plain(po, pok, row0, g, sinkcol=None):
        rc, rck = recip_den(po, pok, sinkcol)
        i = cnt['outb'] % 2
        cnt['outb'] += 1
        ob, obk = OUTB[i], ('outb', i)
        S.op('dve', lambda h: h.tensor_tensor(out=ob[0:64, :], in0=po[0:64, :], in1=rc[0:64, :], op=ALU.mult),
             reads=[pok, rck], writes=[obk])
        store_out(ob, obk, row0, g)

    def epi_d_first(po, pok):
        rc, rck = recip_den(po, pok)
        S.op('dve', lambda h: h.tensor_tensor(out=A1[0:64, :], in0=po[0:64, :], in1=rc[0:64, :], op=ALU.mult),
             reads=[pok, rck], writes=['A1'])

    def epi_d_second(po, pok):
        rc, rck = recip_den(po, pok)
        S.op('dve', lambda h: h.tensor_tensor(out=A2[0:64, :], in0=po[0:64, :], in1=rc[0:64, :], op=ALU.mult),
             reads=[pok, rck], writes=['A2'])
        S.op('dve', lambda h: h.scalar_tensor_tensor(out=O1[0:64, :], in0=A2[0:64, :], scalar=SM[0:64, 5:6], in1=A1[0:64, :],
                                                      op0=ALU.mult, op1=ALU.add), reads=['A1', 'A2', 'NEGLAM'], writes=['O1'])
        S.op('pool', lambda h: h.tensor_tensor(out=SQd[0:64, :], in0=O1[0:64, :], in1=O1[0:64, :], op=ALU.mult), reads=['O1'], writes=['SQd'])
        S.op('pool', lambda h: h.tensor_copy(out=SQHd[0:64, :], in_=SQd[0:64, :]), reads=['SQd'], writes=['SQHd'])
        S.op('pool', lambda h: h.tensor_tensor(out=SQLd[0:64, :], in0=SQd[0:64, :], in1=SQHd[0:64, :], op=ALU.subtract),
             reads=['SQd', 'SQHd'], writes=['SQLd'])

    def epi_d_final(row0, g):
        S.op('pe', lambda h: h.matmul(PBC[0:64, :], lhsT=ONESB[0:64, 0:64], rhs=SQHd[0:64, :], start=True, stop=False),
             reads=['ONESB', 'SQHd'], writes=['pbc'])
        S.op('pe', lambda h: h.matmul(PBC[0:64, :], lhsT=ONESB[0:64, 0:64], rhs=SQLd[0:64, :], start=False, stop=True),
             reads=['ONESB', 'SQLd'], writes=['pbc'])
        S.op('act', lambda h: h.activation(out=Rr[0:64, :], in_=PBC[0:64, :], func=AF.Ln, bias=EPS, scale=1.0 / 64),
             reads=['pbc'], writes=['Rr'])
        S.op('act', lambda h: h.activation(out=Rr[0:64, :], in_=Rr[0:64, :], func=AF.Exp, scale=-0.5), reads=['Rr'], writes=['Rr'])
        i = cnt['outb'] % 2
        cnt['outb'] += 1
        ob, obk = OUTB[i], ('outb', i)
        S.op('dve', lambda h: h.scalar_tensor_tensor(out=ob[0:64, :], in0=O1[0:64, :], scalar=SM[0:64, 6:7], in1=Rr[0:64, :],
                                                      op0=ALU.mult, op1=ALU.mult), reads=['O1', 'GSUB', 'Rr'], writes=[obk])
        store_out(ob, obk, row0, g)

    def vload(slot, c0):
        return [lambda h, j=j: h.dma_start(out=VS[slot][:, j * 32:(j + 1) * 32, 0:64],
                                           in_=Vall[j][:, c0:c0 + 64].rearrange("(t p) c -> p t c", p=128)) for j in range(4)]

    def issue(fns, keys, sem):
        S.op('sp', lambda h: [f(h) for f in fns], writes=keys, dma=sem, ndma=len(fns))

    def load_B(hh, slot):
        ks = KS[slot]
        fns = []
        for j in range(4):
            fns.append(lambda h, j=j: h.dma_start(out=ks[0:64, j * TOK:(j + 1) * TOK], in_=KTall[j][KROW_BN + hh * 64:KROW_BN + (hh + 1) * 64, :]))
            fns.append(lambda h, j=j: h.dma_start(out=ks[64:96, j * TOK:(j + 1) * TOK], in_=KTall[j][KROW_BR:KROW_BR + 32, :]))
        issue(fns, [('K', slot)], f"k{slot}")
        issue(vload(slot, VCOL_B + hh * 64), [('V', slot)], f"v{slot}")
        issue([lambda h: h.dma_start(out=QS[slot][0:96, :], in_=QT[QROW_B + hh * 96:QROW_B + (hh + 1) * 96, :])], [('Q', slot)], f"q{slot}")

    def load_C(kv, slot):
        ks = KS[slot]
        fns = []
        for b in (0, 64):
            for j in range(4):
                fns.append(lambda h, j=j, b=b: h.dma_start(out=ks[b:b + 64, j * TOK:(j + 1) * TOK],
                                                            in_=KTall[j][KROW_C + kv * 64:KROW_C + (kv + 1) * 64, :]))
        issue(fns, [('K', slot)], f"k{slot}")
        issue(vload(slot, VCOL_C + kv * 64), [('V', slot)], f"v{slot}")
        issue([lambda h: h.dma_start(out=QS[slot][0:128, :], in_=QT[QROW_C + kv * 128:QROW_C + (kv + 1) * 128, :])], [('Q', slot)], f"q{slot}")

    def load_D(hh, slot):
        ks, qs = KS[slot], QS[slot]
        fns, qf = [], []
        for m in range(2):
            r0 = KROW_D + (hh * 2 + m) * 32
            for j in range(4):
                fns.append(lambda h, j=j, m=m, r0=r0: h.dma_start(out=ks[m * 64:m * 64 + 32, j * TOK:(j + 1) * TOK], in_=KTall[j][r0:r0 + 32, :]))
            fns.append(lambda h, m=m, r0=r0: h.dma_start(out=ks[m * 64:m * 64 + 32, 4 * TOK:5 * TOK], in_=KTall[0][r0:r0 + 32, :]))
            fns.append(lambda h, m=m: h.dma_start(out=ks[m * 64 + 32:m * 64 + 36, :], in_=kaug[hh * 4:(hh + 1) * 4, :]))
        issue(fns, [('K', slot)], f"k{slot}")
        issue(vload(slot, VCOL_D + hh * 64), [('V', slot)], f"v{slot}")
        q0 = QROW_D + (hh * 2) * 32
        S.op('pool', lambda h: h.memset(qs[:, :], 0.0), writes=[('Q', slot)])
        qf.append(lambda h: h.dma_start(out=qs[0:32, :], in_=QT[q0:q0 + 32, :]))
        qf.append(lambda h: h.dma_start(out=qs[32:36, :], in_=qaug[hh * 4:(hh + 1) * 4, :]))
        issue(qf, [('Q', slot)], f"q{slot}")

    def load_D_late(hh):
        q1 = QROW_D + (hh * 2 + 1) * 32
        S.op('pool', lambda h: h.memset(QD[:, :], 0.0), writes=['QD'])
        issue([lambda h: h.dma_start(out=QD[64:96, :], in_=QT[q1:q1 + 32, :]),
               lambda h: h.dma_start(out=QD[96:100, :], in_=qaug[hh * 4:(hh + 1) * 4, :])], ['QD'], "qd")

    def load_A(kv, slot):
        r0 = KROW_A + kv * 64
        c0 = VCOL_A + kv * 64
        KA, VA = KS[slot], VS[slot]
        fns = []
        for b in (0, 64):
            fns.append(lambda h, b=b: h.dma_start(out=KA[b:b + 64, 128:128 + TOK], in_=KTall[0][r0:r0 + 64, :]))
            fns.append(lambda h, b=b: h.dma_start(out=KA[b:b + 64, 0:128], in_=KTall[3][r0:r0 + 64, TOK - 128:TOK]))
            fns.append(lambda h, b=b: h.dma_start(out=KA[b:b + 64, 128 + TOK:256 + TOK], in_=KTall[1][r0:r0 + 64, 0:128]))
        issue(fns, [('K', slot)], f"k{slot}")
        vf = [lambda h: h.dma_start(out=VA[:, 1:33, 0:64], in_=Vall[0][:, c0:c0 + 64].rearrange("(t p) c -> p t c", p=128)),
              lambda h: h.dma_start(out=VA[:, 0, 0:64], in_=Vall[3][TOK - 128:TOK, c0:c0 + 64]),
              lambda h: h.dma_start(out=VA[:, 33, 0:64], in_=Vall[1][0:128, c0:c0 + 64])]
        issue(vf, [('V', slot)], f"v{slot}")
        issue([lambda h: h.dma_start(out=QS[slot][0:128, :], in_=QT[QROW_A + kv * 128:QROW_A + (kv + 1) * 128, :])], [('Q', slot)], f"q{slot}")

    groups = list(range(NG)) if AT_GROUPS is None else list(AT_GROUPS)

    def dense_steps(slot, maps, dtype_d=None):
        ks, vs, qs = KS[slot], VS[slot], QS[slot]
        rk = [('K', slot), ('Q', slot)]
        rv = [('V', slot), ('Vones', slot)]
        steps = []
        for g in groups:
            for mi, mp in enumerate(maps):
                po, pok = po_next()
                b, dk = mp['b'], mp['dk']
                qm = mp['q'] if mp.get('q') is not None else qs
                fullk = mp.get('fullk', False)
                rkm = rk + (['QD'] if mp.get('q') is not None else [])
                for t in range(128):
                    kind, col = 'plain', t * 128
                    if dtype_d is not None and t < 32:
                        if 4 * g <= t < 4 * g + 4:
                            kind = 'diag'
                        elif t >= 4 * g + 4:
                            col = 4 * TOK + t * 128
                    st = {}
                    if kind == 'plain':
                        def s_fn(ps, psk, b=b, dk=dk, col=col, g=g, qm=qm, fullk=fullk, rkm=rkm):
                            lo, hi = (0, 128) if fullk else (b, b + dk)
                            S.op('pe', lambda h: h.matmul(ps[:], lhsT=ks[lo:hi, col:col + 128], rhs=qm[lo:hi, g * 512:(g + 1) * 512],
                                                           start=True, stop=True), reads=rkm, writes=[psk])
                    else:
                        def s_fn(ps, psk, b=b, col=col, g=g, j=t - 4 * g, hd=dtype_d, qm=qm, rkm=rkm):
                            S.op('pe', lambda h: h.matmul(ps[:], lhsT=ks[b:b + 32, col:col + 128], rhs=qm[b:b + 32, g * 512:(g + 1) * 512],
                                                           start=True, stop=False), reads=rkm, writes=[psk])
                            S.op('pe', lambda h: h.matmul(ps[:], lhsT=NSI[:, 4 + hd, :], rhs=DD[:, j, 0, :], start=False, stop=False),
                                 reads=['CONST'], writes=[psk])
                            S.op('pe', lambda h: h.matmul(ps[:], lhsT=NSI[:, 4 + hd, :], rhs=DD[:, j, 1, :], start=False, stop=True),
                                 reads=['CONST'], writes=[psk])
                    st['s'] = s_fn

                    def e_fn(ps, psk, pt, ptk):
                        S.op('act', lambda h: h.activation(out=pt[:], in_=ps[:], func=AF.Exp), reads=[psk], writes=[ptk])
                    st['e'] = e_fn

                    def v_fn(pt, ptk, t=t, po=po, pok=pok):
                        S.op('pe', lambda h: h.matmul(po[0:96, :], lhsT=vs[:, t, :], rhs=pt[:], start=(t == 0), stop=(t == 127)),
                             reads=rv + [ptk], writes=[pok])
                    st['v'] = v_fn
                    if t == 127:
                        st['post'] = (lambda po=po, pok=pok, mp=mp, g=g: mp['epi'](po, pok, g))
                        if mp.get('final') is not None:
                            st['defer'] = (lambda mp=mp, g=g: mp['final'](g))
                    steps.append(st)
        return steps

    def window_steps(kv, slot):
        qs = QS[slot]
        KA, VA = KS[slot], VS[slot]
        steps = []
        for hl in range(2):
            hd = kv * 2 + hl
            b = hl * 64
            po = pok = None
            for i in range(NTT):
                if AT_GROUPS is not None and (i // 4) not in AT_GROUPS:
                    continue
                if i % 4 == 0:
                    po, pok = po_next()
                st = {}

                def s_fn(ps, psk, i=i, b=b, hd=hd):
                    for jj in range(3):
                        mi = jj
                        if i == 0 and jj == 0:
                            mi = 3
                        if i == NTT - 1 and jj == 2:
                            mi = 4
                        sl = slice(jj * 128, (jj + 1) * 128)
                        S.op('pe', lambda h, jj=jj, sl=sl: h.matmul(ps[:, sl], lhsT=KA[b:b + 64, (i + jj) * 128:(i + jj + 1) * 128],
                                                                    rhs=qs[b:b + 64, i * 128:(i + 1) * 128], start=True, stop=False),
                             reads=[('K', slot), ('Q', slot)], writes=[psk])
                        S.op('pe', lambda h, jj=jj, sl=sl: h.matmul(ps[:, sl], lhsT=NSI[:, hd, :], rhs=WDIST[:, jj, :], start=False, stop=False),
                             reads=['CONST'], writes=[psk])
                        S.op('pe', lambda h, mi=mi, sl=sl: h.matmul(ps[:, sl], lhsT=IDB[:], rhs=WMASK[:, mi, :], start=False, stop=True),
                             reads=['CONST', 'identa'], writes=[psk])
                st['s'] = s_fn

                def e_fn(ps, psk, pt, ptk):
                    S.op('act', lambda h: h.activation(out=pt[:, 0:384], in_=ps[:, 0:384], func=AF.Exp), reads=[psk], writes=[ptk])
                st['e'] = e_fn

                def v_fn(pt, ptk, i=i, po=po, pok=pok):
                    for jj in range(3):
                        S.op('pe', lambda h, jj=jj: h.matmul(po[0:96, (i % 4) * 128:(i % 4 + 1) * 128], lhsT=VA[:, i + jj, :],
                                                             rhs=pt[:, jj * 128:(jj + 1) * 128], start=(jj == 0), stop=(jj == 2)),
                             reads=[('V', slot), ('Vones', slot), ptk], writes=[pok])
                st['v'] = v_fn
                if i % 4 == 3:
                    st['post'] = (lambda po=po, pok=pok, hd=hd, g=i // 4: epi_plain(po, pok, hd * 64, g, sinkcol=8 + hd))
                steps.append(st)
        return steps

    units = []
    for kv in range(2):
        units.append(('A%d' % kv, lambda slot, kv=kv: load_A(kv, slot), lambda slot, kv=kv: window_steps(kv, slot), None))
    for hh in range(4):
        units.append(('B%d' % hh, lambda slot, hh=hh: load_B(hh, slot),
                      lambda slot, hh=hh: dense_steps(slot, [dict(b=0, dk=96, epi=lambda po, pok, g, hh=hh: epi_plain(po, pok, 256 + hh * 64, g))]), None))
    for kv in range(2):
        units.append(('C%d' % kv, lambda slot, kv=kv: load_C(kv, slot),
                      lambda slot, kv=kv: dense_steps(slot, [
                          dict(b=hl * 64, dk=64, epi=lambda po, pok, g, hd=kv * 2 + hl: epi_plain(po, pok, 512 + hd * 64, g))
                          for hl in range(2)]), None))
    for hh in range(4):
        units.append(('D%d' % hh, lambda slot, hh=hh: load_D(hh, slot),
                      lambda slot, hh=hh: dense_steps(slot, [
                          dict(b=0, dk=36, fullk=True, epi=lambda po, pok, g: epi_d_first(po, pok)),
                          dict(b=64, dk=36, fullk=True, q=QD, epi=lambda po, pok, g: epi_d_second(po, pok),
                               final=lambda g, hh=hh: epi_d_final(768 + hh * 64, g))], dtype_d=hh),
                      (lambda hh=hh: load_D_late(hh))))
    if AT_UNITS is not None:
        units = [u for u in units if u[0] in AT_UNITS]

    LOOK = 2
    DEFER = 6
    units[0][1](0)
    allsteps = []
    for ui, (name, loader, gen, late) in enumerate(units):
        steps = gen(ui % 2)
        if late is not None:
            steps[0]['pre0'] = late
        if ui + 1 < len(units):
            steps[min(LOOK + 1, len(steps) - 1)]['pre'] = (lambda nxt=units[ui + 1][1], slot=(ui + 1) % 2: nxt(slot))
        allsteps += steps
    n = len(allsteps)
    deferred = []
    for i in range(n + LOOK):
        if i < n:
            st = allsteps[i]
            b = i % 3
            if 'pre0' in st:
                st['pre0']()
            if 'pre' in st:
                st['pre']()
            st['s'](PSS[b], ('pss', b))
            st['e'](PSS[b], ('pss', b), PT[b], ('pt', b))
        if i >= LOOK:
            st = allsteps[i - LOOK]
            b = (i - LOOK) % 3
            st['v'](PT[b], ('pt', b))
            if 'post' in st:
                st['post']()
            if 'defer' in st:
                deferred.append((i + DEFER, st['defer']))
        while deferred and deferred[0][0] <= i:
            deferred.pop(0)[1]()
    while deferred:
        deferred.pop(0)[1]()


def phase_o(C, xin, mixT, w_out, lng, lnb, xout, tag="o"):
    nc, S = C.nc, C.S
    P = {}
    WOUT = C.sb("WOUT" + tag, [128, 8, 1024], BF16)
    WST = [C.sb(f"WSTo{i}" + tag, [128, 1024], F32) for i in range(2)]
    MIX = [C.sb(f"MIX{i}" + tag, [128, 8, 512], BF16) for i in range(2)]
    P['XR'] = [C.sb(f"XR{i}" + tag, [128, 4, 1024], F32) for i in range(2)]
    P['G'] = C.sb("G" + tag, [128, 1, 1024], F32)
    P['B'] = C.sb("B" + tag, [128, 1, 1024], F32)
    P['ot'] = [C.sb(f"ot{i}" + tag, [128, 1024], F32) for i in range(2)]
    P['st'] = C.sb("st" + tag, [128, 4, 12], F32)
    P['mv'] = C.sb("mv" + tag, [128, 4, 2], F32)
    P['rstd'] = C.sb("rstd" + tag, [128, 4], F32)
    PY = [C.ps(f"py{i}" + tag, [128, 512], F32) for i in range(4)]
    S.op('sp', lambda h: [h.dma_start(out=P['G'][:, 0, :], in_=lng.partition_broadcast(128)),
                          h.dma_start(out=P['B'][:, 0, :], in_=lnb.partition_broadcast(128))],
         writes=['GB'], dma="gb", ndma=2)
    for k in range(8):
        wst = WST[k % 2]
        S.op('sp', lambda h, k=k, wst=wst: h.dma_start(out=wst[:], in_=w_out[k * 128:(k + 1) * 128, :]),
             writes=[('wst', k % 2)], dma=f"wst{k % 2}")
        S.op('dve', lambda h, k=k, wst=wst: h.tensor_copy(out=WOUT[:, k, :], in_=wst[:]),
             reads=[('wst', k % 2)], writes=['WOUT'])
    mix_r = mixT.rearrange("(k p) t -> p k t", p=128)
    n = 0
    for g in range(NG):
        XR = P['XR'][g % 2]
        rk = ('XR', g % 2)
        mx = MIX[g % 2]
        S.op('sp', lambda h, g=g, mx=mx: h.dma_start(out=mx[:], in_=mix_r[:, :, g * 512:(g + 1) * 512]),
             writes=[('mix', g % 2)], dma=f"mix{g % 2}")
        for t in range(4):
            tt = g * 4 + t
            S.op('sp', lambda h, t=t, tt=tt, XR=XR: h.dma_start(out=XR[:, t, :], in_=xin[tt * 128:(tt + 1) * 128, :]),
                 writes=[rk + (t,)], dma=f"xr{g % 2}{t}")
            S.op('act', lambda h, t=t, XR=XR: h.mul(out=XR[:, t, :], in_=XR[:, t, :], mul=ALPHA),
                 reads=[rk + (t,)], writes=[rk + (t,)])
            for hh in range(2):
                py = PY[n % 4]
                pk = ('py', n % 4)
                n += 1
                for k in range(8):
                    S.op('pe', lambda h, k=k, t=t, hh=hh, py=py, mx=mx: h.matmul(
                        py[:], lhsT=mx[:, k, t * 128:(t + 1) * 128], rhs=WOUT[:, k, hh * 512:(hh + 1) * 512],
                        start=(k == 0), stop=(k == 7)), reads=[('mix', g % 2), 'WOUT'], writes=[pk])
                S.op('dve', lambda h, t=t, hh=hh, py=py, XR=XR: h.tensor_tensor(
                    out=XR[:, t, hh * 512:(hh + 1) * 512], in0=py[:], in1=XR[:, t, hh * 512:(hh + 1) * 512], op=ALU.add),
                    reads=[pk, rk + (t,)], writes=[rk + (t,)])
        ln_tiles(C, P, g, xout, 0)


def _inv_freq(dim):
    return np.power(np.float32(10000.0), -(np.arange(0, dim, 2, dtype=np.float32) / np.float32(dim))).astype(np.float32)


def rope_tables(rank):
    pos = (rank * TOK + np.arange(TOK)).astype(np.int64)
    inv = _inv_freq(32)

    def cs(p):
        ang = (p.astype(np.float32)[:, None] * inv[None, :]).astype(np.float32).astype(np.float64)
        return np.cos(ang).T, np.sin(ang).T

    c_t, s_t = cs(pos)
    tbc = np.concatenate([c_t, c_t], 0).astype(np.float32)
    tbs = np.concatenate([-s_t, s_t], 0).astype(np.float32)
    c_r, s_r = cs(pos // 64)
    c_c, s_c = cs(pos % 64)
    c64 = np.concatenate([c_r, c_r, c_c, c_c], 0)
    s64 = np.concatenate([-s_r, s_r, -s_c, s_c], 0)
    tcc = np.concatenate([c64, c64], 0).astype(np.float32)
    tcs = np.concatenate([s64, s64], 0).astype(np.float32)
    return dict(tcc=np.ascontiguousarray(tcc), tcs=np.ascontiguousarray(tcs),
                tbc=np.ascontiguousarray(tbc), tbs=np.ascontiguousarray(tbs))


def _perm64(v):
    v = np.asarray(v)
    return np.concatenate([v[16:32], v[0:16], v[48:64], v[32:48]])


def par_table(mla_q_norm, mla_kv_norm, ax_q_norm, ax_k_norm):
    t = np.zeros((128, NPAR), np.float32)
    t[:, PAR_GQ0] = mla_q_norm[0:128]
    t[:, PAR_GQ1] = mla_q_norm[128:256]
    t[:, PAR_GKV] = mla_kv_norm
    t[:, PAR_AXQ] = np.concatenate([ax_q_norm, ax_q_norm])
    t[:, PAR_AXQP] = np.concatenate([_perm64(ax_q_norm), _perm64(ax_q_norm)])
    t[:, PAR_AXK] = np.concatenate([ax_k_norm, ax_k_norm])
    t[:, PAR_AXKP] = np.concatenate([_perm64(ax_k_norm), _perm64(ax_k_norm)])
    return t


def _bf(a):
    return np.ascontiguousarray(np.asarray(a, np.float32).astype(ml_dtypes.bfloat16))


def at_tables(rank):
    slopes = 2.0 ** (-(np.arange(8) + 1.0))
    t = np.arange(TOK)
    kaug = np.zeros((4, 4, KCOLS), np.float32)
    qaug = np.zeros((4, 4, TOK), np.float32)
    q_abs = rank * TOK + t
    for hd in range(4):
        s_ = slopes[4 + hd]
        qaug[hd, 0] = s_ * 128 * (q_abs // 128)
        qaug[hd, 1] = s_ * (q_abs % 128)
        qaug[hd, 2] = 1.0
        qaug[hd, 3] = 1.0
        for j in range(5):
            if j == 4:
                rj, sg = rank, -1.0
            else:
                rj = (rank + j) % 4
                sg = 1.0 if (j == 0 or rj < rank) else -1.0
            k_abs = rj * TOK + t
            sl = slice(j * TOK, (j + 1) * TOK)
            kaug[hd, 0, sl] = -sg
            kaug[hd, 1, sl] = -sg
            kaug[hd, 2, sl] = sg * s_ * 128 * (k_abs // 128)
            kaug[hd, 3, sl] = sg * s_ * (k_abs % 128)
    kr = np.arange(128)[:, None]
    qr = np.arange(512)[None, :]
    dd = np.zeros((128, 4, 2, 512), np.float32)
    for j in range(4):
        d = np.abs(qr - 128 * j - kr)
        dd[:, j, 0, :] = 16 * (d // 16)
        dd[:, j, 1, :] = d % 16
    nsi = np.zeros((128, 8, 128), np.float32)
    for i in range(8):
        nsi[:, i, :] = -slopes[i] * np.eye(128)
    q1 = np.arange(128)[None, :]
    wdist = np.zeros((128, 3, 128), np.float32)
    wmask = np.zeros((128, 5, 128), np.float32)
    for jj in range(3):
        d = np.abs(q1 - kr + 128 * (1 - jj))
        wdist[:, jj, :] = d
        wmask[:, jj, :] = np.where(d <= 128, 0.0, -1e30)
    wmask[:, 3, :] = wmask[:, 0, :] if rank > 0 else -1e30
    wmask[:, 4, :] = wmask[:, 2, :] if rank < 3 else -1e30
    return dict(kaug=_bf(kaug.reshape(16, KCOLS)), qaug=_bf(qaug.reshape(16, TOK)), ddiag=_bf(dd.reshape(128, -1)),
                nsi=_bf(nsi.reshape(128, -1)), wdist=_bf(wdist.reshape(128, -1)), wmask=_bf(wmask.reshape(128, -1)))


def lpar_table(sink, diff_lambda, diff_subln, l):
    t = np.zeros((128, NLP), np.float32)
    t[:, LP_SINK:LP_SINK + 4] = np.asarray(sink, np.float32)[None, :]
    t[:, LP_LAM:LP_LAM + 128] = np.asarray(diff_lambda, np.float32).reshape(1, 128)
    lam_init = 0.8 - 0.6 * math.exp(-0.3 * l)
    t[:, LP_LINIT] = lam_init
    t[:, LP_1MLINIT] = 1.0 - lam_init
    t[:, LP_SUBG] = np.concatenate([diff_subln, diff_subln])
    return t


def build_at_program():
    if 'at' in _CACHE:
        return _CACHE['at']
    C = Ctx()
    nc = C.nc
    ei = lambda n, sh, dt=BF16: nc.dram_tensor(n, list(sh), dt, kind="ExternalInput").ap()
    QT = ei("QT", [RQ, TOK])
    KTall = ei("KTall", [4, RK, TOK])
    Vall = ei("Vall", [4, TOK, RV])
    kaug, qaug = ei("kaug", [16, KCOLS]), ei("qaug", [16, TOK])
    ddiag, nsi = ei("ddiag", [128, 4 * 2 * 512]), ei("nsi", [128, 8 * 128])
    wdist, wmask = ei("wdist", [128, 3 * 128]), ei("wmask", [128, 5 * 128])
    lpar = ei("lpar", [128, NLP], F32)
    mixT = nc.dram_tensor("mixT", [D_MODEL, TOK], BF16, kind="ExternalOutput").ap()
    phase_at(C, QT, KTall, Vall, kaug, qaug, ddiag, nsi, wdist, wmask, lpar, mixT)
    C.finish()
    _CACHE['at'] = nc
    return nc


def run_at(pr_outs, inp, l):
    nc = build_at_program()
    lp = lpar_table(inp['win_sink'][l], inp['diff_lambda'][l], inp['diff_subln'][l], l)
    in_maps = []
    for c in range(8):
        b, r = c // 4, c % 4
        order = [b * 4 + (r + j) % 4 for j in range(4)]
        m = dict(QT=np.ascontiguousarray(pr_outs[c][0]),
                 KTall=np.ascontiguousarray(np.stack([pr_outs[o][1] for o in order])),
                 Vall=np.ascontiguousarray(np.stack([pr_outs[o][2] for o in order])), lpar=lp)
        m.update(at_tables(r))
        in_maps.append(m)
    import os
    res = run_bass_kernel_spmd(nc, in_maps, core_ids=list(range(8)), trace=bool(os.environ.get("AT_TRACE")))
    if os.environ.get("AT_TRACE"):
        print("AT exec_time_ns", res.exec_time_ns)
    return [r["mixT"] for r in res.results]


def build_pr_program():
    if 'pr' in _CACHE:
        return _CACHE['pr']
    C = Ctx()
    nc = C.nc
    ei = lambda n, sh, dt=F32: nc.dram_tensor(n, list(sh), dt, kind="ExternalInput").ap()
    eo = lambda n, sh, dt=BF16: nc.dram_tensor(n, list(sh), dt, kind="ExternalOutput").ap()
    xin = ei("xin", [TOK, D_MODEL])
    w_in = ei("w_in", [D_MODEL, N_IN])
    w_uq = ei("w_uq", [256, 384])
    w_ukv = ei("w_ukv", [128, 512])
    par = ei("par", [128, NPAR])
    tcc, tcs = ei("tcc", [128, TOK]), ei("tcs", [128, TOK])
    tbc, tbs = ei("tbc", [32, TOK]), ei("tbs", [32, TOK])
    QT, KT, V = eo("QT", [RQ, TOK]), eo("KT", [RK, TOK]), eo("V", [TOK, RV])
    phase_pr(C, xin, w_in, w_uq, w_ukv, par, tcc, tcs, tbc, tbs, QT, KT, V)
    C.finish()
    _CACHE['pr'] = nc
    return nc


def run_pr(xs, inp, l):
    nc = build_pr_program()
    par = par_table(inp['mla_q_norm'][l], inp['mla_kv_norm'][l], inp['ax_q_norm'][l], inp['ax_k_norm'][l])
    in_maps = []
    for c in range(8):
        m = dict(xin=xs[c], w_in=np.ascontiguousarray(inp['w_in'][l]), w_uq=np.ascontiguousarray(inp['mla_w_uq'][l]),
                 w_ukv=np.ascontiguousarray(inp['mla_w_ukv'][l]), par=par)
        m.update(rope_tables(c % 4))
        in_maps.append(m)
    res = run_bass_kernel_spmd(nc, in_maps, core_ids=list(range(8)))
    return [(r["QT"], r["KT"], r["V"]) for r in res.results]


def build_ffn_program():
    if 'ffn' in _CACHE:
        return _CACHE['ffn']
    C = Ctx()
    nc = C.nc
    xin = nc.dram_tensor("xin", [TOK, D_MODEL], F32, kind="ExternalInput").ap()
    wgu = nc.dram_tensor("wgu", [D_MODEL, 2 * D_FF], F32, kind="ExternalInput").ap()
    wd = nc.dram_tensor("wd", [D_FF, D_MODEL], F32, kind="ExternalInput").ap()
    lng = nc.dram_tensor("lng", [D_MODEL], F32, kind="ExternalInput").ap()
    lnb = nc.dram_tensor("lnb", [D_MODEL], F32, kind="ExternalInput").ap()
    xout = nc.dram_tensor("xout", [TOK, D_MODEL], F32, kind="ExternalOutput").ap()
    phase_ffn(C, xin, xout, wgu, wd, lng, lnb)
    C.finish()
    _CACHE['ffn'] = nc
    return nc


def run_ffn(xs, wgu, wd, g, b):
    nc = build_ffn_program()
    in_maps = [dict(xin=xs[c], wgu=wgu, wd=wd, lng=g, lnb=b) for c in range(8)]
    res = run_bass_kernel_spmd(nc, in_maps, core_ids=list(range(8)))
    return [r["xout"] for r in res.results]


def build_o_program():
    if 'o' in _CACHE:
        return _CACHE['o']
    C = Ctx()
    nc = C.nc
    xin = nc.dram_tensor("xin", [TOK, D_MODEL], F32, kind="ExternalInput").ap()
    mixT = nc.dram_tensor("mixT", [D_MODEL, TOK], BF16, kind="ExternalInput").ap()
    w_out = nc.dram_tensor("w_out", [D_MODEL, D_MODEL], F32, kind="ExternalInput").ap()
    lng = nc.dram_tensor("lng", [D_MODEL], F32, kind="ExternalInput").ap()
    lnb = nc.dram_tensor("lnb", [D_MODEL], F32, kind="ExternalInput").ap()
    xout = nc.dram_tensor("xout", [TOK, D_MODEL], F32, kind="ExternalOutput").ap()
    phase_o(C, xin, mixT, w_out, lng, lnb, xout)
    C.finish()
    _CACHE['o'] = nc
    return nc


def run_o(xs, mixs, w_out, g, b):
    nc = build_o_program()
    in_maps = [dict(xin=xs[c], mixT=np.ascontiguousarray(mixs[c]), w_out=w_out, lng=g, lnb=b) for c in range(8)]
    res = run_bass_kernel_spmd(nc, in_maps, core_ids=list(range(8)))
    return [r["xout"] for r in res.results]


def _decl_inputs(nc, with_at, with_o_f2, with_f1_pr):
    ei = lambda n, sh, dt=F32: nc.dram_tensor(n, list(sh), dt, kind="ExternalInput").ap()
    T = {}
    T['x_in'] = ei("x_in", [TOK, D_MODEL])
    if with_at:
        T['QT'] = ei("QT", [RQ, TOK], BF16)
        T['KTall'] = ei("KTall", [4, RK, TOK], BF16)
        T['Vall'] = ei("Vall", [4, TOK, RV], BF16)
        T['kaug'], T['qaug'] = ei("kaug", [16, KCOLS], BF16), ei("qaug", [16, TOK], BF16)
        T['ddiag'], T['nsi'] = ei("ddiag", [128, 4 * 2 * 512], BF16), ei("nsi", [128, 8 * 128], BF16)
        T['wdist'], T['wmask'] = ei("wdist", [128, 3 * 128], BF16), ei("wmask", [128, 5 * 128], BF16)
        T['lpar'] = ei("lpar", [128, NLP])
    if with_o_f2:
        T['o_w'] = ei("o_w", [D_MODEL, D_MODEL])
        T['o_g'], T['o_b'] = ei("o_g", [D_MODEL]), ei("o_b", [D_MODEL])
        T['f2_wgu'], T['f2_wd'] = ei("f2_wgu", [D_MODEL, 2 * D_FF]), ei("f2_wd", [D_FF, D_MODEL])
        T['f2_g'], T['f2_b'] = ei("f2_g", [D_MODEL]), ei("f2_b", [D_MODEL])
    if with_f1_pr:
        T['f1_wgu'], T['f1_wd'] = ei("f1_wgu", [D_MODEL, 2 * D_FF]), ei("f1_wd", [D_FF, D_MODEL])
        T['f1_g'], T['f1_b'] = ei("f1_g", [D_MODEL]), ei("f1_b", [D_MODEL])
        T['w_in'] = ei("w_in", [D_MODEL, N_IN])
        T['w_uq'], T['w_ukv'] = ei("w_uq", [256, 384]), ei("w_ukv", [128, 512])
        T['par'] = ei("par", [128, NPAR])
        T['tcc'], T['tcs'] = ei("tcc", [128, TOK]), ei("tcs", [128, TOK])
        T['tbc'], T['tbs'] = ei("tbc", [32, TOK]), ei("tbs", [32, TOK])
    return T


def build_program(kind):
    if kind in _CACHE:
        return _CACHE[kind]
    C = Ctx()
    nc = C.nc
    with_at = kind in ('mid', 'last')
    with_f1 = kind in ('first', 'mid')
    T = _decl_inputs(nc, with_at, with_at, with_f1)
    eo = lambda n, sh, dt: nc.dram_tensor(n, list(sh), dt, kind="ExternalOutput").ap()
    x_cur = T['x_in']
    first = True
    if with_at:
        mixT = C.dram("mixT_i", [D_MODEL, TOK], BF16)
        phase_at(C, T['QT'], T['KTall'], T['Vall'], T['kaug'], T['qaug'], T['ddiag'], T['nsi'], T['wdist'], T['wmask'],
                 T['lpar'], mixT)
        C.new_phase()
        x2 = C.dram("x2_i", [TOK, D_MODEL], F32)
        phase_o(C, x_cur, mixT, T['o_w'], T['o_g'], T['o_b'], x2)
        C.new_phase()
        x3 = eo("x_out", [TOK, D_MODEL], F32) if kind == 'last' else C.dram("x3_i", [TOK, D_MODEL], F32)
        phase_ffn(C, x2, x3, T['f2_wgu'], T['f2_wd'], T['f2_g'], T['f2_b'], tag="f2")
        x_cur = x3
        first = False
    if with_f1:
        if not first:
            C.new_phase()
        x1 = eo("x_out", [TOK, D_MODEL], F32)
        phase_ffn(C, x_cur, x1, T['f1_wgu'], T['f1_wd'], T['f1_g'], T['f1_b'], tag="f1")
        C.new_phase()
        QT, KT, V = eo("QTo", [RQ, TOK], BF16), eo("KTo", [RK, TOK], BF16), eo("Vo", [TOK, RV], BF16)
        phase_pr(C, x1, T['w_in'], T['w_uq'], T['w_ukv'], T['par'], T['tcc'], T['tcs'], T['tbc'], T['tbs'], QT, KT, V)
    C.finish()
    _CACHE[kind] = nc
    return nc


_TAB = {}


def _tables():
    if not _TAB:
        _TAB['rope'] = [rope_tables(r) for r in range(4)]
        _TAB['at'] = [at_tables(r) for r in range(4)]
    return _TAB


def kernel(x, w_in, win_sink, mla_q_norm, mla_w_uq, mla_kv_norm, mla_w_ukv, ax_q_norm, ax_k_norm,
           diff_lambda, diff_subln, w_out, ffn_w_gu, ffn_w_down, ln_g, ln_b):
    inp = dict(x=x, w_in=w_in, win_sink=win_sink, mla_q_norm=mla_q_norm, mla_w_uq=mla_w_uq, mla_kv_norm=mla_kv_norm,
               mla_w_ukv=mla_w_ukv, ax_q_norm=ax_q_norm, ax_k_norm=ax_k_norm, diff_lambda=diff_lambda,
               diff_subln=diff_subln, w_out=w_out, ffn_w_gu=ffn_w_gu, ffn_w_down=ffn_w_down, ln_g=ln_g, ln_b=ln_b)
    inp = {k: np.asarray(v, np.float32) for k, v in inp.items()}
    ca = np.ascontiguousarray
    tabs = _tables()

    def f1pr_inputs(l, r):
        m = dict(f1_wgu=ca(inp['ffn_w_gu'][l, 0]), f1_wd=ca(inp['ffn_w_down'][l, 0]), f1_g=ca(inp['ln_g'][l, 0]),
                 f1_b=ca(inp['ln_b'][l, 0]), w_in=ca(inp['w_in'][l]), w_uq=ca(inp['mla_w_uq'][l]),
                 w_ukv=ca(inp['mla_w_ukv'][l]),
                 par=par_table(inp['mla_q_norm'][l], inp['mla_kv_norm'][l], inp['ax_q_norm'][l], inp['ax_k_norm'][l]))
        m.update(tabs['rope'][r])
        return m

    def at_inputs(l, c, qkv):
        b, r = c // 4, c % 4
        order = [b * 4 + (r + j) % 4 for j in range(4)]
        m = dict(QT=ca(qkv[c][0]), KTall=ca(np.stack([qkv[o][1] for o in order])),
                 Vall=ca(np.stack([qkv[o][2] for o in order])),
                 lpar=lpar_table(inp['win_sink'][l], inp['diff_lambda'][l], inp['diff_subln'][l], l),
                 o_w=ca(inp['w_out'][l]), o_g=ca(inp['ln_g'][l, 1]), o_b=ca(inp['ln_b'][l, 1]),
                 f2_wgu=ca(inp['ffn_w_gu'][l, 1]), f2_wd=ca(inp['ffn_w_down'][l, 1]), f2_g=ca(inp['ln_g'][l, 2]),
                 f2_b=ca(inp['ln_b'][l, 2]))
        m.update(tabs['at'][r])
        return m

    xs = [ca(inp['x'][c // 4, (c % 4) * TOK:(c % 4 + 1) * TOK]) for c in range(8)]
    nc = build_program('first')
    res = run_bass_kernel_spmd(nc, [dict(x_in=xs[c], **f1pr_inputs(0, c % 4)) for c in range(8)], core_ids=list(range(8)))
    xs = [r["x_out"] for r in res.results]
    qkv = [(r["QTo"], r["KTo"], r["Vo"]) for r in res.results]
    for l in range(DEPTH):
        lastl = (l == DEPTH - 1)
        nc = build_program('last' if lastl else 'mid')
        in_maps = []
        for c in range(8):
            m = dict(x_in=xs[c], **at_inputs(l, c, qkv))
            if not lastl:
                m.update(f1pr_inputs(l + 1, c % 4))
            in_maps.append(m)
        res = run_bass_kernel_spmd(nc, in_maps, core_ids=list(range(8)))
        xs = [r["x_out"] for r in res.results]
        if not lastl:
            qkv = [(r["QTo"], r["KTo"], r["Vo"]) for r in res.results]
    out = np.zeros((2, SEQ, D_MODEL), np.float32)
    for c in range(8):
        out[c // 4, (c % 4) * TOK:(c % 4 + 1) * TOK] = xs[c]
    return out
```
